# Optimizing a Trainium2 kernel written in Bass

```python
import math
import jax, jax.numpy as jnp
from jax import lax
import numpy as np

D_MODEL = 1024
BATCH = 8
SEQ = 2048
DEPTH = 1

GRID_W = 64
CTX_LEN = 256

MLA_HEADS = 8
MLA_NOPE = 64
MLA_ROPE = 32
MLA_QK = MLA_NOPE + MLA_ROPE
MLA_V = 64
MLA_Q_LORA = 384
MLA_KV_LORA = 256
MLA_WIDTH = MLA_HEADS * MLA_V
ROPE_AXIS = MLA_ROPE // 2
ROPE_THETA = 10000.0
Q_BLOCK = 128

DN_HEADS = 8
DN_DK = 64
DN_DV = 64
DN_WIDTH_K = DN_HEADS * DN_DK
DN_WIDTH = DN_HEADS * DN_DV
CONV_W = 5
CHUNK = 64
N_DIR = 2

EPS = 1e-6

IN_SPLITS = (MLA_Q_LORA, MLA_KV_LORA, MLA_ROPE, MLA_WIDTH,
             DN_WIDTH_K, DN_WIDTH_K, DN_WIDTH, DN_WIDTH,
             N_DIR * DN_HEADS, N_DIR * DN_HEADS, 2 * D_MODEL)
IN_DIM = (MLA_Q_LORA + MLA_KV_LORA + MLA_ROPE + MLA_WIDTH + 2 * DN_WIDTH_K + 2 * DN_WIDTH
          + 2 * N_DIR * DN_HEADS + 2 * D_MODEL)

kernel_name = "hybrid_mla_gated_deltanet_ctx_prefix"


def rms_norm(x, w):
    x32 = x.astype(jnp.float32)
    y = x32 * lax.rsqrt(jnp.mean(x32 * x32, axis=-1, keepdims=True) + EPS)
    return y.astype(x.dtype) * w


def l2_normalize(x):
    x32 = x.astype(jnp.float32)
    return (x32 * lax.rsqrt(jnp.sum(x32 * x32, axis=-1, keepdims=True) + EPS)).astype(x.dtype)


def split_in(p):
    offsets = np.cumsum(np.array(IN_SPLITS))[:-1].tolist()
    return jnp.split(p, offsets, axis=-1)


def axial_rope_tables(n_tokens, dtype):
    rows = n_tokens // GRID_W
    row = jnp.repeat(jnp.arange(rows, dtype=jnp.float32), GRID_W)
    col = jnp.tile(jnp.arange(GRID_W, dtype=jnp.float32), rows)
    inv_freq = ROPE_THETA ** (-jnp.arange(0, ROPE_AXIS, 2, dtype=jnp.float32) / ROPE_AXIS)
    ang = jnp.concatenate([row[:, None] * inv_freq, col[:, None] * inv_freq], axis=-1)
    return jnp.cos(ang).astype(dtype), jnp.sin(ang).astype(dtype)


def apply_axial_rope(t, cos, sin):
    half = ROPE_AXIS // 2
    tr = t.reshape(t.shape[:-1] + (2, 2, half))
    t1, t2 = tr[..., 0, :], tr[..., 1, :]
    c = cos.reshape(-1, 2, half)
    s = sin.reshape(-1, 2, half)
    out = jnp.stack([t1 * c - t2 * s, t1 * s + t2 * c], axis=-2)
    return out.reshape(t.shape)


def rope_tail(t, cos, sin):
    return jnp.concatenate([t[..., :MLA_NOPE], apply_axial_rope(t[..., MLA_NOPE:], cos, sin)], axis=-1)


def mla_queries(cq, q_norm_w, w_uq, q_head_norm_w):
    B, T, _ = cq.shape
    q = (rms_norm(cq, q_norm_w) @ w_uq).reshape(B, T, MLA_HEADS, MLA_QK)
    return rms_norm(q, q_head_norm_w).transpose(0, 2, 1, 3)


def mla_keys_values(ckv, krope, kv_norm_w, w_ukv, k_head_norm_w):
    B, T, _ = ckv.shape
    kv = (rms_norm(ckv, kv_norm_w) @ w_ukv).reshape(B, T, MLA_HEADS, MLA_NOPE + MLA_V)
    k_nope, v = kv[..., :MLA_NOPE], kv[..., MLA_NOPE:]
    k_rope = jnp.broadcast_to(krope[:, :, None, :], (B, T, MLA_HEADS, MLA_ROPE))
    k = rms_norm(jnp.concatenate([k_nope, k_rope], axis=-1), k_head_norm_w)
    return k.transpose(0, 2, 1, 3), v.transpose(0, 2, 1, 3)


def block_softmax_attention(q, k, v):
    B, H, T, dq = q.shape
    nb = T // Q_BLOCK
    scale = MLA_QK ** -0.5
    qb = q.reshape(B, H, nb, Q_BLOCK, dq).transpose(2, 0, 1, 3, 4)

    def attend(q_blk):
        s = jnp.einsum('bhqd,bhkd->bhqk', q_blk, k).astype(jnp.float32) * scale
        p = jax.nn.softmax(s, axis=-1).astype(v.dtype)
        return jnp.einsum('bhqk,bhkd->bhqd', p, v)

    o = lax.map(attend, qb)
    return o.transpose(1, 2, 0, 3, 4).reshape(B, H, T, v.shape[-1])


def centred_depthwise_conv(x, w):
    y = lax.conv_general_dilated(x, w[:, None, :].astype(x.dtype), window_strides=(1,),
                                 padding=[(CONV_W // 2, CONV_W // 2)],
                                 dimension_numbers=('NWC', 'WIO', 'NWC'),
                                 feature_group_count=x.shape[-1])
    return jax.nn.silu(y)


def deltanet_inputs(dq, dk, dv, db, da, conv_w, a_log, dt_bias):
    B, T, _ = dq.shape
    qkv = centred_depthwise_conv(jnp.concatenate([dq, dk, dv], axis=-1), conv_w)
    q, k, v = jnp.split(qkv, [DN_WIDTH_K, 2 * DN_WIDTH_K], axis=-1)
    heads = lambda t, d: t.reshape(B, T, DN_HEADS, d).transpose(0, 2, 1, 3)
    q = l2_normalize(heads(q, DN_DK)) * (DN_DK ** -0.5)
    k = l2_normalize(heads(k, DN_DK))
    v = heads(v, DN_DV)
    db = db.reshape(B, T, N_DIR, DN_HEADS).astype(jnp.float32)
    da = da.reshape(B, T, N_DIR, DN_HEADS).astype(jnp.float32)
    beta = jax.nn.sigmoid(db).transpose(2, 0, 3, 1)
    g = -jnp.exp(a_log.astype(jnp.float32))[:, None, :, None] * \
        jax.nn.softplus(da + dt_bias.astype(jnp.float32)).transpose(2, 0, 3, 1)
    return q, k, v, beta, g


def chunk_gated_delta(q, k, v, beta, g, s0):
    out_dtype = v.dtype
    f32 = jnp.float32
    B, H, T, dk = q.shape
    dv = v.shape[-1]
    n = T // CHUNK
    q = q.astype(f32).reshape(B, H, n, CHUNK, dk)
    k = k.astype(f32).reshape(B, H, n, CHUNK, dk)
    v = v.astype(f32).reshape(B, H, n, CHUNK, dv)
    beta = beta.astype(f32).reshape(B, H, n, CHUNK)
    gc = jnp.cumsum(g.astype(f32).reshape(B, H, n, CHUNK), axis=-1)
    incl = jnp.tril(jnp.ones((CHUNK, CHUNK), dtype=bool))
    strict = jnp.tril(jnp.ones((CHUNK, CHUNK), dtype=bool), -1)
    diff = gc[..., :, None] - gc[..., None, :]
    decay = jnp.where(incl, jnp.exp(jnp.where(incl, diff, 0.0)), 0.0)
    kb = k * beta[..., None]
    lower = jnp.where(strict, jnp.einsum('bhnid,bhnjd->bhnij', kb, k) * decay, 0.0)
    a_mat = lower + jnp.eye(CHUNK, dtype=f32)
    rhs = jnp.concatenate([v * beta[..., None], kb * jnp.exp(gc)[..., None]], axis=-1)
    sol = lax.linalg.triangular_solve(a_mat, rhs, left_side=True, lower=True, unit_diagonal=True)
    u, w = sol[..., :dv], sol[..., dv:]
    qk = jnp.where(incl, jnp.einsum('bhnid,bhnjd->bhnij', q, k) * decay, 0.0)
    g_last = gc[..., -1]
    q_dec = q * jnp.exp(gc)[..., None]
    k_dec = k * jnp.exp(g_last[..., None] - gc)[..., None]
    xs = tuple(jnp.moveaxis(t, 2, 0) for t in (q_dec, qk, u, w, k_dec, g_last))

    def step(state, inp):
        qd, a, u_c, w_c, kd, gl = inp
        v_new = u_c - jnp.einsum('bhcd,bhde->bhce', w_c, state)
        o = jnp.einsum('bhcd,bhde->bhce', qd, state) + jnp.einsum('bhcj,bhje->bhce', a, v_new)
        state = state * jnp.exp(gl)[..., None, None] + jnp.einsum('bhcd,bhce->bhde', kd, v_new)
        return state, o

    s_final, o = lax.scan(step, s0, xs)
    o = jnp.moveaxis(o, 0, 2).reshape(B, H, T, dv).astype(out_dtype)
    return o, s_final


def bidirectional_gated_delta(ctx_in, lat_in):
    qc, kc, vc, bc, gc = ctx_in
    ql, kl, vl, bl, gl = lat_in
    B, H = qc.shape[0], qc.shape[1]
    o_ctx, o_lat = 0.0, 0.0
    for d in range(N_DIR):
        flip = (lambda t: jnp.flip(t, axis=2)) if d == 1 else (lambda t: t)
        s0 = jnp.zeros((B, H, DN_DK, DN_DV), jnp.float32)
        oc, s_ctx = chunk_gated_delta(flip(qc), flip(kc), flip(vc), flip(bc[d]), flip(gc[d]), s0)
        ol, _ = chunk_gated_delta(flip(ql), flip(kl), flip(vl), flip(bl[d]), flip(gl[d]), s_ctx)
        o_ctx = o_ctx + flip(oc)
        o_lat = o_lat + flip(ol)
    return o_ctx, o_lat


def merge_branches(o_mla, z_mla, o_dn, z_dn, gates, mla_w_o, dn_out_norm_w, dn_w_o, w_out):
    B, _, T, _ = o_mla.shape
    y_mla = (o_mla.transpose(0, 2, 1, 3).reshape(B, T, MLA_WIDTH) * jax.nn.silu(z_mla)) @ mla_w_o
    o_dn = rms_norm(o_dn.transpose(0, 2, 1, 3), dn_out_norm_w).reshape(B, T, DN_WIDTH)
    y_dn = (o_dn * jax.nn.silu(z_dn)) @ dn_w_o
    g_mla, g_dn = jnp.split(gates, 2, axis=-1)
    return (jax.nn.sigmoid(g_mla) * y_mla + jax.nn.sigmoid(g_dn) * y_dn) @ w_out


def hybrid_layer(x, ctx, c, c_ctx, rope_cos, rope_sin, w_mod, b_mod, norm_w, w_in,
                 mla_q_norm_w, mla_w_uq, mla_kv_norm_w, mla_w_ukv, mla_q_head_norm_w, mla_k_head_norm_w, mla_w_o,
                 dn_conv_w, dn_a_log, dn_dt_bias, dn_out_norm_w, dn_w_o, w_out, update_ctx):
    shift, scale, gate = jnp.split(jax.nn.silu(c) @ w_mod + b_mod, 3, axis=-1)
    shift_c, scale_c, gate_c = jnp.split(jax.nn.silu(c_ctx) @ w_mod + b_mod, 3, axis=-1)
    h = rms_norm(x, norm_w) * (1.0 + scale[:, None]) + shift[:, None]
    hc = rms_norm(ctx, norm_w) * (1.0 + scale_c) + shift_c
    cq_l, ckv_l, kr_l, zm_l, dq_l, dk_l, dv_l, zd_l, db_l, da_l, gates_l = split_in(h @ w_in)
    cq_c, ckv_c, kr_c, zm_c, dq_c, dk_c, dv_c, zd_c, db_c, da_c, gates_c = split_in(hc @ w_in)

    k_ctx, v_ctx = mla_keys_values(ckv_c, kr_c, mla_kv_norm_w, mla_w_ukv, mla_k_head_norm_w)
    k_lat, v_lat = mla_keys_values(ckv_l, kr_l, mla_kv_norm_w, mla_w_ukv, mla_k_head_norm_w)
    k_lat = rope_tail(k_lat, rope_cos, rope_sin)
    q_lat = rope_tail(mla_queries(cq_l, mla_q_norm_w, mla_w_uq, mla_q_head_norm_w), rope_cos, rope_sin)
    o_mla_lat = block_softmax_attention(q_lat, jnp.concatenate([k_ctx, k_lat], axis=2),
                                        jnp.concatenate([v_ctx, v_lat], axis=2))

    dn_ctx = deltanet_inputs(dq_c, dk_c, dv_c, db_c, da_c, dn_conv_w, dn_a_log, dn_dt_bias)
    dn_lat = deltanet_inputs(dq_l, dk_l, dv_l, db_l, da_l, dn_conv_w, dn_a_log, dn_dt_bias)
    o_dn_ctx, o_dn_lat = bidirectional_gated_delta(dn_ctx, dn_lat)

    y_lat = merge_branches(o_mla_lat, zm_l, o_dn_lat, zd_l, gates_l, mla_w_o, dn_out_norm_w, dn_w_o, w_out)
    x = x + gate[:, None] * y_lat
    if update_ctx:
        q_ctx = mla_queries(cq_c, mla_q_norm_w, mla_w_uq, mla_q_head_norm_w)
        o_mla_ctx = block_softmax_attention(q_ctx, k_ctx, v_ctx)
        y_ctx = merge_branches(o_mla_ctx, zm_c, o_dn_ctx, zd_c, gates_c, mla_w_o, dn_out_norm_w, dn_w_o, w_out)
        ctx = ctx + gate_c * y_ctx
    return x, ctx


def setup_inputs(seed: int = 0) -> dict:
    key = jax.random.key(seed)
    ks = jax.random.split(key, 21)
    f32 = jnp.float32
    L = DEPTH
    nrm = lambda k, shape, fan_in: jax.random.normal(k, shape, f32) * (fan_in ** -0.5)
    gain = lambda k, shape: 1.0 + 0.01 * jax.random.normal(k, shape, f32)
    dt = jnp.exp(jax.random.uniform(ks[17], (L, N_DIR, DN_HEADS), f32, minval=math.log(1e-3), maxval=math.log(1e-1)))
    return {
        "x": jax.random.normal(ks[0], (BATCH, SEQ, D_MODEL), f32),
        "c": jax.random.normal(ks[1], (BATCH, D_MODEL), f32),
        "ctx": jax.random.normal(ks[2], (BATCH, CTX_LEN, D_MODEL), f32),
        "c_ctx": jax.random.normal(ks[3], (D_MODEL,), f32),
        "w_mod": nrm(ks[4], (L, D_MODEL, 3 * D_MODEL), D_MODEL),
        "b_mod": 0.02 * jax.random.normal(ks[5], (L, 3 * D_MODEL), f32),
        "norm_w": gain(ks[6], (L, D_MODEL)),
        "w_in": nrm(ks[7], (L, D_MODEL, IN_DIM), D_MODEL),
        "mla_q_norm_w": gain(ks[8], (L, MLA_Q_LORA)),
        "mla_w_uq": nrm(ks[9], (L, MLA_Q_LORA, MLA_HEADS * MLA_QK), MLA_Q_LORA),
        "mla_kv_norm_w": gain(ks[10], (L, MLA_KV_LORA)),
        "mla_w_ukv": nrm(ks[11], (L, MLA_KV_LORA, MLA_HEADS * (MLA_NOPE + MLA_V)), MLA_KV_LORA),
        "mla_q_head_norm_w": gain(ks[12], (L, MLA_QK)),
        "mla_k_head_norm_w": gain(ks[13], (L, MLA_QK)),
        "mla_w_o": nrm(ks[14], (L, MLA_WIDTH, D_MODEL), MLA_WIDTH),
        "dn_conv_w": nrm(ks[15], (L, CONV_W, 2 * DN_WIDTH_K + DN_WIDTH), CONV_W),
        "dn_a_log": jnp.log(jax.random.uniform(ks[16], (L, N_DIR, DN_HEADS), f32, minval=1.0, maxval=16.0)),
        "dn_dt_bias": dt + jnp.log(-jnp.expm1(-dt)),
        "dn_out_norm_w": gain(ks[18], (L, DN_DV)),
        "dn_w_o": nrm(ks[19], (L, DN_WIDTH, D_MODEL), DN_WIDTH),
        "w_out": nrm(ks[20], (L, D_MODEL, D_MODEL), D_MODEL),
    }


def reference(x, c, ctx, c_ctx, w_mod, b_mod, norm_w, w_in, mla_q_norm_w, mla_w_uq, mla_kv_norm_w, mla_w_ukv,
              mla_q_head_norm_w, mla_k_head_norm_w, mla_w_o, dn_conv_w, dn_a_log, dn_dt_bias, dn_out_norm_w, dn_w_o,
              w_out):
    rope_cos, rope_sin = axial_rope_tables(x.shape[1], x.dtype)
    for layer in range(DEPTH):
        x, ctx = hybrid_layer(x, ctx, c, c_ctx, rope_cos, rope_sin, w_mod[layer], b_mod[layer], norm_w[layer],
                              w_in[layer], mla_q_norm_w[layer], mla_w_uq[layer], mla_kv_norm_w[layer],
                              mla_w_ukv[layer], mla_q_head_norm_w[layer], mla_k_head_norm_w[layer], mla_w_o[layer],
                              dn_conv_w[layer], dn_a_log[layer], dn_dt_bias[layer], dn_out_norm_w[layer],
                              dn_w_o[layer], w_out[layer], update_ctx=(layer < DEPTH - 1))
    return x
```

```python
import math
import os
from contextlib import ExitStack

import numpy as np
import concourse.bass as bass
import concourse.mybir as mybir
from concourse.bass_utils import run_bass_kernel_spmd

F32 = mybir.dt.float32
BF16 = mybir.dt.bfloat16
ALU = mybir.AluOpType
AF = mybir.ActivationFunctionType
AX = mybir.AxisListType

D = 1024
T_LAT = 2048
T_CTX = 256
T_ALL = 2304
NT = 18
IN_DIM = 5312
EPS = 1e-6


class Buf:
    __slots__ = ("name", "lw", "rd")

    def __init__(self, name):
        self.name = name
        self.lw = None
        self.rd = {}


class Eng:
    def __init__(self, name, be, sem, same_raw):
        self.name = name
        self.be = be
        self.sem = sem
        self.cnt = 0
        self.waited = {}
        self.same_raw = same_raw


class FW:
    def __init__(self, nc, stack):
        self.nc = nc
        self.stack = stack
        self.nsem = 0
        self.engs = {}
        for name, be, same_raw in (("pe", nc.tensor, False), ("act", nc.scalar, True),
                                   ("dve", nc.vector, True), ("pool", nc.gpsimd, True),
                                   ("sp", nc.sync, False)):
            self.engs[name] = Eng(name, be, self.new_sem("e_" + name), same_raw)
        self.dma_sems = {}
        self.n_ops = 0
        self.n_waits = 0
        self.bufs = {}
        self.trace = {n: [] for n in self.engs}

    def new_sem(self, name):
        self.nsem += 1
        return self.stack.enter_context(self.nc.semaphore(name))

    def sbuf(self, name, shape, dtype, stack=None):
        self.nalloc = getattr(self, "nalloc", 0) + 1
        return (stack or self.stack).enter_context(self.nc.sbuf_tensor("s%d_%s" % (self.nalloc, name), list(shape), dtype))

    def psum(self, name, shape, dtype=F32):
        return self.stack.enter_context(self.nc.psum_tensor("p_" + name, list(shape), dtype))

    def B(self, name):
        b = self.bufs.get(name)
        if b is None:
            b = self.bufs[name] = Buf(name)
        return b

    def _bl(self, xs):
        return [self.B(x) if isinstance(x, str) else x for x in xs]

    def _deps(self, eng, reads, writes):
        deps = {}

        def add(sem, val):
            k = id(sem)
            if k not in deps or deps[k][1] < val:
                deps[k] = (sem, val)

        for b in reads:
            if b.lw is not None:
                s, v = b.lw
                if s is eng.sem and not eng.same_raw:
                    continue
                add(s, v)
        for b in writes:
            if b.lw is not None:
                s, v = b.lw
                if s is not eng.sem or eng.same_raw:
                    add(s, v)
            for k, (s, v) in b.rd.items():
                if s is not eng.sem or eng.same_raw:
                    add(s, v)
        for k, (s, v) in deps.items():
            if eng.waited.get(k, 0) < v:
                eng.be.wait_ge(s, v)
                eng.waited[k] = v
                self.n_waits += 1
                self.trace[eng.name].append(("w", k, v))

    def _commit(self, sem, val, reads, writes):
        k = id(sem)
        for b in reads:
            if k not in b.rd or b.rd[k][1] < val:
                b.rd[k] = (sem, val)
        for b in writes:
            b.lw = (sem, val)
            b.rd = {}

    def op(self, ename, fn, reads=(), writes=(), inc=True):
        eng = self.engs[ename]
        reads = self._bl(reads)
        writes = self._bl(writes)
        self._deps(eng, reads, writes)
        ins = fn(eng.be)
        if inc:
            eng.cnt += 1
            ins.then_inc(eng.sem, 1)
            self._commit(eng.sem, eng.cnt, reads, writes)
            self.trace[ename].append(("i", id(eng.sem), 1))
        else:
            self._commit(eng.sem, eng.cnt + 1, reads, writes)
            self.trace[ename].append(("n", 0, 0))
        self.n_ops += 1
        return ins

    def dma(self, qname, out, in_, reads=(), writes=(), slot=None, **kw):
        eng = self.engs[qname]
        reads = self._bl(reads)
        writes = self._bl(writes)
        self._deps(eng, reads, writes)
        if slot is None:
            slot = writes[0].name if writes else reads[0].name
        if slot not in self.dma_sems:
            self.dma_sems[slot] = [self.new_sem("d%d" % len(self.dma_sems)), 0]
        ent = self.dma_sems[slot]
        ins = eng.be.dma_start(out=out, in_=in_, **kw)
        ent[1] += 16
        ins.then_inc(ent[0], 16)
        self.trace[qname].append(("i", id(ent[0]), 16))
        self._commit(ent[0], ent[1], reads, writes)
        self.n_ops += 1
        return ins

    def wait_bufs(self, ename, bufs):
        self._deps(self.engs[ename], self._bl(bufs), ())

    def simulate(self):
        sem = {}
        ptr = {n: 0 for n in self.trace}
        progress = True
        while progress:
            progress = False
            for n, tr in self.trace.items():
                while ptr[n] < len(tr):
                    kind, k, v = tr[ptr[n]]
                    if kind == "w":
                        if sem.get(k, 0) >= v:
                            ptr[n] += 1
                            progress = True
                        else:
                            break
                    else:
                        if kind == "i":
                            sem[k] = sem.get(k, 0) + v
                        ptr[n] += 1
                        progress = True
        stuck = {n: (ptr[n], len(tr)) for n, tr in self.trace.items() if ptr[n] < len(tr)}
        return stuck

    def barrier(self):
        targets = [(e.sem, e.cnt) for e in self.engs.values() if e.cnt > 0]
        targets += [(s, v) for (s, v) in self.dma_sems.values() if v > 0]
        for e in self.engs.values():
            for s, v in targets:
                if s is e.sem:
                    continue
                if e.waited.get(id(s), 0) < v:
                    e.be.wait_ge(s, v)
                    e.waited[id(s)] = v
                    self.n_waits += 1
                    self.trace[e.name].append(("w", id(s), v))


def _host_consts():
    c = {}
    c["ident"] = np.eye(128, dtype=np.float32)
    m = np.arange(128)[:, None]
    i = np.arange(128)[None, :]
    c["tri"] = np.stack([(m <= i), (m < i), (m >= i), (m > i), np.ones((128, 128), bool),
                         (m // 64 == i // 64)]).astype(np.float32)
    c["negoff"] = -(m != i).astype(np.float32)
    hm = [(m // 8 == i // 8)]
    for s_ in (8, 16, 32, 64):
        hm.append((m // (2 * s_) == i // (2 * s_)) & (m % (2 * s_) >= s_) & (i % (2 * s_) < s_))
    for s_ in (8, 16, 32, 64):
        hm.append((m // (2 * s_) == i // (2 * s_)) & (i % (2 * s_) >= s_) & (m % (2 * s_) < s_))
    c["hmask"] = np.stack(hm).astype(np.float32)
    t = np.arange(T_LAT)
    row = (t // 64).astype(np.float64)
    col = (t % 64).astype(np.float64)
    inv = 10000.0 ** (-np.arange(0, 16, 2, dtype=np.float64) / 16)
    ang = np.concatenate([row[:, None] * inv, col[:, None] * inv], axis=-1)
    c["rope_cos"] = np.cos(ang).astype(np.float32)
    c["rope_sin"] = np.sin(ang).astype(np.float32)
    return c


def build_program(taps=(), stop_after=None):
    nc = bass.Bass("TRN2", target_bir_lowering=False)

    def din(name, shape):
        return nc.dram_tensor(name, list(shape), F32, kind="ExternalInput").ap()

    x = din("x", [T_LAT, D])
    ctx = din("ctx", [T_CTX, D])
    cvec = din("c", [D])
    cctx = din("c_ctx", [D])
    w_mod = din("w_mod", [D, 3 * D])
    b_mod = din("b_mod", [3 * D])
    norm_w = din("norm_w", [D])
    ident_d = din("ident", [128, 128])
    tri_d = din("tri", [6, 128, 128])
    negoff_d = din("negoff", [128, 128])
    hmask_d = din("hmask", [9, 128, 128])
    dn_w = din("dn_out_norm_w", [64])
    cos_d = din("rope_cos", [T_LAT, 16])
    sin_d = din("rope_sin", [T_LAT, 16])
    q_norm_w = din("mla_q_norm_w", [384])
    w_uq = din("mla_w_uq", [384, 768])
    kv_norm_w = din("mla_kv_norm_w", [256])
    w_ukv = din("mla_w_ukv", [256, 1024])
    qh_w = din("mla_q_head_norm_w", [96])
    kh_w = din("mla_k_head_norm_w", [96])
    mla_w_o = din("mla_w_o", [512, D])
    dn_w_o = din("dn_w_o", [512, D])
    w_out = din("w_out", [D, D])
    w_in = din("w_in", [D, IN_DIM])
    conv_w = din("dn_conv_w", [5, 1536])
    a_log = din("dn_a_log", [16])
    dt_bias = din("dn_dt_bias", [16])
    out = nc.dram_tensor("out", [T_LAT, D], F32, kind="ExternalOutput").ap()
    tap_out = {}

    with ExitStack() as st:
        fw = FW(nc, st)
        op, dma = fw.op, fw.dma

        def tap(name, sb_ap, shape, dtype, reads):
            if name not in taps:
                return
            t = nc.dram_tensor("tap_" + name, list(shape), dtype, kind="ExternalOutput").ap()
            tap_out[name] = t
            dma("sp", t, sb_ap, reads=reads, writes=["tap_" + name])

        ident = fw.sbuf("ident", [128, 128], BF16)
        identf = fw.sbuf("identf", [128, 128], F32)
        gateB = fw.sbuf("gateB", [128, D], F32)
        odT = fw.sbuf("odT", [128, 4, T_LAT], BF16)
        modT = fw.sbuf("modT", [128, 3, 8, 2], F32)
        gam = fw.sbuf("gam", [128, 8, 2], F32)
        shf = fw.sbuf("shf", [128, 8, 2], F32)
        psall = fw.psum("psall", [128, 8, 512], F32)
        pb = [psall[:, i, :] for i in range(8)]
        PB = [fw.B("pb%d" % i) for i in range(8)]

        dma("pool", ident[:], ident_d, writes=["ident"])
        dma("sp", identf[:], ident_d, writes=["identf"])

        with ExitStack() as pa:
            cf = fw.sbuf("cf", [128, 2, 8], F32, pa)
            csil = fw.sbuf("csil", [128, 2, 8], BF16, pa)
            csilB = fw.sbuf("csilB", [128, 8, 128], BF16, pa)
            bmodT = fw.sbuf("bmodT", [128, 3, 8], F32, pa)
            nwT = fw.sbuf("nwT", [128, 8], F32, pa)
            bgB = fw.sbuf("bgB", [128, D], F32, pa)
            wm = [fw.sbuf("wm%d" % i, [128, 8, 512], BF16, pa) for i in range(2)]
            with nc.allow_non_contiguous_dma(reason="small vector column layouts"):
                dma("sp", cf[:, 0, :], cvec.rearrange("(kt p) -> p kt", p=128), writes=["cf"])
                dma("sp", cf[:, 1, :], cctx.rearrange("(kt p) -> p kt", p=128), writes=["cf"], slot="cf2")
                for s in range(3):
                    dma("sp", bmodT[:, s, :], b_mod[s * D:(s + 1) * D].rearrange("(kt p) -> p kt", p=128),
                        writes=["bmodT"], slot="bmodT%d" % s)
                dma("sp", nwT[:], norm_w.rearrange("(kt p) -> p kt", p=128), writes=["nwT"])
            dma("sp", bgB[:], b_mod[2 * D:3 * D].partition_broadcast(128), writes=["bgB"])
            op("act", lambda e: e.activation(out=csil[:], in_=cf[:], func=AF.Silu), reads=["cf"], writes=["csil"])
            op("dve", lambda e: e.tensor_copy(out=csilB[:], in_=csil[:, 0, :].unsqueeze(2).to_broadcast([128, 8, 128])),
               reads=["csil"], writes=["csilB"])
            wmv = w_mod.rearrange("(kt p) n -> p kt n", p=128)
            mod_ps = pb[0][:, 0:48].rearrange("p (s k v) -> p s k v", s=3, k=8)
            for j in range(6):
                sec, half = j // 2, j % 2
                wmj = wm[j % 2]
                wname = "wm%d" % (j % 2)
                dma("pool", wmj[:], wmv[:, :, j * 512:(j + 1) * 512], writes=[wname])
                for q in range(4):
                    kto = half * 4 + q
                    for kti in range(8):
                        op("pe", (lambda q=q, kti=kti, kto=kto, sec=sec, wmj=wmj: lambda e: e.matmul(
                            mod_ps[:, sec, kto, :], lhsT=wmj[:, kti, q * 128:(q + 1) * 128], rhs=csil[:, :, kti],
                            start=(kti == 0), stop=(kti == 7)))(),
                           reads=[wname, "csil"], writes=[PB[0]], inc=(kti == 7))
                if sec == 2:
                    for kti in range(8):
                        op("pe", (lambda kti=kti, half=half, wmj=wmj: lambda e: e.matmul(
                            pb[1 + half][:], lhsT=csilB[:, kti, :], rhs=wmj[:, kti, :],
                            start=(kti == 0), stop=(kti == 7)))(),
                           reads=[wname, "csilB"], writes=[PB[1 + half]], inc=(kti == 7))
            op("dve", lambda e: e.tensor_tensor(out=modT[:], in0=mod_ps,
                                                in1=bmodT[:].unsqueeze(3).to_broadcast([128, 3, 8, 2]), op=ALU.add),
               reads=[PB[0], "bmodT"], writes=["modT"])
            op("dve", lambda e: e.tensor_scalar(out=gam[:], in0=modT[:, 1, :, :], scalar1=1.0, scalar2=None, op0=ALU.add),
               reads=["modT"], writes=["gam"])
            op("dve", lambda e: e.tensor_tensor(out=gam[:], in0=gam[:], in1=nwT[:].unsqueeze(2).to_broadcast([128, 8, 2]),
                                                op=ALU.mult), reads=["gam", "nwT"], writes=["gam"])
            op("dve", lambda e: e.tensor_copy(out=shf[:], in_=modT[:, 0, :, :]), reads=["modT"], writes=["shf"])
            for half in range(2):
                op("dve", (lambda half=half: lambda e: e.tensor_tensor(
                    out=gateB[:, half * 512:(half + 1) * 512], in0=pb[1 + half][:], in1=bgB[:, half * 512:(half + 1) * 512],
                    op=ALU.add))(), reads=[PB[1 + half], "bgB"], writes=["gateB"])
            tap("modT", modT[:], [128, 3, 8, 2], F32, ["modT"])
            tap("gateB", gateB[:], [128, D], F32, ["gateB"])
            fw.barrier()

        def phase_B(hT):
            with ExitStack() as pbk:
                xs = [fw.sbuf("xs%d" % i, [128, D], F32, pbk) for i in range(3)]
                junk = fw.sbuf("junkB", [128, D], BF16, pbk)
                xn = [fw.sbuf("xn%d" % i, [128, D], BF16, pbk) for i in range(2)]
                ssq = fw.sbuf("ssq", [128, NT], F32, pbk)
                def stats(tt):
                    xt, xtn = xs[tt % 3], "xs%d" % (tt % 3)
                    src = ctx[tt * 128:(tt + 1) * 128, :] if tt < 2 else x[(tt - 2) * 128:(tt - 1) * 128, :]
                    dma("sp", xt[:], src, writes=[xtn])
                    sc = ssq[:, tt:tt + 1]
                    op("act", lambda e: e.activation(out=junk[:], in_=xt[:], func=AF.Square, accum_out=sc), [xtn], ["junkB", "ssq%d" % tt])
                    op("dve", lambda e: e.tensor_scalar(out=sc, in0=sc, scalar1=1.0 / D, scalar2=EPS, op0=ALU.mult, op1=ALU.add),
                       ["ssq%d" % tt], ["ssq%d" % tt])
                    op("act", lambda e: e.activation(out=sc, in_=sc, func=AF.Ln), ["ssq%d" % tt], ["ssq%d" % tt])
                    op("act", lambda e: e.activation(out=sc, in_=sc, func=AF.Exp, scale=-0.5), ["ssq%d" % tt], ["ssq%d" % tt])

                stats(0)
                for tt in range(NT):
                    v = 1 if tt < 2 else 0
                    xt, xtn = xs[tt % 3], "xs%d" % (tt % 3)
                    sc = ssq[:, tt:tt + 1]
                    if tt + 1 < NT:
                        stats(tt + 1)
                    xb, xbn = xn[tt % 2], "xn%d" % (tt % 2)
                    bank = 2 + (tt % 2)
                    pT = pb[bank][:].bitcast(BF16).rearrange("p (k t) -> p k t", k=8)
                    op("dve", lambda e: e.tensor_scalar(out=xb[:], in0=xt[:], scalar1=sc, scalar2=None, op0=ALU.mult),
                       [xtn, "ssq%d" % tt], [xbn])
                    for kt in range(8):
                        op("pe", lambda e: e.transpose(out=pT[:, kt, :], in_=xb[:, kt * 128:(kt + 1) * 128], identity=ident[:]),
                           [xbn, "ident"], [PB[bank]], inc=(kt == 7))
                    for kt in range(8):
                        dst = hT[:, kt, tt * 128:(tt + 1) * 128]
                        if kt % 2 == 0:
                            op("act", lambda e: e.activation(out=dst, in_=pT[:, kt, :], func=AF.Identity, scale=gam[:, kt, v:v + 1],
                                                             bias=shf[:, kt, v:v + 1]), [PB[bank], "gam", "shf"], ["hT%d" % tt])
                        else:
                            op("dve", lambda e: e.tensor_scalar(out=dst, in0=pT[:, kt, :], scalar1=gam[:, kt, v:v + 1],
                                                                scalar2=shf[:, kt, v:v + 1], op0=ALU.mult, op1=ALU.add),
                               [PB[bank], "gam", "shf"], ["hT%d" % tt])
                fw.barrier()

        def MM(out_, lhsT, rhs, reads, writes, start=True, stop=True, inc=True):
            return op("pe", lambda e: e.matmul(out_, lhsT=lhsT, rhs=rhs, start=start, stop=stop), reads, writes, inc)

        def TR(out_, in_, idn, reads, writes, inc=True):
            return op("pe", lambda e: e.transpose(out=out_, in_=in_, identity=idn), reads, writes, inc)

        def ACTF(out_, in_, func, reads, writes, **kw):
            return op("act", lambda e: e.activation(out=out_, in_=in_, func=func, **kw), reads, writes)

        def TT(en, out_, in0, in1, alu, reads, writes):
            return op(en, lambda e: e.tensor_tensor(out=out_, in0=in0, in1=in1, op=alu), reads, writes)

        def TS(en, out_, in0, s1, s2, op0, op1, reads, writes):
            if s2 is None:
                return op(en, lambda e: e.tensor_scalar(out=out_, in0=in0, scalar1=s1, scalar2=None, op0=op0), reads, writes)
            return op(en, lambda e: e.tensor_scalar(out=out_, in0=in0, scalar1=s1, scalar2=s2, op0=op0, op1=op1), reads, writes)

        def STT(en, out_, in0, scalar, in1, op0, op1, reads, writes):
            return op(en, lambda e: e.scalar_tensor_tensor(out=out_, in0=in0, scalar=scalar, in1=in1, op0=op0, op1=op1),
                      reads, writes)

        def CP(en, out_, in_, reads, writes):
            if en == "act":
                return op("act", lambda e: e.activation(out=out_, in_=in_, func=AF.Copy), reads, writes)
            return op(en, lambda e: e.tensor_copy(out=out_, in_=in_), reads, writes)

        def hT_bufs(t0, n):
            return ["hT%d" % t for t in range(t0 // 128, (t0 + n + 127) // 128)]

        winv = w_in.rearrange("(kt p) n -> p kt n", p=128)
        GROUPS = [(0, 256)] + [(256 + 512 * i, 512) for i in range(4)]
        evac_rr = [0]

        EV_N = int(os.environ.get("EV_N", "3"))
        EV_A = int(os.environ.get("EV_A", "2"))

        def evac_copy(out_, in_, reads, writes):
            evac_rr[0] += 1
            return CP("act" if (evac_rr[0] % EV_N) < EV_A else "dve", out_, in_, reads, writes)


        dn_era = ExitStack()
        dnqT = fw.sbuf("dnqT", [128, 4, T_ALL], BF16, dn_era)
        dnkT = fw.sbuf("dnkT", [128, 4, T_ALL], BF16, dn_era)
        k_tok = fw.sbuf("k_tok", [128, NT, 8, 64], BF16, dn_era)
        v_tok = fw.sbuf("v_tok", [128, NT, 8, 64], BF16, dn_era)
        beta = fw.sbuf("beta", [128, NT, 16], F32, dn_era)
        gdec = fw.sbuf("gdec", [128, NT, 16], F32, dn_era)
        tri = fw.sbuf("tri", [128, 6, 128], F32, dn_era)
        onesbd = fw.sbuf("onesbd", [128, 128], BF16, dn_era)
        negoff = fw.sbuf("negoff", [128, 128], F32, dn_era)
        hm = fw.sbuf("hm", [128, 9, 128], BF16, dn_era)
        dma("sp", tri[:], tri_d.rearrange("k p i -> p k i"), writes=["tri"])
        dma("pool", onesbd[:], tri_d[5], writes=["onesbd"])
        dma("sp", negoff[:], negoff_d, writes=["negoff"])
        dma("pool", hm[:], hmask_d.rearrange("k p i -> p k i"), writes=["hm"])
        h_era = ExitStack()
        hT = fw.sbuf("hT", [128, 8, T_ALL], BF16, h_era)
        phase_B(hT)
        tap("hT", hT[:], [128, 8, T_ALL], BF16, ["hT%d" % t for t in range(NT)])
        hT_spill = nc.dram_tensor("hT_spill", [128, 8, T_ALL], BF16).ap()
        for kt in range(8):
            dma("sp", hT_spill[:, kt, :], hT[:, kt, :], reads=["hT%d" % t for t in range(NT)], writes=["hT_spill"], slot="spill%d" % kt)

        with ExitStack() as c1:
            wdnb = [fw.sbuf("wdn%d" % i, [128, 8, 512], BF16, c1) for i in range(2)]
            wbg = fw.sbuf("wbg", [128, 8, 32], BF16, c1)
            convT = fw.sbuf("convT", [128, 12, 5], F32, c1)
            diagwb = [fw.sbuf("diagw%d" % i, [128, 5, 128], BF16, c1) for i in range(2)]
            xc = [fw.sbuf("xc%d" % i, [128, 2312], BF16, c1) for i in range(2)]
            ys = [fw.sbuf("ys%d" % i, [128, T_ALL], F32, c1) for i in range(2)]
            sqS = [fw.sbuf("sq%d" % i, [128, T_ALL], BF16, c1) for i in range(2)]
            rsS = [fw.sbuf("rs%d" % i, [128, 512], F32, c1) for i in range(2)]
            zz = fw.sbuf("zz", [128, NT, 16], F32, c1)
            zm_ = fw.sbuf("zm_", [128, NT, 16], F32, c1)
            ze = fw.sbuf("ze", [128, NT, 16], F32, c1)
            dtbB = fw.sbuf("dtbB", [128, 16], F32, c1)
            negA = fw.sbuf("negA", [128, 16], F32, c1)
            wbg_n = fw.sbuf("wbg_n", [128, 8, 32], BF16, c1)
            dma("pool", wbg_n[:], winv[:, :, 3232:3264], writes=["wbg_n"])
            CP("dve", wbg[:].rearrange("p k (a par hp) -> p (k a) par hp", par=2, hp=4),
               wbg_n[:].rearrange("p k (a hp par) -> p (k a) par hp", par=2, hp=4), ["wbg_n"], ["wbg"])
            with nc.allow_non_contiguous_dma(reason="small conv weight column layout"):
                for j in range(5):
                    dma("sp", convT[:, :, j], conv_w[j, :].rearrange("(c p) -> p c", p=128), writes=["convT"], slot="convT%d" % j)
            dtb_n = fw.sbuf("dtb_n", [128, 16], F32, c1)
            alog_n = fw.sbuf("alog_n", [128, 16], F32, c1)
            dma("sp", dtb_n[:], dt_bias.partition_broadcast(128), writes=["dtb_n"])
            dma("sp", alog_n[:], a_log.partition_broadcast(128), writes=["alog_n"])
            CP("dve", dtbB[:].rearrange("p (a par hp) -> p a par hp", par=2, hp=4),
               dtb_n[:].rearrange("p (a hp par) -> p a par hp", par=2, hp=4), ["dtb_n"], ["dtbB"])
            CP("dve", negA[:].rearrange("p (a par hp) -> p a par hp", par=2, hp=4),
               alog_n[:].rearrange("p (a hp par) -> p a par hp", par=2, hp=4), ["alog_n"], ["negA"])
            for i in range(2):
                for lo, hi in ((0, 2), (258, 262), (2310, 2312)):
                    op("pool", lambda e: e.memset(xc[i][:, lo:hi], 0.0), (), ["xc%d" % i])
            zps = psall[:, 4:6, 0:288].rearrange("p b (t c) -> p b t c", c=32)
            for tt in range(NT):
                for kt in range(8):
                    MM(zps[:, tt // 9, tt % 9, :], hT[:, kt, tt * 128:(tt + 1) * 128], wbg[:, kt, :], ["hT%d" % tt, "wbg"],
                       [PB[4 + tt // 9]], start=(kt == 0), stop=(kt == 7), inc=(kt == 7))
            v4 = lambda t: t[:].rearrange("p (b t) c -> p b t c", b=2)
            ACTF(v4(beta), zps[:, :, :, 0:16], AF.Sigmoid, [PB[4], PB[5]], ["beta"])
            TT("dve", v4(zz), zps[:, :, :, 16:32], dtbB[:].unsqueeze(1).unsqueeze(1).to_broadcast([128, 2, 9, 16]), ALU.add,
               [PB[4], PB[5], "dtbB"], ["zz"])
            TS("dve", zm_[:], zz[:], 0.0, None, ALU.max, None, ["zz"], ["zm_"])
            STT("dve", ze[:], zm_[:], -2.0, zz[:], ALU.mult, ALU.add, ["zm_", "zz"], ["ze"])
            ACTF(ze[:], ze[:], AF.Exp, ["ze"], ["ze"])
            ACTF(ze[:], ze[:], AF.Ln, ["ze"], ["ze"], bias=1.0)
            ACTF(negA[:], negA[:], AF.Exp, ["negA"], ["negA"])
            TS("dve", negA[:], negA[:], -1.0, None, ALU.mult, None, ["negA"], ["negA"])
            TT("dve", ze[:], ze[:], zm_[:], ALU.add, ["ze", "zm_"], ["ze"])
            TT("dve", gdec[:], ze[:], negA[:].unsqueeze(1).to_broadcast([128, NT, 16]), ALU.mult, ["ze", "negA"], ["gdec"])
            def chunk_stream(c):
                which, hc = c // 4, c % 4
                par = c % 2
                xcb, xcn = xc[par], "xc%d" % par
                ysc, ysn = ys[par], "ys%d" % par
                sqc, sqn = sqS[par], "sq%d" % par
                wdn, wn = wdnb[which % 2], "wdn%d" % (which % 2)
                if hc == 0:
                    dma("pool", wdn[:], winv[:, :, 1184 + which * 512:1184 + (which + 1) * 512], writes=[wn])
                diagw, dgn = diagwb[par], "diagw%d" % par
                for j in range(5):
                    TS("dve" if j % 2 else "pool", diagw[:, j, :], identf[:], convT[:, c, j:j + 1], None, ALU.mult, None,
                       ["identf", "convT"], [dgn])
                for gi, (t0, n) in enumerate(GROUPS):
                    bk = gi % 2
                    col = t0 + 2 if t0 < 256 else t0 + 6
                    for kt in range(8):
                        MM(pb[bk][:, 0:n], wdn[:, kt, hc * 128:(hc + 1) * 128], hT[:, kt, t0:t0 + n], [wn] + hT_bufs(t0, n), [PB[bk]],
                           start=(kt == 0), stop=(kt == 7), inc=(kt == 7))
                    evac_copy(xcb[:, col:col + n], pb[bk][:, 0:n], [PB[bk]], [xcn])
                    if gi in (1, 3):
                        yield
                yield
                for gi, (t0, n) in enumerate(GROUPS):
                    bk = 2 + gi % 2
                    col = t0 + 2 if t0 < 256 else t0 + 6
                    for j in range(5):
                        MM(pb[bk][:, 0:n], diagw[:, j, :], xcb[:, col + j - 2:col + j - 2 + n], [dgn, xcn], [PB[bk]],
                           start=(j == 0), stop=(j == 4), inc=(j == 4))
                    if which == 2:
                        ACTF(sqc[:, t0:t0 + n], pb[bk][:, 0:n], AF.Silu, [PB[bk]], [sqn])
                    else:
                        ACTF(ysc[:, t0:t0 + n], pb[bk][:, 0:n], AF.Silu, [PB[bk]], [ysn])
                yield
                if which < 2:
                    TT("pool", sqc[:], ysc[:], ysc[:], ALU.mult, [ysn], [sqn])
                    yield
                    dst = (dnqT if which == 0 else dnkT)
                    dn_ = ("dnqT%d" if which == 0 else "dnkT%d") % hc
                    for gi, (t0, n) in enumerate(GROUPS):
                        bk = 4 + gi % 2
                        rsg, rsn = rsS[gi % 2], "rs%d" % (gi % 2)
                        MM(pb[bk][:, 0:n], onesbd[:], sqc[:, t0:t0 + n], ["onesbd", sqn], [PB[bk]])
                        ACTF(rsg[:, 0:n], pb[bk][:, 0:n], AF.Ln, [PB[bk]], [rsn], bias=EPS)
                        ACTF(rsg[:, 0:n], rsg[:, 0:n], AF.Exp, [rsn], [rsn], scale=-0.5)
                        if which == 0:
                            STT("dve", dst[:, hc, t0:t0 + n], ysc[:, t0:t0 + n], 0.125, rsg[:, 0:n], ALU.mult, ALU.mult, [ysn, rsn], [dn_])
                        else:
                            TT("dve", dst[:, hc, t0:t0 + n], ysc[:, t0:t0 + n], rsg[:, 0:n], ALU.mult, [ysn, rsn], [dn_])
                    yield
                if which >= 1:
                    src = dnkT[:, hc, :] if which == 1 else sqc[:]
                    srcn = ("dnkT%d" % hc) if which == 1 else sqn
                    dtok = k_tok if which == 1 else v_tok
                    dtn = "k_tok" if which == 1 else "v_tok"
                    for bi, (tt0, ntl) in enumerate(((0, 8), (8, 8), (16, 2))):
                        bk = 6 + bi % 2
                        pT = pb[bk][:].bitcast(BF16).rearrange("p (k t) -> p k t", k=8)
                        for i in range(ntl):
                            tt = tt0 + i
                            TR(pT[:, i, :], src[:, tt * 128:(tt + 1) * 128], ident[:], [srcn, "ident"], [PB[bk]], inc=(i == ntl - 1))
                        evac_copy(dtok[:, tt0:tt0 + ntl, 2 * hc:2 * hc + 2, :].rearrange("p t h d -> p t (h d)"), pT[:, 0:ntl, :],
                                  [PB[bk]], [dtn])
                    yield

            NCH = int(os.environ.get("C1_NCH", "12"))
            active = []
            nxt_c = 0
            while nxt_c < NCH or active:
                if nxt_c < NCH and len(active) < 2:
                    active.append(chunk_stream(nxt_c))
                    nxt_c += 1
                for g in list(active):
                    try:
                        next(g)
                    except StopIteration:
                        active.remove(g)
            tap("dnqT", dnqT[:], [128, 4, T_ALL], BF16, ["dnqT%d" % i for i in range(4)])
            tap("dnkT", dnkT[:], [128, 4, T_ALL], BF16, ["dnkT%d" % i for i in range(4)])
            tap("k_tok", k_tok[:], [128, NT, 8, 64], BF16, ["k_tok"])
            tap("v_tok", v_tok[:], [128, NT, 8, 64], BF16, ["v_tok"])
            tap("beta", beta[:], [128, NT, 16], F32, ["beta"])
            tap("gdec", gdec[:], [128, NT, 16], F32, ["gdec"])
            fw.barrier()
        h_era.close()
        if stop_after == "C1":
            dn_era.close()

        if stop_after != "C1":
            with ExitStack() as dd:
                A_ = lambda name, shape, dt: fw.sbuf(name, shape, dt, dd)
                rg = A_("rgEs", [128, 8, 128], F32)
                Es = rg
                DT = A_("DT", [128, 8, 128], F32)
                E_ = A_("E_", [128, 8, 128], F32)
                AqkS = [A_("Aqk%d" % i, [128, 8, 128], BF16) for i in range(3)]
                R0S = [A_("R0b%d" % i, [128, 8, 128], BF16) for i in range(2)]
                P0S = [A_("P0b%d" % i, [128, 8, 128], BF16) for i in range(2)]
                BA = [[A_("BA%d%d" % (w, i), [128, 4, 2, 128], BF16) for i in range(2)] for w in range(2)]
                BB = [[A_("BB%d%d" % (w, i), [128, 4, 2, 128], BF16) for i in range(2)] for w in range(2)]
                Wb = [[A_("Wb%d%d" % (w, i), [128, 4, 128], BF16) for i in range(2)] for w in range(2)]
                Db = [[A_("Db%d%d" % (w, i), [128, 4, 128], BF16) for i in range(2)] for w in range(2)]
                Yb = [A_("Yb%d" % w, [128, 4, 128], BF16) for w in range(2)]
                Ypb = [A_("Ypb%d" % w, [128, 4, 128], BF16) for w in range(2)]
                Cmb = [A_("Cmb%d" % i, [128, 8, 128], BF16) for i in range(2)]
                Cfb = [A_("Cfb%d" % i, [128, 8, 128], BF16) for i in range(2)]
                XTS = [A_("XT%d" % i, [128, 8, 128], BF16) for i in range(2)]
                kg = A_("kg", [128, 8, 64], BF16)
                kdb = A_("kdb", [128, 8, 64], BF16)
                up = A_("up", [128, 8, 64], F32)
                wT = A_("wT", [128, 4, 128], BF16)
                vt = A_("vt", [128, 8, 64], BF16)
                S32 = A_("S32", [128, 4, 64], F32)
                Sbf = A_("Sbf", [128, 4, 64], BF16)
                egS = [A_("eg%d" % i, [128, 16], F32) for i in range(3)]
                egl2S = [A_("egl2%d" % i, [128, 4], F32) for i in range(3)]
                eb = A_("eb", [128, 8], F32)
                otmp = A_("otmp", [128, 8, 64], F32)
                ofin = A_("ofin", [128, 8, 64], F32)
                onb = A_("onb", [128, 8, 64], BF16)
                oss = A_("oss", [128, 8], F32)
                dnwB = A_("dnwB", [128, 64], F32)
                o_acc = A_("o_acc", [128, 16, 8, 64], BF16)
                dma("sp", dnwB[:], dn_w.partition_broadcast(128), writes=["dnwB"])
                LE, LT_, GE, GT_, ONES = (tri[:, i, :] for i in range(5))

                def bc_h(m):
                    return m.unsqueeze(1).to_broadcast([128, 8, 128])

                def bc_i(v, n):
                    return v.unsqueeze(2).to_broadcast([128, 8, n])

                ps2 = lambda b0: psall[:, b0:b0 + 2, :].rearrange("p b (i c) -> p (b i) c", c=128)
                v4q = lambda t: t.rearrange("p (par hp) e -> p par hp e", par=2)
                bc4 = lambda v, n: v.rearrange("p (par hp) -> p par hp", par=2).unsqueeze(3).to_broadcast([128, 2, 4, n])
                bcm = lambda m_: m_.unsqueeze(1).to_broadcast([128, 4, 128])
                A4 = lambda bk: pb[bk].rearrange("p (i c) -> p i c", c=128)
                A8 = lambda bk: psall[:, bk:bk + 2, :].rearrange("p b (i c) -> p (b i) c", c=256)

                DN_NT = int(os.environ.get("DN_NT", "18"))

                def stream_P(d, n, tt):
                    (Mc, Mrest, Mr, Ml, Mi) = (LE, GT_, LE, GT_, LE) if d == 0 else (GE, LT_, GE, LT_, GE)
                    lat = tt >= 2
                    tok = slice(tt * 128, (tt + 1) * 128)
                    g_t = gdec[:, tt, d * 8:(d + 1) * 8]
                    b_t = beta[:, tt, d * 8:(d + 1) * 8]
                    eg, egn = egS[n % 3], "eg%d" % (n % 3)
                    egl2, egl2n = egl2S[n % 3], "egl2%d" % (n % 3)
                    Aqk, Aqkn = AqkS[n % 3], "Aqk%d" % (n % 3)
                    R0b, R0n = R0S[n % 2], "R0b%d" % (n % 2)
                    P0b, P0n = P0S[n % 2], "P0b%d" % (n % 2)
                    small = pb[7]
                    MM(small[:, 0:8], Mc, g_t, ["tri", "gdec"], [PB[7]], inc=False)
                    MM(small[:, 8:16], Mrest, g_t, ["tri", "gdec"], [PB[7]], inc=False)
                    MM(small[:, 16:24], ONES, g_t, ["tri", "gdec"], [PB[7]])
                    ACTF(eg[:], small[:, 0:16], AF.Exp, [PB[7]], [egn])
                    ACTF(egl2[0:64, :], small[0:64, 16:20], AF.Exp, [PB[7]], [egl2n])
                    ACTF(egl2[64:128, :], small[64:128, 20:24], AF.Exp, [PB[7]], [egl2n])
                    TT("pool", rg[:], bc_h(Mr), bc_i(g_t, 128), ALU.mult, ["tri", "gdec"], ["rgEs"])
                    yield
                    for hh in range(2):
                        MM(pb[4 + hh][:], Ml, rg[:, hh * 4:(hh + 1) * 4, :].rearrange("p h i -> p (h i)"), ["tri", "rgEs"], [PB[4 + hh]])
                    ACTF(DT[:], ps2(4), AF.Exp, [PB[4], PB[5]], ["DT"])
                    TT("dve", DT[:], DT[:], bc_h(Mi), ALU.mult, ["DT", "tri"], ["DT"])
                    yield
                    TT("dve", E_[:], DT[:], bc_i(b_t, 128), ALU.mult, ["DT", "beta"], ["E_"])
                    TT("pool", Es[:], E_[:], bc_h(negoff[:]), ALU.mult, ["E_", "negoff"], ["rgEs"])
                    yield
                    for h in range(8):
                        pr, hp = (h % 2) * 64, h // 2
                        kT_h = dnkT[pr:pr + 64, hp, tok]
                        MM(psall[:, 4 + h % 2, hp * 128:(hp + 1) * 128], kT_h, kT_h, ["dnkT%d" % hp], [PB[4 + h % 2]], inc=(h >= 6))
                    if lat:
                        for h in range(8):
                            pr, hp = (h % 2) * 64, h // 2
                            MM(psall[:, 6 + h % 2, hp * 128:(hp + 1) * 128], dnkT[pr:pr + 64, hp, tok],
                               dnqT[pr:pr + 64, hp, tok], ["dnkT%d" % hp, "dnqT%d" % hp], [PB[6 + h % 2]], inc=(h >= 6))
                    TT("dve", R0b[:], ps2(4), Es[:], ALU.mult, [PB[4], PB[5], "rgEs"], [R0n])
                    if lat:
                        TT("dve", Aqk[:], ps2(6), E_[:], ALU.mult, [PB[6], PB[7], "E_"], [Aqkn])
                    yield
                    pT = pb[6][:].bitcast(BF16).rearrange("p (k t) -> p k t", k=8)
                    for q in range(8):
                        TR(pT[:, q, :], R0b[:, q, :], ident[:], [R0n, "ident"], [PB[6]], inc=(q == 7))
                    CP("act", P0b[:], pT, [PB[6]], [P0n])
                    yield

                def stream_I(d, n, tt):
                    R0b, R0n = R0S[n % 2], "R0b%d" % (n % 2)
                    P0b, P0n = P0S[n % 2], "P0b%d" % (n % 2)
                    XT, XTn = XTS[n % 2], "XT%d" % (n % 2)
                    for w in range(2):
                        sl = slice(w * 4, (w + 1) * 4)
                        TT(os.environ.get("BA_ENG", "pool"), BA[w][0][:, :, 0, :], R0b[:, sl, :], bcm(hm[:, 0, :]), ALU.mult, [R0n, "hm"], ["BA%d0" % w])
                        TT(os.environ.get("BB_ENG", "dve"), BB[w][0][:, :, 0, :], P0b[:, sl, :], bcm(hm[:, 0, :]), ALU.mult, [P0n, "hm"], ["BB%d0" % w])
                        TT(os.environ.get("BA_ENG", "pool"), BA[w][1][:, :, 1, :], BA[w][0][:, :, 0, :], bcm(ident[:]), ALU.add, ["BA%d0" % w, "ident"], ["BA%d1" % w])
                    yield
                    CmAll = Cmb + Cfb
                    for li in range(4):
                        mnat = hm[:, (1 + li) if d == 0 else (5 + li), :]
                        TT("pool", CmAll[li][:], P0b[:], bc_h(mnat), ALU.mult, [P0n, "hm"], ["CmAll%d" % li])
                    for w in range(2):
                        b0 = 2 * w
                        for i in range(4):
                            MM(A4(b0)[:, i, :], BA[w][0][:, i, 0, :], BB[w][0][:, i, 0, :], ["BA%d0" % w, "BB%d0" % w], [PB[b0]], inc=False)
                            MM(A4(b0 + 1)[:, i, :], BB[w][0][:, i, 0, :], BA[w][0][:, i, 0, :], ["BA%d0" % w, "BB%d0" % w], [PB[b0 + 1]],
                               inc=(i == 3))
                        evac_copy(BB[w][1][:, :, 0, :], A4(b0), [PB[b0]], ["BB%d1" % w])
                        evac_copy(BA[w][1][:, :, 0, :], A4(b0 + 1), [PB[b0 + 1]], ["BA%d1" % w])
                    yield
                    for w in range(2):
                        b0 = 2 * w
                        for i in range(4):
                            MM(A8(b0)[:, i, :], BB[w][1][:, i, 0, :], BA[w][1][:, i, :, :].rearrange("p a c -> p (a c)"),
                               ["BA%d1" % w, "BB%d1" % w], [PB[b0 + i // 2]], start=True, stop=False, inc=False)
                            MM(A8(b0)[:, i, 128:256], ident[:], BA[w][1][:, i, 1, :], ["ident", "BA%d1" % w], [PB[b0 + i // 2]],
                               start=False, stop=True, inc=(i == 3))
                        evac_copy(BA[w][0][:].rearrange("p i a c -> p i (a c)"), A8(b0), [PB[b0], PB[b0 + 1]], ["BA%d0" % w])
                    for w in range(2):
                        b0 = 4 + 2 * w
                        for i in range(4):
                            MM(A4(b0)[:, i, :], BA[w][1][:, i, 0, :], BB[w][1][:, i, 0, :], ["BA%d1" % w, "BB%d1" % w], [PB[b0]], inc=(i == 3))
                        evac_copy(BB[w][0][:, :, 0, :], A4(b0), [PB[b0]], ["BB%d0" % w])
                    yield
                    for w in range(2):
                        b0 = 2 * w
                        for i in range(4):
                            MM(A4(b0)[:, i, :], BB[w][0][:, i, 0, :], BA[w][0][:, i, 1, :], ["BA%d0" % w, "BB%d0" % w], [PB[b0]],
                               start=True, stop=False, inc=False)
                            MM(A4(b0)[:, i, :], ident[:], BA[w][0][:, i, 1, :], ["ident", "BA%d0" % w], [PB[b0]], start=False, stop=True,
                               inc=(i == 3))
                        evac_copy(Wb[w][0][:], A4(b0), [PB[b0]], ["Wb%d0" % w])
                    yield
                    for li in range(4):
                        cur, nxt = li % 2, (li + 1) % 2
                        last = (li == 3)
                        Cm, Cmn = CmAll[li], "CmAll%d" % li
                        for w in range(2):
                            b0 = 2 * w
                            Wc, Wcn, Dc, Dcn = Wb[w][cur], "Wb%d%d" % (w, cur), Db[w][cur], "Db%d%d" % (w, cur)
                            pTw = pb[b0 + 1][:].bitcast(BF16).rearrange("p (k t) -> p k t", k=8)
                            for i in range(4):
                                q = w * 4 + i
                                MM(A4(b0)[:, i, :], Cm[:, q, :], Wc[:, i, :], [Cmn, Wcn], [PB[b0]], inc=False)
                                TR(pTw[:, i, :], Wc[:, i, :], ident[:], [Wcn, "ident"], [PB[b0 + 1]], inc=(i == 3))
                            evac_copy(Yb[w][:], A4(b0), [PB[b0]], ["Yb%d" % w])
                            evac_copy(Dc[:], pTw[:, 0:4, :], [PB[b0 + 1]], [Dcn])
                        yield
                        for w in range(2):
                            b0 = 2 * w
                            Wc, Wcn, Dc, Dcn = Wb[w][cur], "Wb%d%d" % (w, cur), Db[w][cur], "Db%d%d" % (w, cur)
                            for i in range(4):
                                MM(A4(b0)[:, i, :], Dc[:, i, :], Yb[w][:, i, :], [Dcn, "Yb%d" % w], [PB[b0]], start=True, stop=False, inc=False)
                                MM(A4(b0)[:, i, :], ident[:], Wc[:, i, :], ["ident", Wcn], [PB[b0]], start=False, stop=True, inc=(i == 3))
                            if last:
                                evac_copy(XT[:, w * 4:(w + 1) * 4, :], A4(b0), [PB[b0]], [XTn])
                            else:
                                evac_copy(Wb[w][nxt][:], A4(b0), [PB[b0]], ["Wb%d%d" % (w, nxt)])
                        yield

                def stream_C(d, n, tt):
                    lat = tt >= 2
                    lt = tt - 2
                    tok = slice(tt * 128, (tt + 1) * 128)
                    b_t = beta[:, tt, d * 8:(d + 1) * 8]
                    eg, egn = egS[n % 3], "eg%d" % (n % 3)
                    egl2, egl2n = egl2S[n % 3], "egl2%d" % (n % 3)
                    Aqk, Aqkn = AqkS[n % 3], "Aqk%d" % (n % 3)
                    XT, XTn = XTS[n % 2], "XT%d" % (n % 2)
                    ktn = k_tok[:, tt, :, :].rearrange("p (hp par) e -> p par hp e", par=2)
                    TT("pool", v4q(kg[:]), ktn, bc4(eg[:, 0:8], 64), ALU.mult, ["k_tok", egn], ["kg"])
                    TT("pool", eb[:], eg[:, 8:16], b_t, ALU.mult, [egn, "beta"], ["eb"])
                    TT("pool", v4q(kdb[:]), ktn, bc4(eb[:], 64), ALU.mult, ["k_tok", "eb"], ["kdb"])
                    up_ps = pb[4].rearrange("p (q e) -> p q e", e=64)
                    for h in range(8):
                        q = (h % 2) * 4 + h // 2
                        MM(up_ps[:, q, :], XT[:, q, :], v_tok[:, tt, h, :], [XTn, "v_tok"], [PB[4]], inc=(h == 7))
                    for h in range(8):
                        par, hp = h % 2, h // 2
                        q = par * 4 + hp
                        MM(psall[par * 64:(par + 1) * 64, 5, hp * 128:(hp + 1) * 128], kg[:, q, :], XT[:, q, :], ["kg", XTn],
                           [PB[5]], inc=(h == 7))
                    CP("act", up[:], up_ps, [PB[4]], ["up"])
                    CP("act", wT[:], pb[5].rearrange("p (a i) -> p a i", i=128), [PB[5]], ["wT"])
                    yield
                    wS_ps = psall[:, 6:8, 0:256].rearrange("p b (hp e) -> p b hp e", e=64)
                    for h in range(8):
                        par, hp = h % 2, h // 2
                        pr = par * 64
                        MM(wS_ps[:, par, hp, :], wT[pr:pr + 64, hp, :], Sbf[pr:pr + 64, hp, :], ["wT", "Sbf"], [PB[6 + par]], inc=(h >= 6))
                    TT("dve", v4q(vt[:]), v4q(up[:]), wS_ps, ALU.subtract, ["up", PB[6], PB[7]], ["vt"])
                    yield
                    if lat:
                        qS_ps = psall[:, 4:6, 0:256].rearrange("p b (hp e) -> p b hp e", e=64)
                        Av_ps = pb[6].rearrange("p (q e) -> p q e", e=64)
                        for h in range(8):
                            par, hp = h % 2, h // 2
                            pr = par * 64
                            MM(qS_ps[:, par, hp, :], dnqT[pr:pr + 64, hp, tok], Sbf[pr:pr + 64, hp, :], ["dnqT%d" % hp, "Sbf"],
                               [PB[4 + par]], inc=(h >= 6))
                        for q in range(8):
                            MM(Av_ps[:, q, :], Aqk[:, q, :], vt[:, q, :], [Aqkn, "vt"], [PB[6]], inc=(q == 7))
                    for h in range(8):
                        par, hp = h % 2, h // 2
                        q = par * 4 + hp
                        MM(psall[par * 64:(par + 1) * 64, 7, hp * 64:(hp + 1) * 64], kdb[:, q, :], vt[:, q, :], ["kdb", "vt"],
                           [PB[7]], inc=(h == 7))
                    TT("dve", S32[:], S32[:], egl2[:].unsqueeze(2).to_broadcast([128, 4, 64]), ALU.mult, ["S32", egl2n], ["S32"])
                    TT("dve", S32[:], S32[:], pb[7][:, 0:256].rearrange("p (a e) -> p a e", e=64), ALU.add, ["S32", PB[7]], ["S32"])
                    CP("act", Sbf[:], S32[:], ["S32"], ["Sbf"])
                    if lat:
                        TT("dve", v4q(otmp[:]), qS_ps, bc4(eg[:, 0:8], 64), ALU.mult, [PB[4], PB[5], egn], ["otmp"])
                        if d == 0:
                            TT("dve", o_acc[:, lt, :, :], otmp[:], Av_ps, ALU.add, ["otmp", PB[6]], ["o_acc%d" % lt])
                        else:
                            TT("dve", ofin[:], otmp[:], Av_ps, ALU.add, ["otmp", PB[6]], ["ofin"])
                    if os.environ.get("DN_TAP") == "%d,%d" % (d, tt):
                        tap("eg", eg[:], [128, 16], F32, [egn])
                        tap("XT", XT[:], [128, 8, 128], BF16, [XTn])
                        tap("up", up[:], [128, 8, 64], F32, ["up"])
                        tap("wT", wT[:], [128, 4, 128], BF16, ["wT"])
                        tap("vt", vt[:], [128, 8, 64], BF16, ["vt"])
                        tap("S32", S32[:], [128, 4, 64], F32, ["S32"])
                        tap("kg", kg[:], [128, 8, 64], BF16, ["kg"])
                        tap("R0", R0S[n % 2][:], [128, 8, 128], BF16, ["R0b%d" % (n % 2)])
                    yield
                    if lat:
                        if d == 1:
                            TT("pool", ofin[:], ofin[:], o_acc[:, lt, :, :], ALU.add, ["ofin", "o_acc%d" % lt], ["ofin"])
                            TT("pool", otmp[:], ofin[:], ofin[:], ALU.mult, ["ofin"], ["otmp"])
                            op("dve", lambda e: e.tensor_reduce(out=oss[:], in_=otmp[:], axis=AX.X, op=ALU.add), ["otmp"], ["oss"])
                            TS("dve", oss[:], oss[:], 1.0 / 64, EPS, ALU.mult, ALU.add, ["oss"], ["oss"])
                            ACTF(oss[:], oss[:], AF.Ln, ["oss"], ["oss"])
                            ACTF(oss[:], oss[:], AF.Exp, ["oss"], ["oss"], scale=-0.5)
                            TT("dve", ofin[:], ofin[:], bc_i(oss[:], 64), ALU.mult, ["ofin", "oss"], ["ofin"])
                            TT("dve", onb[:].rearrange("p (hp par) e -> p par hp e", par=2), v4q(ofin[:]),
                               dnwB[:].unsqueeze(1).unsqueeze(1).to_broadcast([128, 2, 4, 64]), ALU.mult, ["ofin", "dnwB"], ["onb"])
                            yield
                            pT = pb[5][:].bitcast(BF16).rearrange("p (k t) -> p k t", k=8)
                            onv = onb[:].rearrange("p (a b) e -> p a (b e)", b=2)
                            for hp in range(4):
                                TR(pT[:, 4 + hp, :], onv[:, hp, :], ident[:], ["onb", "ident"], [PB[5]], inc=(hp == 3))
                            CP("act", odT[:, :, lt * 128:(lt + 1) * 128], pT[:, 4:8, :], [PB[5]], ["odT"])
                    yield

                def run_streams(gens, periods=None):
                    items = [(g, (periods[k] if periods else 1)) for k, g in enumerate(gens) if g is not None]
                    rnd = 0
                    while items:
                        for it in list(items):
                            g, per = it
                            if rnd % per:
                                continue
                            try:
                                next(g)
                            except StopIteration:
                                items.remove(it)
                        rnd += 1

                for d in range(int(os.environ.get("DN_PASSES", "2"))):
                    order = list(range(NT)) if d == 0 else [1, 0] + list(range(NT - 1, 1, -1))
                    order = order[:DN_NT]
                    op("pool", lambda e: e.memset(S32[:], 0.0), (), ["S32"])
                    op("pool", lambda e: e.memset(Sbf[:], 0.0), (), ["Sbf"])
                    nn = len(order)
                    run_streams([stream_P(d, 0, order[0])])
                    for n in range(nn):
                        run_streams([stream_I(d, n, order[n]),
                                     stream_P(d, n + 1, order[n + 1]) if n + 1 < nn else None,
                                     stream_C(d, n - 1, order[n - 1]) if n >= 1 else None],
                                    periods=[int(os.environ.get("PER_I", "1")), int(os.environ.get("PER_P", "1")), int(os.environ.get("PER_C", "1"))])
                    run_streams([stream_C(d, nn - 1, order[nn - 1])])
                tap("odT", odT[:], [128, 4, T_LAT], BF16, ["odT"])
                fw.barrier()
            dn_era.close()

        if stop_after not in ("C1", "D", "A"):
            omT = fw.sbuf("omT", [128, 4, T_LAT], BF16)
            hT = fw.sbuf("hT2", [128, 8, T_ALL], BF16)
            for kt in range(8):
                dma("sp", hT[:, kt, :], hT_spill[:, kt, :], reads=["hT_spill"], writes=["hT%d" % t for t in range(NT)], slot="fill%d" % kt)
            mla_era = ExitStack()
            qT_all = fw.sbuf("qT_all", [128, 8, T_LAT], BF16, mla_era)
            kT_all = fw.sbuf("kT_all", [128, 8, T_ALL], BF16, mla_era)
            V_all = fw.sbuf("V_all", [128, NT, 8, 65], BF16, mla_era)
            negC = fw.sbuf("negC", [128, 1], F32, mla_era)
            with ExitStack() as c2:
                A_ = lambda name, shape, dt: fw.sbuf(name, shape, dt, c2)
                wtok = A_("wtok", [128, 8, 672], BF16)
                wuq = A_("wuq", [128, 3, 768], BF16)
                wukv = A_("wukv", [128, 2, 1024], BF16)
                qnwT = A_("qnwT", [128, 3], F32)
                kvnwT = A_("kvnwT", [128, 2], F32)
                qhwB = A_("qhwB", [128, 96], F32)
                khwB = A_("khwB", [128, 96], F32)
                cosT = A_("cosT", [128, 16, 16], F32)
                sinT = A_("sinT", [128, 16, 16], F32)
                invn = A_("invn", [128, 2], F32)
                ssA = A_("ssA", [128, 4], F32)
                rs2 = A_("rs2", [128, 2], F32)
                junk2 = A_("junk2", [128, 384], BF16)
                cqn = A_("cqn", [128, 384], BF16)
                ckvn = A_("ckvn", [128, 256], BF16)
                cqnT = A_("cqnT", [128, 3, 128], BF16)
                ckvnT = A_("ckvnT", [128, 2, 128], BF16)
                sqq = A_("sqq", [128, 8, 96], F32)
                ss16 = A_("ss16", [128, 16], F32)
                q_fin = A_("q_fin", [128, 8, 96], BF16)
                k_fin = A_("k_fin", [128, 8, 96], BF16)
                tl = A_("tl", [128, 8, 32], F32)
                ra = A_("ra", [128, 8, 2, 8], F32)
                rb = A_("rb", [128, 8, 2, 8], F32)
                cmx = A_("cmx", [128, 4], F32)
                dma("pool", wtok[:], winv[:, :, 0:672], writes=["wtok"])
                dma("pool", wuq[:], w_uq.rearrange("(kt p) n -> p kt n", p=128), writes=["wuq"])
                dma("pool", wukv[:], w_ukv.rearrange("(kt p) n -> p kt n", p=128), writes=["wukv"])
                with nc.allow_non_contiguous_dma(reason="small vector column layouts"):
                    dma("sp", qnwT[:], q_norm_w.rearrange("(kt p) -> p kt", p=128), writes=["qnwT"])
                    dma("sp", kvnwT[:], kv_norm_w.rearrange("(kt p) -> p kt", p=128), writes=["kvnwT"])
                    dma("sp", cosT[:], cos_d.rearrange("(t p) c -> p t c", p=128), writes=["cosT"])
                    dma("sp", sinT[:], sin_d.rearrange("(t p) c -> p t c", p=128), writes=["sinT"])
                dma("sp", qhwB[:], qh_w.partition_broadcast(128), writes=["qhwB"])
                dma("sp", khwB[:], kh_w.partition_broadcast(128), writes=["khwB"])
                op("pool", lambda e: e.memset(ssA[:], 1.0), (), ["ssA"])
                op("pool", lambda e: e.memset(invn[:, 0:1], 1.0 / 384), (), ["invn"])
                op("pool", lambda e: e.memset(invn[:, 1:2], 1.0 / 256), (), ["invn"])
                op("pool", lambda e: e.memset(V_all[:, :, :, 64:65], 1.0), (), ["V_all"])
                TT("dve", sqq[:, 0, :], qhwB[:], qhwB[:], ALU.mult, ["qhwB"], ["sqq"])
                TT("dve", sqq[:, 1, :], khwB[:], khwB[:], ALU.mult, ["khwB"], ["sqq"])
                op("dve", lambda e: e.tensor_reduce(out=cmx[:, 0:2], in_=sqq[:, 0:2, :], axis=AX.X, op=ALU.max), ["sqq"], ["cmx"])
                TT("dve", cmx[:, 2:3], cmx[:, 0:1], cmx[:, 1:2], ALU.mult, ["cmx"], ["cmx"])
                ACTF(cmx[:, 2:3], cmx[:, 2:3], AF.Ln, ["cmx"], ["cmx"])
                ACTF(cmx[:, 3:4], cmx[:, 2:3], AF.Exp, ["cmx"], ["cmx"], scale=0.5)
                TS("dve", negC[:], cmx[:, 3:4], -math.sqrt(96.0), None, ALU.mult, None, ["cmx"], ["negC"])
                v8 = lambda bk: psall[:, bk:bk + 2, :].rearrange("p b (h c) -> p (b h) c", c=128)
                qfS = [A_("qfS%d" % i, [128, 8, 96], F32) for i in range(2)]
                kvfS = [A_("kvfS%d" % i, [128, 8, 128], F32) for i in range(2)]
                krsS = [A_("krsS%d" % i, [128, 32], F32) for i in range(2)]
                skrS = [A_("skrS%d" % i, [128, 1], F32) for i in range(2)]
                bq = lambda v_, n: v_.unsqueeze(2).to_broadcast([128, 8, n])
                bh = lambda v_, n: v_.unsqueeze(1).to_broadcast([128, 8, n])
                r4 = lambda t_, b_: t_.rearrange("p h (a b f) -> p h a b f", a=2, b=2)[:, :, :, b_, :]

                def stream_X(tt):
                    lat = tt >= 2
                    tok = slice(tt * 128, (tt + 1) * 128)
                    par = tt % 2
                    hb = ["hT%d" % tt]
                    qf_, qfn = qfS[par], "qfS%d" % par
                    kvf, kvfn = kvfS[par], "kvfS%d" % par
                    krs_, krsn = krsS[par], "krsS%d" % par
                    skr, skrn = skrS[par], "skrS%d" % par
                    p_cq, p_kv = pb[0][:, 0:384], pb[1][:, 0:288]
                    for kt in range(8):
                        if lat:
                            MM(p_cq, hT[:, kt, tok], wtok[:, kt, 0:384], hb + ["wtok"], [PB[0]], start=(kt == 0), stop=(kt == 7), inc=False)
                        MM(p_kv, hT[:, kt, tok], wtok[:, kt, 384:672], hb + ["wtok"], [PB[1]], start=(kt == 0), stop=(kt == 7), inc=(kt == 7))
                    if lat:
                        op("act", lambda e: e.activation(out=junk2[:], in_=p_cq, func=AF.Square, accum_out=ssA[:, 0:1]), [PB[0]], ["junk2", "ssA"])
                    op("act", lambda e: e.activation(out=junk2[:, 0:256], in_=p_kv[:, 0:256], func=AF.Square, accum_out=ssA[:, 1:2]),
                       [PB[1]], ["junk2", "ssA"])
                    op("act", lambda e: e.activation(out=junk2[:, 0:32], in_=p_kv[:, 256:288], func=AF.Square, accum_out=skr[:]),
                       [PB[1]], ["junk2", skrn])
                    TT("dve", rs2[:], ssA[:, 0:2], invn[:], ALU.mult, ["ssA", "invn"], ["rs2"])
                    TS("dve", rs2[:], rs2[:], EPS, None, ALU.add, None, ["rs2"], ["rs2"])
                    ACTF(rs2[:], rs2[:], AF.Ln, ["rs2"], ["rs2"])
                    ACTF(rs2[:], rs2[:], AF.Exp, ["rs2"], ["rs2"], scale=-0.5)
                    yield
                    if lat:
                        ACTF(cqn[:], p_cq, AF.Identity, [PB[0], "rs2"], ["cqn"], scale=rs2[:, 0:1])
                    TS("dve", ckvn[:], p_kv[:, 0:256], rs2[:, 1:2], None, ALU.mult, None, [PB[1], "rs2"], ["ckvn"])
                    CP("act", krs_[:], p_kv[:, 256:288], [PB[1]], [krsn])
                    pT = pb[2][:].bitcast(BF16).rearrange("p (k t) -> p k t", k=8)
                    if lat:
                        for i in range(3):
                            TR(pT[:, i, :], cqn[:, i * 128:(i + 1) * 128], ident[:], ["cqn", "ident"], [PB[2]], inc=False)
                    for i in range(2):
                        TR(pT[:, 3 + i, :], ckvn[:, i * 128:(i + 1) * 128], ident[:], ["ckvn", "ident"], [PB[2]], inc=(i == 1))
                    yield
                    if lat:
                        TT("dve", cqnT[:], pT[:, 0:3, :], qnwT[:].unsqueeze(2).to_broadcast([128, 3, 128]), ALU.mult, [PB[2], "qnwT"], ["cqnT"])
                    TT("dve", ckvnT[:], pT[:, 3:5, :], kvnwT[:].unsqueeze(2).to_broadcast([128, 2, 128]), ALU.mult, [PB[2], "kvnwT"], ["ckvnT"])
                    if lat:
                        for (c0, c1, bk) in ((0, 512, 3), (512, 768, 4)):
                            for kt in range(3):
                                MM(pb[bk][:, 0:c1 - c0], cqnT[:, kt, :], wuq[:, kt, c0:c1], ["cqnT", "wuq"], [PB[bk]],
                                   start=(kt == 0), stop=(kt == 2), inc=(kt == 2))
                    for nh in range(2):
                        for kt in range(2):
                            MM(pb[5 + nh][:], ckvnT[:, kt, :], wukv[:, kt, nh * 512:(nh + 1) * 512], ["ckvnT", "wukv"], [PB[5 + nh]],
                               start=(kt == 0), stop=(kt == 1), inc=(kt == 1))
                    yield
                    qff = qf_[:].rearrange("p h c -> p (h c)")
                    kvff = kvf[:].rearrange("p h c -> p (h c)")
                    if lat:
                        evac_copy(qff[:, 0:512], pb[3][:], [PB[3]], [qfn])
                        evac_copy(qff[:, 512:768], pb[4][:, 0:256], [PB[4]], [qfn])
                    evac_copy(kvff[:, 0:512], pb[5][:], [PB[5]], [kvfn])
                    evac_copy(kvff[:, 512:1024], pb[6][:], [PB[6]], [kvfn])
                    yield

                def stream_Y(tt):
                    sqk = sqq[:, :, 0:64]
                    lat = tt >= 2
                    lt = tt - 2
                    tok = slice(tt * 128, (tt + 1) * 128)
                    par = tt % 2
                    qf, qfn = qfS[par], "qfS%d" % par
                    kvv, kvfn = kvfS[par], "kvfS%d" % par
                    krs, krsn = krsS[par], "krsS%d" % par
                    skr, skrn = skrS[par], "skrS%d" % par
                    if lat:
                        TT("pool", sqq[:], qf[:], qf[:], ALU.mult, [qfn], ["sqq"])
                        op("dve", lambda e: e.tensor_reduce(out=ss16[:, 0:8], in_=sqq[:], axis=AX.X, op=ALU.add), ["sqq"], ["ss16"])
                    else:
                        op("pool", lambda e: e.memset(ss16[:, 0:8], 1.0), (), ["ss16"])
                    TT("pool", sqk, kvv[:, :, 0:64], kvv[:, :, 0:64], ALU.mult, [kvfn], ["sqq"])
                    op("dve", lambda e: e.tensor_reduce(out=ss16[:, 8:16], in_=sqk, axis=AX.X, op=ALU.add), ["sqq"], ["ss16"])
                    TS("dve", ss16[:, 8:16], ss16[:, 8:16], skr[:], None, ALU.add, None, ["ss16", skrn], ["ss16"])
                    TS("dve", ss16[:], ss16[:], 1.0 / 96, EPS, ALU.mult, ALU.add, ["ss16"], ["ss16"])
                    ACTF(ss16[:], ss16[:], AF.Ln, ["ss16"], ["ss16"])
                    ACTF(ss16[:], ss16[:], AF.Exp, ["ss16"], ["ss16"], scale=-0.5)
                    yield

                    def rope(src_t, dst_fin, cs, sn):
                        cB = cs.rearrange("p (a f) -> p a f", a=2).unsqueeze(1).to_broadcast([128, 8, 2, 8])
                        sB = sn.rearrange("p (a f) -> p a f", a=2).unsqueeze(1).to_broadcast([128, 8, 2, 8])
                        t1, t2 = r4(src_t, 0), r4(src_t, 1)
                        o1, o2 = r4(dst_fin, 0), r4(dst_fin, 1)
                        TT("dve", ra[:], t1, cB, ALU.mult, ["tl", "cosT"], ["ra"])
                        TT("pool", rb[:], t2, sB, ALU.mult, ["tl", "sinT"], ["rb"])
                        TT("dve", o1, ra[:], rb[:], ALU.subtract, ["ra", "rb"], ["fin"])
                        TT("dve", ra[:], t1, sB, ALU.mult, ["tl", "sinT"], ["ra"])
                        TT("pool", rb[:], t2, cB, ALU.mult, ["tl", "cosT"], ["rb"])
                        TT("dve", o2, ra[:], rb[:], ALU.add, ["ra", "rb"], ["fin"])

                    if lat:
                        TT("dve", qf[:], qf[:], bq(ss16[:, 0:8], 96), ALU.mult, [qfn, "ss16"], [qfn])
                        TT("dve", q_fin[:, :, 0:64], qf[:, :, 0:64], bh(qhwB[:, 0:64], 64), ALU.mult, [qfn, "qhwB"], ["fin"])
                        TT("dve", tl[:], qf[:, :, 64:96], bh(qhwB[:, 64:96], 32), ALU.mult, [qfn, "qhwB"], ["tl"])
                        rope(tl[:], q_fin[:, :, 64:96], cosT[:, lt, :], sinT[:, lt, :])
                        pq = pb[7][:].bitcast(BF16).rearrange("p (k t) -> p k t", k=8)
                        for h in range(8):
                            TR(pq[0:96, h, :], q_fin[:, h, :], ident[:], ["fin", "ident"], [PB[7]], inc=(h == 7))
                        evac_copy(qT_all[0:96, :, lt * 128:(lt + 1) * 128], pq[0:96, :, :], [PB[7]], ["qT_all"])
                    yield
                    TT("dve", sqk, kvv[:, :, 0:64], bq(ss16[:, 8:16], 64), ALU.mult, [kvfn, "ss16"], ["sqq"])
                    TT("dve", k_fin[:, :, 0:64], sqk, bh(khwB[:, 0:64], 64), ALU.mult, ["sqq", "khwB"], ["fin"])
                    TT("dve", tl[:], bh(krs[:], 32), bq(ss16[:, 8:16], 32), ALU.mult, [krsn, "ss16"], ["tl"])
                    TT("dve", tl[:], tl[:], bh(khwB[:, 64:96], 32), ALU.mult, ["tl", "khwB"], ["tl"])
                    if lat:
                        rope(tl[:], k_fin[:, :, 64:96], cosT[:, lt, :], sinT[:, lt, :])
                    else:
                        CP("act", k_fin[:, :, 64:96], tl[:], ["tl"], ["fin"])
                    CP("act", V_all[:, tt, :, 0:64], kvv[:, :, 64:128], [kvfn], ["V_all"])
                    pk = pb[7][:].bitcast(BF16).rearrange("p (k t) -> p k t", k=8)
                    for h in range(8):
                        TR(pk[0:96, h, :], k_fin[:, h, :], ident[:], ["fin", "ident"], [PB[7]], inc=(h == 7))
                    evac_copy(kT_all[0:96, :, tok], pk[0:96, :, :], [PB[7]], ["kT_all"])
                    yield

                def run2(gens):
                    gens = [g for g in gens if g is not None]
                    while gens:
                        for g in list(gens):
                            try:
                                next(g)
                            except StopIteration:
                                gens.remove(g)

                run2([stream_X(0)])
                for tt in range(NT):
                    run2([stream_Y(tt), stream_X(tt + 1) if tt + 1 < NT else None])
                tap("qT_all", qT_all[0:96, :, :], [96, 8, T_LAT], BF16, ["qT_all"])
                tap("kT_all", kT_all[0:96, :, :], [96, 8, T_ALL], BF16, ["kT_all"])
                tap("V_all", V_all[:], [128, NT, 8, 65], BF16, ["V_all"])
                fw.barrier()

            if stop_after != "C2":
                with ExitStack() as pe_:
                    PT = [fw.sbuf("PT%d" % i, [128, 2, 512], BF16, pe_) for i in range(2)]
                    o_tok = fw.sbuf("o_tok", [128, 4, 512], BF16, pe_)
                    rec = fw.sbuf("rec", [128, 4], F32, pe_)
                    SCALE = 96.0 ** -0.5
                    steps = [(g, h, jp) for g in range(4) for h in range(8) for jp in range(9)]

                    def acc_of(g, h):
                        ab = 4 + ((g * 8 + h) % 2)
                        return ab, pb[ab][:, 0:260].rearrange("p (q c) -> p q c", c=65)

                    def scores(k):
                        g, h, jp = steps[k]
                        sb = 2 * (k % 2)
                        for t in range(2):
                            kt_ = 2 * jp + t
                            MM(pb[sb + t][:], kT_all[0:96, h, kt_ * 128:(kt_ + 1) * 128], qT_all[0:96, h, g * 512:(g + 1) * 512],
                               ["kT_all", "qT_all"], [PB[sb + t]], inc=(t == 1))

                    def expo(k):
                        sb = 2 * (k % 2)
                        Pt, Ptn = PT[k % 2], "PT%d" % (k % 2)
                        op("act", lambda e: e.activation(out=Pt[:], in_=psall[:, sb:sb + 2, :], func=AF.Exp, scale=SCALE,
                                                         bias=negC[:]), [PB[sb], PB[sb + 1], "negC"], [Ptn])

                    def pv(k):
                        g, h, jp = steps[k]
                        ab, acc = acc_of(g, h)
                        Pt, Ptn = PT[k % 2], "PT%d" % (k % 2)
                        for t in range(2):
                            kt_ = 2 * jp + t
                            for qs in range(4):
                                first = (jp == 0 and t == 0 and qs == 0)
                                lastmm = (jp == 8 and t == 1 and qs == 3)
                                op("pe", lambda e: e.matmul(acc[:, qs, :], lhsT=Pt[:, t, qs * 128:(qs + 1) * 128],
                                                            rhs=V_all[:, kt_, h, :], start=first, stop=lastmm,
                                                            skip_group_check=True),
                                   [Ptn, "V_all"], [PB[ab]], inc=(t == 1 and qs == 3))

                    scores(0)
                    for k in range(len(steps)):
                        g, h, jp = steps[k]
                        expo(k)
                        if k + 1 < len(steps):
                            scores(k + 1)
                        pv(k)
                        if jp != 8:
                            continue
                        ab, acc = acc_of(g, h)
                        qs_tok = slice(g * 512, (g + 1) * 512)
                        op("dve", lambda e: e.reciprocal(out=rec[:], in_=acc[:, :, 64]), [PB[ab]], ["rec"])
                        TT("dve", o_tok[:, :, h * 64:(h + 1) * 64], acc[:, :, 0:64], rec[:].unsqueeze(2).to_broadcast([128, 4, 64]),
                           ALU.mult, [PB[ab], "rec"], ["o_tok"])
                        if h != 7:
                            continue
                        for half in range(2):
                            bk = 6 + half
                            pT = pb[bk][:].bitcast(BF16).rearrange("p (k t) -> p k t", k=8)
                            for qq in range(2):
                                qs = half * 2 + qq
                                for c4 in range(4):
                                    TR(pT[:, qq * 4 + c4, :], o_tok[:, qs, c4 * 128:(c4 + 1) * 128], ident[:], ["o_tok", "ident"], [PB[bk]],
                                       inc=(qq == 1 and c4 == 3))
                            dst = omT[:, :, qs_tok].rearrange("p c (qs t) -> p qs c t", qs=4)[:, half * 2:half * 2 + 2, :, :]
                            evac_copy(dst, pT.rearrange("p (qq c) t -> p qq c t", qq=2), [PB[bk]], ["omT"])
                    tap("omT", omT[:], [128, 4, T_LAT], BF16, ["omT"])
                    fw.barrier()
            mla_era.close()

            if stop_after not in ("C2", "E"):
                with ExitStack() as pf:
                    A_ = lambda name, shape, dt: fw.sbuf(name, shape, dt, pf)
                    wz = A_("wz", [128, 8, 3072], BF16)
                    wmo = A_("wmo", [128, 4, D], BF16)
                    wdo = A_("wdo", [128, 4, D], BF16)
                    wo = A_("wo", [128, 8, D], BF16)
                    sg = A_("sg", [128, 16, 512], BF16)
                    szm = A_("szm", [128, 512], BF16)
                    om_s = A_("om_s", [128, 4, 512], BF16)
                    od_s = A_("od_s", [128, 4, 512], BF16)
                    mT = A_("mT", [128, 8, 512], BF16)
                    t1S = [A_("t1%d" % i, [128, 512], F32) for i in range(2)]
                    t2S = [A_("t2%d" % i, [128, 512], F32) for i in range(2)]
                    xt = [A_("xt%d" % i, [128, D], F32) for i in range(2)]
                    ot = [A_("ot%d" % i, [128, D], F32) for i in range(1)]
                    dma("pool", wz[:, :, 0:512], winv[:, :, 672:1184], writes=["wz_m"])
                    dma("pool", wz[:, :, 512:1024], winv[:, :, 2720:3232], writes=["wz_d"])
                    for i in range(4):
                        dma("pool", wz[:, :, 1024 + i * 512:1536 + i * 512], winv[:, :, 3264 + i * 512:3776 + i * 512], writes=["wz_g%d" % i])
                    dma("pool", wmo[:], mla_w_o.rearrange("(c p) n -> p c n", p=128), writes=["wmo"])
                    dma("pool", wdo[:], dn_w_o.rearrange("(c p) n -> p c n", p=128), writes=["wdo"])
                    dma("pool", wo[:], w_out.rearrange("(c p) n -> p c n", p=128), writes=["wo"])
                    rr = 0
                    for g in range(4):
                        lt0 = g * 4
                        ltok = slice(g * 512, (g + 1) * 512)
                        htok = slice(256 + g * 512, 256 + (g + 1) * 512)
                        hb = hT_bufs(256 + g * 512, 512)
                        for (which, src_T, srcn, dst_s, dsn, wn) in ((0, omT, "omT", om_s, "om_s", "wz_m"), (1, odT, "odT", od_s, "od_s", "wz_d")):
                            for f in range(4):
                                bk = rr % 4
                                rr += 1
                                for kt in range(8):
                                    MM(pb[bk][:], wz[:, kt, which * 512 + f * 128:which * 512 + (f + 1) * 128], hT[:, kt, htok], hb + [wn],
                                       [PB[bk]], start=(kt == 0), stop=(kt == 7), inc=(kt == 7))
                                ACTF(szm[:], pb[bk][:], AF.Silu, [PB[bk]], ["szm"])
                                TT("dve", dst_s[:, f, :], src_T[:, f, ltok], szm[:], ALU.mult, [srcn, "szm"], [dsn])
                        for c in range(16):
                            bk = rr % 4
                            rr += 1
                            for kt in range(8):
                                MM(pb[bk][:], wz[:, kt, 1024 + c * 128:1024 + (c + 1) * 128], hT[:, kt, htok], hb + ["wz_g%d" % (c // 4)],
                                   [PB[bk]], start=(kt == 0), stop=(kt == 7), inc=(kt == 7))
                            ACTF(sg[:, c, :], pb[bk][:], AF.Sigmoid, [PB[bk]], ["sg"])
                        for f8 in range(8):
                            ba, bb_ = (4, 5) if f8 % 2 == 0 else (6, 7)
                            t1_, t1n = t1S[f8 % 2], "t1%d" % (f8 % 2)
                            t2_, t2n = t2S[f8 % 2], "t2%d" % (f8 % 2)
                            for c4 in range(4):
                                MM(pb[ba][:], wmo[:, c4, f8 * 128:(f8 + 1) * 128], om_s[:, c4, :], ["wmo", "om_s"], [PB[ba]],
                                   start=(c4 == 0), stop=(c4 == 3), inc=(c4 == 3))
                            for c4 in range(4):
                                MM(pb[bb_][:], wdo[:, c4, f8 * 128:(f8 + 1) * 128], od_s[:, c4, :], ["wdo", "od_s"], [PB[bb_]],
                                   start=(c4 == 0), stop=(c4 == 3), inc=(c4 == 3))
                            TT("dve", t1_[:], pb[ba][:], sg[:, f8, :], ALU.mult, [PB[ba], "sg"], [t1n])
                            TT("dve", t2_[:], pb[bb_][:], sg[:, 8 + f8, :], ALU.mult, [PB[bb_], "sg"], [t2n])
                            TT("pool", mT[:, f8, :], t1_[:], t2_[:], ALU.add, [t1n, t2n], ["mT"])
                        for qs in range(4):
                            lt = lt0 + qs
                            xb_, xbn = xt[lt % 2], "xt%d" % (lt % 2)
                            ob_, obn = ot[0], "ot0"
                            dma("sp", xb_[:], x[lt * 128:(lt + 1) * 128, :], writes=[xbn])
                            for nh in range(2):
                                bk = 2 * (qs % 2) + nh
                                for f8 in range(8):
                                    MM(pb[bk][:], mT[:, f8, qs * 128:(qs + 1) * 128], wo[:, f8, nh * 512:(nh + 1) * 512], ["mT", "wo"], [PB[bk]],
                                       start=(f8 == 0), stop=(f8 == 7), inc=(f8 == 7))
                                cs = slice(nh * 512, (nh + 1) * 512)
                                TT("dve", ob_[:, cs], pb[bk][:], gateB[:, cs], ALU.mult, [PB[bk], "gateB"], [obn])
                                TT("pool", xb_[:, cs], ob_[:, cs], xb_[:, cs], ALU.add, [obn, xbn], [xbn])
                            dma("sp", out[lt * 128:(lt + 1) * 128, :], xb_[:], reads=[xbn], writes=["out_dram"], slot="out%d" % (lt % 2))
                    fw.wait_bufs("sp", ["out_dram"])
                    fw.barrier()

        fw.wait_bufs("sp", ["tap_" + n for n in tap_out] + ([] if stop_after else []))
        fw.barrier()
        print("[build] ops=%d waits=%d sems=%d" % (fw.n_ops, fw.n_waits, fw.nsem))
        print("[build] per-engine incs:", {n: e.cnt for n, e in fw.engs.items()})
        stuck = fw.simulate()
        print("[build] deadlock check:", "OK" if not stuck else "STUCK %s" % stuck)
    return nc, tap_out


def _in_maps(inputs):
    cst = _host_consts()
    maps = []
    f = lambda a: np.ascontiguousarray(np.asarray(a, dtype=np.float32))
    for b in range(8):
        m = {
            "x": f(inputs["x"][b]), "ctx": f(inputs["ctx"][b]), "c": f(inputs["c"][b]), "c_ctx": f(inputs["c_ctx"]),
            "w_mod": f(inputs["w_mod"][0]), "b_mod": f(inputs["b_mod"][0]), "norm_w": f(inputs["norm_w"][0]),
            "w_in": f(inputs["w_in"][0]), "dn_conv_w": f(inputs["dn_conv_w"][0]),
            "dn_a_log": f(inputs["dn_a_log"][0]).reshape(16), "dn_dt_bias": f(inputs["dn_dt_bias"][0]).reshape(16),
            "dn_out_norm_w": f(inputs["dn_out_norm_w"][0]),
            "mla_q_norm_w": f(inputs["mla_q_norm_w"][0]), "mla_w_uq": f(inputs["mla_w_uq"][0]),
            "mla_kv_norm_w": f(inputs["mla_kv_norm_w"][0]), "mla_w_ukv": f(inputs["mla_w_ukv"][0]),
            "mla_q_head_norm_w": f(inputs["mla_q_head_norm_w"][0]), "mla_k_head_norm_w": f(inputs["mla_k_head_norm_w"][0]),
            "mla_w_o": f(inputs["mla_w_o"][0]), "dn_w_o": f(inputs["dn_w_o"][0]), "w_out": f(inputs["w_out"][0]),
        }
        m.update(cst)
        maps.append(m)
    return maps


def kernel(**inputs):
    nc, _ = build_program()
    res = run_bass_kernel_spmd(nc, _in_maps(inputs), core_ids=list(range(8)))
    return np.stack([r["out"] for r in res.results], axis=0).astype(np.float32)
```

```python
import math
import os
from contextlib import ExitStack

import numpy as np
import concourse.bass as bass
import concourse.mybir as mybir
from concourse.bass_utils import run_bass_kernel_spmd

F32 = mybir.dt.float32
BF16 = mybir.dt.bfloat16
ALU = mybir.AluOpType
AF = mybir.ActivationFunctionType
AX = mybir.AxisListType

D = 1024
T_LAT = 2048
T_CTX = 256
T_ALL = 2304
NT = 18
IN_DIM = 5312
EPS = 1e-6


class Buf:
    __slots__ = ("name", "lw", "rd")

    def __init__(self, name):
        self.name = name
        self.lw = None
        self.rd = {}


class Eng:
    def __init__(self, name, be, sem, same_raw):
        self.name = name
        self.be = be
        self.sem = sem
        self.cnt = 0
        self.waited = {}
        self.same_raw = same_raw


class FW:
    def __init__(self, nc, stack):
        self.nc = nc
        self.stack = stack
        self.nsem = 0
        self.engs = {}
        for name, be, same_raw in (("pe", nc.tensor, False), ("act", nc.scalar, True),
                                   ("dve", nc.vector, True), ("pool", nc.gpsimd, True),
                                   ("sp", nc.sync, False)):
            self.engs[name] = Eng(name, be, self.new_sem("e_" + name), same_raw)
        self.dma_sems = {}
        self.n_ops = 0
        self.n_waits = 0
        self.bufs = {}
        self.trace = {n: [] for n in self.engs}

    def new_sem(self, name):
        self.nsem += 1
        return self.stack.enter_context(self.nc.semaphore(name))

    def sbuf(self, name, shape, dtype, stack=None):
        self.nalloc = getattr(self, "nalloc", 0) + 1
        return (stack or self.stack).enter_context(self.nc.sbuf_tensor("s%d_%s" % (self.nalloc, name), list(shape), dtype))

    def psum(self, name, shape, dtype=F32):
        return self.stack.enter_context(self.nc.psum_tensor("p_" + name, list(shape), dtype))

    def B(self, name):
        b = self.bufs.get(name)
        if b is None:
            b = self.bufs[name] = Buf(name)
        return b

    def _bl(self, xs):
        return [self.B(x) if isinstance(x, str) else x for x in xs]

    def _deps(self, eng, reads, writes):
        deps = {}

        def add(sem, val):
            k = id(sem)
            if k not in deps or deps[k][1] < val:
                deps[k] = (sem, val)

        for b in reads:
            if b.lw is not None:
                s, v = b.lw
                if s is eng.sem and not eng.same_raw:
                    continue
                add(s, v)
        for b in writes:
            if b.lw is not None:
                s, v = b.lw
                if s is not eng.sem or eng.same_raw:
                    add(s, v)
            for k, (s, v) in b.rd.items():
                if s is not eng.sem or eng.same_raw:
                    add(s, v)
        for k, (s, v) in deps.items():
            if eng.waited.get(k, 0) < v:
                eng.be.wait_ge(s, v)
                eng.waited[k] = v
                self.n_waits += 1
                self.trace[eng.name].append(("w", k, v))

    def _commit(self, sem, val, reads, writes):
        k = id(sem)
        for b in reads:
            if k not in b.rd or b.rd[k][1] < val:
                b.rd[k] = (sem, val)
        for b in writes:
            b.lw = (sem, val)
            b.rd = {}

    def op(self, ename, fn, reads=(), writes=(), inc=True):
        eng = self.engs[ename]
        reads = self._bl(reads)
        writes = self._bl(writes)
        self._deps(eng, reads, writes)
        ins = fn(eng.be)
        if inc:
            eng.cnt += 1
            ins.then_inc(eng.sem, 1)
            self._commit(eng.sem, eng.cnt, reads, writes)
            self.trace[ename].append(("i", id(eng.sem), 1))
        else:
            self._commit(eng.sem, eng.cnt + 1, reads, writes)
            self.trace[ename].append(("n", 0, 0))
        self.n_ops += 1
        return ins

    def dma(self, qname, out, in_, reads=(), writes=(), slot=None, **kw):
        eng = self.engs[qname]
        reads = self._bl(reads)
        writes = self._bl(writes)
        self._deps(eng, reads, writes)
        if slot is None:
            slot = writes[0].name if writes else reads[0].name
        if slot not in self.dma_sems:
            self.dma_sems[slot] = [self.new_sem("d%d" % len(self.dma_sems)), 0]
        ent = self.dma_sems[slot]
        ins = eng.be.dma_start(out=out, in_=in_, **kw)
        ent[1] += 16
        ins.then_inc(ent[0], 16)
        self.trace[qname].append(("i", id(ent[0]), 16))
        self._commit(ent[0], ent[1], reads, writes)
        self.n_ops += 1
        return ins

    def wait_bufs(self, ename, bufs):
        self._deps(self.engs[ename], self._bl(bufs), ())

    def simulate(self):
        sem = {}
        ptr = {n: 0 for n in self.trace}
        progress = True
        while progress:
            progress = False
            for n, tr in self.trace.items():
                while ptr[n] < len(tr):
                    kind, k, v = tr[ptr[n]]
                    if kind == "w":
                        if sem.get(k, 0) >= v:
                            ptr[n] += 1
                            progress = True
                        else:
                            break
                    else:
                        if kind == "i":
                            sem[k] = sem.get(k, 0) + v
                        ptr[n] += 1
                        progress = True
        stuck = {n: (ptr[n], len(tr)) for n, tr in self.trace.items() if ptr[n] < len(tr)}
        return stuck

    def barrier(self):
        targets = [(e.sem, e.cnt) for e in self.engs.values() if e.cnt > 0]
        targets += [(s, v) for (s, v) in self.dma_sems.values() if v > 0]
        for e in self.engs.values():
            for s, v in targets:
                if s is e.sem:
                    continue
                if e.waited.get(id(s), 0) < v:
                    e.be.wait_ge(s, v)
                    e.waited[id(s)] = v
                    self.n_waits += 1
                    self.trace[e.name].append(("w", id(s), v))


def _host_consts():
    c = {}
    c["ident"] = np.eye(128, dtype=np.float32)
    m = np.arange(128)[:, None]
    i = np.arange(128)[None, :]
    c["tri"] = np.stack([(m <= i), (m < i), (m >= i), (m > i), np.ones((128, 128), bool),
                         (m // 64 == i // 64)]).astype(np.float32)
    c["negoff"] = -(m != i).astype(np.float32)
    hm = [(m // 8 == i // 8)]
    for s_ in (8, 16, 32, 64):
        hm.append((m // (2 * s_) == i // (2 * s_)) & (m % (2 * s_) >= s_) & (i % (2 * s_) < s_))
    for s_ in (8, 16, 32, 64):
        hm.append((m // (2 * s_) == i // (2 * s_)) & (i % (2 * s_) >= s_) & (m % (2 * s_) < s_))
    c["hmask"] = np.stack(hm).astype(np.float32)
    t = np.arange(T_LAT)
    row = (t // 64).astype(np.float32)
    col = (t % 64).astype(np.float32)
    inv = (10000.0 ** (-np.arange(0, 16, 2, dtype=np.float32) / 16)).astype(np.float32)
    ang = np.concatenate([row[:, None] * inv, col[:, None] * inv], axis=-1).astype(np.float32)
    c["rope_cos"] = np.cos(ang).astype(np.float32)
    c["rope_sin"] = np.sin(ang).astype(np.float32)
    return c


def build_program(taps=(), stop_after=None):
    nc = bass.Bass("TRN2", target_bir_lowering=False)

    def din(name, shape):
        return nc.dram_tensor(name, list(shape), F32, kind="ExternalInput").ap()

    x = din("x", [T_LAT, D])
    ctx = din("ctx", [T_CTX, D])
    cvec = din("c", [D])
    cctx = din("c_ctx", [D])
    w_mod = din("w_mod", [D, 3 * D])
    b_mod = din("b_mod", [3 * D])
    norm_w = din("norm_w", [D])
    ident_d = din("ident", [128, 128])
    tri_d = din("tri", [6, 128, 128])
    negoff_d = din("negoff", [128, 128])
    hmask_d = din("hmask", [9, 128, 128])
    dn_w = din("dn_out_norm_w", [64])
    cos_d = din("rope_cos", [T_LAT, 16])
    sin_d = din("rope_sin", [T_LAT, 16])
    q_norm_w = din("mla_q_norm_w", [384])
    w_uq = din("mla_w_uq", [384, 768])
    kv_norm_w = din("mla_kv_norm_w", [256])
    w_ukv = din("mla_w_ukv", [256, 1024])
    qh_w = din("mla_q_head_norm_w", [96])
    kh_w = din("mla_k_head_norm_w", [96])
    mla_w_o = din("mla_w_o", [512, D])
    dn_w_o = din("dn_w_o", [512, D])
    w_out = din("w_out", [D, D])
    w_in = din("w_in", [D, IN_DIM])
    conv_w = din("dn_conv_w", [5, 1536])
    a_log = din("dn_a_log", [16])
    dt_bias = din("dn_dt_bias", [16])
    out = nc.dram_tensor("out", [T_LAT, D], F32, kind="ExternalOutput").ap()
    tap_out = {}

    with ExitStack() as st:
        fw = FW(nc, st)
        op, dma = fw.op, fw.dma

        def tap(name, sb_ap, shape, dtype, reads):
            if name not in taps:
                return
            t = nc.dram_tensor("tap_" + name, list(shape), dtype, kind="ExternalOutput").ap()
            tap_out[name] = t
            dma("sp", t, sb_ap, reads=reads, writes=["tap_" + name])

        ident = fw.sbuf("ident", [128, 128], BF16)
        identf = fw.sbuf("identf", [128, 128], F32)
        gateB = fw.sbuf("gateB", [128, D], F32)
        odT = fw.sbuf("odT", [128, 4, T_LAT], BF16)
        modT = fw.sbuf("modT", [128, 3, 8, 2], F32)
        gam = fw.sbuf("gam", [128, 8, 2], F32)
        shf = fw.sbuf("shf", [128, 8, 2], F32)
        psall = fw.psum("psall", [128, 8, 512], F32)
        pb = [psall[:, i, :] for i in range(8)]
        PB = [fw.B("pb%d" % i) for i in range(8)]

        dma("pool", ident[:], ident_d, writes=["ident"])
        dma("sp", identf[:], ident_d, writes=["identf"])

        with ExitStack() as pa:
            cf = fw.sbuf("cf", [128, 2, 8], F32, pa)
            csil = fw.sbuf("csil", [128, 2, 8], BF16, pa)
            csilB = fw.sbuf("csilB", [128, 8, 128], BF16, pa)
            bmodT = fw.sbuf("bmodT", [128, 3, 8], F32, pa)
            nwT = fw.sbuf("nwT", [128, 8], F32, pa)
            bgB = fw.sbuf("bgB", [128, D], F32, pa)
            wm = [fw.sbuf("wm%d" % i, [128, 8, 512], BF16, pa) for i in range(2)]
            with nc.allow_non_contiguous_dma(reason="small vector column layouts"):
                dma("sp", cf[:, 0, :], cvec.rearrange("(kt p) -> p kt", p=128), writes=["cf"])
                dma("sp", cf[:, 1, :], cctx.rearrange("(kt p) -> p kt", p=128), writes=["cf"], slot="cf2")
                for s in range(3):
                    dma("sp", bmodT[:, s, :], b_mod[s * D:(s + 1) * D].rearrange("(kt p) -> p kt", p=128),
                        writes=["bmodT"], slot="bmodT%d" % s)
                dma("sp", nwT[:], norm_w.rearrange("(kt p) -> p kt", p=128), writes=["nwT"])
            dma("sp", bgB[:], b_mod[2 * D:3 * D].partition_broadcast(128), writes=["bgB"])
            op("act", lambda e: e.activation(out=csil[:], in_=cf[:], func=AF.Silu), reads=["cf"], writes=["csil"])
            op("dve", lambda e: e.tensor_copy(out=csilB[:], in_=csil[:, 0, :].unsqueeze(2).to_broadcast([128, 8, 128])),
               reads=["csil"], writes=["csilB"])
            wmv = w_mod.rearrange("(kt p) n -> p kt n", p=128)
            mod_ps = pb[0][:, 0:48].rearrange("p (s k v) -> p s k v", s=3, k=8)
            for j in range(6):
                sec, half = j // 2, j % 2
                wmj = wm[j % 2]
                wname = "wm%d" % (j % 2)
                dma("pool", wmj[:], wmv[:, :, j * 512:(j + 1) * 512], writes=[wname])
                for q in range(4):
                    kto = half * 4 + q
                    for kti in range(8):
                        op("pe", (lambda q=q, kti=kti, kto=kto, sec=sec, wmj=wmj: lambda e: e.matmul(
                            mod_ps[:, sec, kto, :], lhsT=wmj[:, kti, q * 128:(q + 1) * 128], rhs=csil[:, :, kti],
                            start=(kti == 0), stop=(kti == 7)))(),
                           reads=[wname, "csil"], writes=[PB[0]], inc=(kti == 7))
                if sec == 2:
                    for kti in range(8):
                        op("pe", (lambda kti=kti, half=half, wmj=wmj: lambda e: e.matmul(
                            pb[1 + half][:], lhsT=csilB[:, kti, :], rhs=wmj[:, kti, :],
                            start=(kti == 0), stop=(kti == 7)))(),
                           reads=[wname, "csilB"], writes=[PB[1 + half]], inc=(kti == 7))
            op("dve", lambda e: e.tensor_tensor(out=modT[:], in0=mod_ps,
                                                in1=bmodT[:].unsqueeze(3).to_broadcast([128, 3, 8, 2]), op=ALU.add),
               reads=[PB[0], "bmodT"], writes=["modT"])
            op("dve", lambda e: e.tensor_scalar(out=gam[:], in0=modT[:, 1, :, :], scalar1=1.0, scalar2=None, op0=ALU.add),
               reads=["modT"], writes=["gam"])
            op("dve", lambda e: e.tensor_tensor(out=gam[:], in0=gam[:], in1=nwT[:].unsqueeze(2).to_broadcast([128, 8, 2]),
                                                op=ALU.mult), reads=["gam", "nwT"], writes=["gam"])
            op("dve", lambda e: e.tensor_copy(out=shf[:], in_=modT[:, 0, :, :]), reads=["modT"], writes=["shf"])
            for half in range(2):
                op("dve", (lambda half=half: lambda e: e.tensor_tensor(
                    out=gateB[:, half * 512:(half + 1) * 512], in0=pb[1 + half][:], in1=bgB[:, half * 512:(half + 1) * 512],
                    op=ALU.add))(), reads=[PB[1 + half], "bgB"], writes=["gateB"])
            tap("modT", modT[:], [128, 3, 8, 2], F32, ["modT"])
            tap("gateB", gateB[:], [128, D], F32, ["gateB"])
            fw.barrier()

        def phase_B(hT):
            with ExitStack() as pbk:
                xs = [fw.sbuf("xs%d" % i, [128, D], F32, pbk) for i in range(3)]
                junk = fw.sbuf("junkB", [128, D], BF16, pbk)
                xn = [fw.sbuf("xn%d" % i, [128, D], BF16, pbk) for i in range(2)]
                ssq = fw.sbuf("ssq", [128, NT], F32, pbk)
                def stats(tt):
                    xt, xtn = xs[tt % 3], "xs%d" % (tt % 3)
                    src = ctx[tt * 128:(tt + 1) * 128, :] if tt < 2 else x[(tt - 2) * 128:(tt - 1) * 128, :]
                    dma("sp", xt[:], src, writes=[xtn])
                    sc = ssq[:, tt:tt + 1]
                    op("act", lambda e: e.activation(out=junk[:], in_=xt[:], func=AF.Square, accum_out=sc), [xtn], ["junkB", "ssq%d" % tt])
                    op("dve", lambda e: e.tensor_scalar(out=sc, in0=sc, scalar1=1.0 / D, scalar2=EPS, op0=ALU.mult, op1=ALU.add),
                       ["ssq%d" % tt], ["ssq%d" % tt])
                    op("act", lambda e: e.activation(out=sc, in_=sc, func=AF.Ln), ["ssq%d" % tt], ["ssq%d" % tt])
                    op("act", lambda e: e.activation(out=sc, in_=sc, func=AF.Exp, scale=-0.5), ["ssq%d" % tt], ["ssq%d" % tt])

                stats(0)
                for tt in range(NT):
                    v = 1 if tt < 2 else 0
                    xt, xtn = xs[tt % 3], "xs%d" % (tt % 3)
                    sc = ssq[:, tt:tt + 1]
                    if tt + 1 < NT:
                        stats(tt + 1)
                    xb, xbn = xn[tt % 2], "xn%d" % (tt % 2)
                    bank = 2 + (tt % 2)
                    pT = pb[bank][:].bitcast(BF16).rearrange("p (k t) -> p k t", k=8)
                    op("dve", lambda e: e.tensor_scalar(out=xb[:], in0=xt[:], scalar1=sc, scalar2=None, op0=ALU.mult),
                       [xtn, "ssq%d" % tt], [xbn])
                    for kt in range(8):
                        op("pe", lambda e: e.transpose(out=pT[:, kt, :], in_=xb[:, kt * 128:(kt + 1) * 128], identity=ident[:]),
                           [xbn, "ident"], [PB[bank]], inc=(kt == 7))
                    for kt in range(8):
                        dst = hT[:, kt, tt * 128:(tt + 1) * 128]
                        if kt % 2 == 0:
                            op("act", lambda e: e.activation(out=dst, in_=pT[:, kt, :], func=AF.Identity, scale=gam[:, kt, v:v + 1],
                                                             bias=shf[:, kt, v:v + 1]), [PB[bank], "gam", "shf"], ["hT%d" % tt])
                        else:
                            op("dve", lambda e: e.tensor_scalar(out=dst, in0=pT[:, kt, :], scalar1=gam[:, kt, v:v + 1],
                                                                scalar2=shf[:, kt, v:v + 1], op0=ALU.mult, op1=ALU.add),
                               [PB[bank], "gam", "shf"], ["hT%d" % tt])
                fw.barrier()

        def MM(out_, lhsT, rhs, reads, writes, start=True, stop=True, inc=True):
            return op("pe", lambda e: e.matmul(out_, lhsT=lhsT, rhs=rhs, start=start, stop=stop), reads, writes, inc)

        def TR(out_, in_, idn, reads, writes, inc=True):
            return op("pe", lambda e: e.transpose(out=out_, in_=in_, identity=idn), reads, writes, inc)

        def ACTF(out_, in_, func, reads, writes, **kw):
            return op("act", lambda e: e.activation(out=out_, in_=in_, func=func, **kw), reads, writes)

        def TT(en, out_, in0, in1, alu, reads, writes):
            return op(en, lambda e: e.tensor_tensor(out=out_, in0=in0, in1=in1, op=alu), reads, writes)

        def TS(en, out_, in0, s1, s2, op0, op1, reads, writes):
            if s2 is None:
                return op(en, lambda e: e.tensor_scalar(out=out_, in0=in0, scalar1=s1, scalar2=None, op0=op0), reads, writes)
            return op(en, lambda e: e.tensor_scalar(out=out_, in0=in0, scalar1=s1, scalar2=s2, op0=op0, op1=op1), reads, writes)

        def STT(en, out_, in0, scalar, in1, op0, op1, reads, writes):
            return op(en, lambda e: e.scalar_tensor_tensor(out=out_, in0=in0, scalar=scalar, in1=in1, op0=op0, op1=op1),
                      reads, writes)

        def CP(en, out_, in_, reads, writes):
            if en == "act":
                return op("act", lambda e: e.activation(out=out_, in_=in_, func=AF.Copy), reads, writes)
            return op(en, lambda e: e.tensor_copy(out=out_, in_=in_), reads, writes)

        def hT_bufs(t0, n):
            return ["hT%d" % t for t in range(t0 // 128, (t0 + n + 127) // 128)]

        winv = w_in.rearrange("(kt p) n -> p kt n", p=128)
        GROUPS = [(0, 256)] + [(256 + 512 * i, 512) for i in range(4)]
        evac_rr = [0]

        EV_N = int(os.environ.get("EV_N", "3"))
        EV_A = int(os.environ.get("EV_A", "2"))

        def evac_copy(out_, in_, reads, writes):
            evac_rr[0] += 1
            return CP("act" if (evac_rr[0] % EV_N) < EV_A else "dve", out_, in_, reads, writes)


        dn_era = ExitStack()
        dnqT = fw.sbuf("dnqT", [128, 4, T_ALL], BF16, dn_era)
        dnkT = fw.sbuf("dnkT", [128, 4, T_ALL], BF16, dn_era)
        k_tok = fw.sbuf("k_tok", [128, NT, 8, 64], BF16, dn_era)
        v_tok = fw.sbuf("v_tok", [128, NT, 8, 64], BF16, dn_era)
        beta = fw.sbuf("beta", [128, NT, 16], F32, dn_era)
        gdec = fw.sbuf("gdec", [128, NT, 16], F32, dn_era)
        tri = fw.sbuf("tri", [128, 6, 128], F32, dn_era)
        onesbd = fw.sbuf("onesbd", [128, 128], BF16, dn_era)
        negoff = fw.sbuf("negoff", [128, 128], F32, dn_era)
        hm = fw.sbuf("hm", [128, 9, 128], BF16, dn_era)
        dma("sp", tri[:], tri_d.rearrange("k p i -> p k i"), writes=["tri"])
        dma("pool", onesbd[:], tri_d[5], writes=["onesbd"])
        dma("sp", negoff[:], negoff_d, writes=["negoff"])
        dma("pool", hm[:], hmask_d.rearrange("k p i -> p k i"), writes=["hm"])
        h_era = ExitStack()
        hT = fw.sbuf("hT", [128, 8, T_ALL], BF16, h_era)
        phase_B(hT)
        tap("hT", hT[:], [128, 8, T_ALL], BF16, ["hT%d" % t for t in range(NT)])
        hT_spill = nc.dram_tensor("hT_spill", [128, 8, T_ALL], BF16).ap()
        for kt in range(8):
            dma("sp", hT_spill[:, kt, :], hT[:, kt, :], reads=["hT%d" % t for t in range(NT)], writes=["hT_spill"], slot="spill%d" % kt)

        with ExitStack() as c1:
            wdnb = [fw.sbuf("wdn%d" % i, [128, 8, 512], BF16, c1) for i in range(2)]
            wbg = fw.sbuf("wbg", [128, 8, 32], BF16, c1)
            convT = fw.sbuf("convT", [128, 12, 5], F32, c1)
            diagwb = [fw.sbuf("diagw%d" % i, [128, 5, 128], BF16, c1) for i in range(2)]
            xc = [fw.sbuf("xc%d" % i, [128, 2312], BF16, c1) for i in range(2)]
            ys = [fw.sbuf("ys%d" % i, [128, T_ALL], F32, c1) for i in range(2)]
            sqS = [fw.sbuf("sq%d" % i, [128, T_ALL], BF16, c1) for i in range(2)]
            rsS = [fw.sbuf("rs%d" % i, [128, 512], F32, c1) for i in range(2)]
            zz = fw.sbuf("zz", [128, NT, 16], F32, c1)
            zm_ = fw.sbuf("zm_", [128, NT, 16], F32, c1)
            ze = fw.sbuf("ze", [128, NT, 16], F32, c1)
            dtbB = fw.sbuf("dtbB", [128, 16], F32, c1)
            negA = fw.sbuf("negA", [128, 16], F32, c1)
            wbg_n = fw.sbuf("wbg_n", [128, 8, 32], BF16, c1)
            dma("pool", wbg_n[:], winv[:, :, 3232:3264], writes=["wbg_n"])
            CP("dve", wbg[:].rearrange("p k (a par hp) -> p (k a) par hp", par=2, hp=4),
               wbg_n[:].rearrange("p k (a hp par) -> p (k a) par hp", par=2, hp=4), ["wbg_n"], ["wbg"])
            with nc.allow_non_contiguous_dma(reason="small conv weight column layout"):
                for j in range(5):
                    dma("sp", convT[:, :, j], conv_w[j, :].rearrange("(c p) -> p c", p=128), writes=["convT"], slot="convT%d" % j)
            dtb_n = fw.sbuf("dtb_n", [128, 16], F32, c1)
            alog_n = fw.sbuf("alog_n", [128, 16], F32, c1)
            dma("sp", dtb_n[:], dt_bias.partition_broadcast(128), writes=["dtb_n"])
            dma("sp", alog_n[:], a_log.partition_broadcast(128), writes=["alog_n"])
            CP("dve", dtbB[:].rearrange("p (a par hp) -> p a par hp", par=2, hp=4),
               dtb_n[:].rearrange("p (a hp par) -> p a par hp", par=2, hp=4), ["dtb_n"], ["dtbB"])
            CP("dve", negA[:].rearrange("p (a par hp) -> p a par hp", par=2, hp=4),
               alog_n[:].rearrange("p (a hp par) -> p a par hp", par=2, hp=4), ["alog_n"], ["negA"])
            for i in range(2):
                for lo, hi in ((0, 2), (258, 262), (2310, 2312)):
                    op("pool", lambda e: e.memset(xc[i][:, lo:hi], 0.0), (), ["xc%d" % i])
            zps = psall[:, 4:6, 0:288].rearrange("p b (t c) -> p b t c", c=32)
            for tt in range(NT):
                for kt in range(8):
                    MM(zps[:, tt // 9, tt % 9, :], hT[:, kt, tt * 128:(tt + 1) * 128], wbg[:, kt, :], ["hT%d" % tt, "wbg"],
                       [PB[4 + tt // 9]], start=(kt == 0), stop=(kt == 7), inc=(kt == 7))
            v4 = lambda t: t[:].rearrange("p (b t) c -> p b t c", b=2)
            ACTF(v4(beta), zps[:, :, :, 0:16], AF.Sigmoid, [PB[4], PB[5]], ["beta"])
            TT("dve", v4(zz), zps[:, :, :, 16:32], dtbB[:].unsqueeze(1).unsqueeze(1).to_broadcast([128, 2, 9, 16]), ALU.add,
               [PB[4], PB[5], "dtbB"], ["zz"])
            TS("dve", zm_[:], zz[:], 0.0, None, ALU.max, None, ["zz"], ["zm_"])
            STT("dve", ze[:], zm_[:], -2.0, zz[:], ALU.mult, ALU.add, ["zm_", "zz"], ["ze"])
            ACTF(ze[:], ze[:], AF.Exp, ["ze"], ["ze"])
            ACTF(ze[:], ze[:], AF.Ln, ["ze"], ["ze"], bias=1.0)
            ACTF(negA[:], negA[:], AF.Exp, ["negA"], ["negA"])
            TS("dve", negA[:], negA[:], -1.0, None, ALU.mult, None, ["negA"], ["negA"])
            TT("dve", ze[:], ze[:], zm_[:], ALU.add, ["ze", "zm_"], ["ze"])
            TT("dve", gdec[:], ze[:], negA[:].unsqueeze(1).to_broadcast([128, NT, 16]), ALU.mult, ["ze", "negA"], ["gdec"])
            def chunk_stream(c):
                which, hc = c // 4, c % 4
                par = c % 2
                xcb, xcn = xc[par], "xc%d" % par
                ysc, ysn = ys[par], "ys%d" % par
                sqc, sqn = sqS[par], "sq%d" % par
                wdn, wn = wdnb[which % 2], "wdn%d" % (which % 2)
                if hc == 0:
                    dma("pool", wdn[:], winv[:, :, 1184 + which * 512:1184 + (which + 1) * 512], writes=[wn])
                diagw, dgn = diagwb[par], "diagw%d" % par
                for j in range(5):
                    TS("dve" if j % 2 else "pool", diagw[:, j, :], identf[:], convT[:, c, j:j + 1], None, ALU.mult, None,
                       ["identf", "convT"], [dgn])
                for gi, (t0, n) in enumerate(GROUPS):
                    bk = gi % 2
                    col = t0 + 2 if t0 < 256 else t0 + 6
                    for kt in range(8):
                        MM(pb[bk][:, 0:n], wdn[:, kt, hc * 128:(hc + 1) * 128], hT[:, kt, t0:t0 + n], [wn] + hT_bufs(t0, n), [PB[bk]],
                           start=(kt == 0), stop=(kt == 7), inc=(kt == 7))
                    evac_copy(xcb[:, col:col + n], pb[bk][:, 0:n], [PB[bk]], [xcn])
                    if gi in (1, 3):
                        yield
                yield
                for gi, (t0, n) in enumerate(GROUPS):
                    bk = 2 + gi % 2
                    col = t0 + 2 if t0 < 256 else t0 + 6
                    for j in range(5):
                        MM(pb[bk][:, 0:n], diagw[:, j, :], xcb[:, col + j - 2:col + j - 2 + n], [dgn, xcn], [PB[bk]],
                           start=(j == 0), stop=(j == 4), inc=(j == 4))
                    if which == 2:
                        ACTF(sqc[:, t0:t0 + n], pb[bk][:, 0:n], AF.Silu, [PB[bk]], [sqn])
                    else:
                        ACTF(ysc[:, t0:t0 + n], pb[bk][:, 0:n], AF.Silu, [PB[bk]], [ysn])
                yield
                if which < 2:
                    TT("pool", sqc[:], ysc[:], ysc[:], ALU.mult, [ysn], [sqn])
                    yield
                    dst = (dnqT if which == 0 else dnkT)
                    dn_ = ("dnqT%d" if which == 0 else "dnkT%d") % hc
                    for gi, (t0, n) in enumerate(GROUPS):
                        bk = 4 + gi % 2
                        rsg, rsn = rsS[gi % 2], "rs%d" % (gi % 2)
                        MM(pb[bk][:, 0:n], onesbd[:], sqc[:, t0:t0 + n], ["onesbd", sqn], [PB[bk]])
                        ACTF(rsg[:, 0:n], pb[bk][:, 0:n], AF.Ln, [PB[bk]], [rsn], bias=EPS)
                        ACTF(rsg[:, 0:n], rsg[:, 0:n], AF.Exp, [rsn], [rsn], scale=-0.5)
                        if which == 0:
                            STT("dve", dst[:, hc, t0:t0 + n], ysc[:, t0:t0 + n], 0.125, rsg[:, 0:n], ALU.mult, ALU.mult, [ysn, rsn], [dn_])
                        else:
                            TT("dve", dst[:, hc, t0:t0 + n], ysc[:, t0:t0 + n], rsg[:, 0:n], ALU.mult, [ysn, rsn], [dn_])
                    yield
                if which >= 1:
                    src = dnkT[:, hc, :] if which == 1 else sqc[:]
                    srcn = ("dnkT%d" % hc) if which == 1 else sqn
                    dtok = k_tok if which == 1 else v_tok
                    dtn = "k_tok" if which == 1 else "v_tok"
                    for bi, (tt0, ntl) in enumerate(((0, 8), (8, 8), (16, 2))):
                        bk = 6 + bi % 2
                        pT = pb[bk][:].bitcast(BF16).rearrange("p (k t) -> p k t", k=8)
                        for i in range(ntl):
                            tt = tt0 + i
                            TR(pT[:, i, :], src[:, tt * 128:(tt + 1) * 128], ident[:], [srcn, "ident"], [PB[bk]], inc=(i == ntl - 1))
                        evac_copy(dtok[:, tt0:tt0 + ntl, 2 * hc:2 * hc + 2, :].rearrange("p t h d -> p t (h d)"), pT[:, 0:ntl, :],
                                  [PB[bk]], [dtn])
                    yield

            NCH = int(os.environ.get("C1_NCH", "12"))
            active = []
            nxt_c = 0
            while nxt_c < NCH or active:
                if nxt_c < NCH and len(active) < 2:
                    active.append(chunk_stream(nxt_c))
                    nxt_c += 1
                for g in list(active):
                    try:
                        next(g)
                    except StopIteration:
                        active.remove(g)
            tap("dnqT", dnqT[:], [128, 4, T_ALL], BF16, ["dnqT%d" % i for i in range(4)])
            tap("dnkT", dnkT[:], [128, 4, T_ALL], BF16, ["dnkT%d" % i for i in range(4)])
            tap("k_tok", k_tok[:], [128, NT, 8, 64], BF16, ["k_tok"])
            tap("v_tok", v_tok[:], [128, NT, 8, 64], BF16, ["v_tok"])
            tap("beta", beta[:], [128, NT, 16], F32, ["beta"])
            tap("gdec", gdec[:], [128, NT, 16], F32, ["gdec"])
            fw.barrier()
        h_era.close()
        if stop_after == "C1":
            dn_era.close()

        if stop_after != "C1":
            with ExitStack() as dd:
                A_ = lambda name, shape, dt: fw.sbuf(name, shape, dt, dd)
                rg = A_("rgEs", [128, 8, 128], F32)
                Es = rg
                DT = A_("DT", [128, 8, 128], F32)
                E_ = A_("E_", [128, 8, 128], F32)
                AqkS = [A_("Aqk%d" % i, [128, 8, 128], BF16) for i in range(3)]
                R0S = [A_("R0b%d" % i, [128, 8, 128], BF16) for i in range(2)]
                P0S = [A_("P0b%d" % i, [128, 8, 128], BF16) for i in range(2)]
                BA = [[A_("BA%d%d" % (w, i), [128, 4, 2, 128], BF16) for i in range(2)] for w in range(2)]
                BB = [[A_("BB%d%d" % (w, i), [128, 4, 2, 128], BF16) for i in range(2)] for w in range(2)]
                Wb = [[A_("Wb%d%d" % (w, i), [128, 4, 128], BF16) for i in range(2)] for w in range(2)]
                Db = [[A_("Db%d%d" % (w, i), [128, 4, 128], BF16) for i in range(2)] for w in range(2)]
                Yb = [A_("Yb%d" % w, [128, 4, 128], BF16) for w in range(2)]
                Ypb = [A_("Ypb%d" % w, [128, 4, 128], BF16) for w in range(2)]
                Cmb = [A_("Cmb%d" % i, [128, 8, 128], BF16) for i in range(2)]
                Cfb = [A_("Cfb%d" % i, [128, 8, 128], BF16) for i in range(2)]
                XTS = [A_("XT%d" % i, [128, 8, 128], BF16) for i in range(2)]
                kg = A_("kg", [128, 8, 64], BF16)
                kdb = A_("kdb", [128, 8, 64], BF16)
                up = A_("up", [128, 8, 64], F32)
                wT = A_("wT", [128, 4, 128], BF16)
                vt = A_("vt", [128, 8, 64], BF16)
                S32 = A_("S32", [128, 4, 64], F32)
                Sbf = A_("Sbf", [128, 4, 64], BF16)
                egS = [A_("eg%d" % i, [128, 16], F32) for i in range(3)]
                egl2S = [A_("egl2%d" % i, [128, 4], F32) for i in range(3)]
                eb = A_("eb", [128, 8], F32)
                otmp = A_("otmp", [128, 8, 64], F32)
                ofin = A_("ofin", [128, 8, 64], F32)
                onb = A_("onb", [128, 8, 64], BF16)
                oss = A_("oss", [128, 8], F32)
                dnwB = A_("dnwB", [128, 64], F32)
                o_acc = A_("o_acc", [128, 16, 8, 64], BF16)
                dma("sp", dnwB[:], dn_w.partition_broadcast(128), writes=["dnwB"])
                LE, LT_, GE, GT_, ONES = (tri[:, i, :] for i in range(5))

                def bc_h(m):
                    return m.unsqueeze(1).to_broadcast([128, 8, 128])

                def bc_i(v, n):
                    return v.unsqueeze(2).to_broadcast([128, 8, n])

                ps2 = lambda b0: psall[:, b0:b0 + 2, :].rearrange("p b (i c) -> p (b i) c", c=128)
                v4q = lambda t: t.rearrange("p (par hp) e -> p par hp e", par=2)
                bc4 = lambda v, n: v.rearrange("p (par hp) -> p par hp", par=2).unsqueeze(3).to_broadcast([128, 2, 4, n])
                bcm = lambda m_: m_.unsqueeze(1).to_broadcast([128, 4, 128])
                A4 = lambda bk: pb[bk].rearrange("p (i c) -> p i c", c=128)
                A8 = lambda bk: psall[:, bk:bk + 2, :].rearrange("p b (i c) -> p (b i) c", c=256)

                DN_NT = int(os.environ.get("DN_NT", "18"))

                def stream_P(d, n, tt):
                    (Mc, Mrest, Mr, Ml, Mi) = (LE, GT_, LE, GT_, LE) if d == 0 else (GE, LT_, GE, LT_, GE)
                    lat = tt >= 2
                    tok = slice(tt * 128, (tt + 1) * 128)
                    g_t = gdec[:, tt, d * 8:(d + 1) * 8]
                    b_t = beta[:, tt, d * 8:(d + 1) * 8]
                    eg, egn = egS[n % 3], "eg%d" % (n % 3)
                    egl2, egl2n = egl2S[n % 3], "egl2%d" % (n % 3)
                    Aqk, Aqkn = AqkS[n % 3], "Aqk%d" % (n % 3)
                    R0b, R0n = R0S[n % 2], "R0b%d" % (n % 2)
                    P0b, P0n = P0S[n % 2], "P0b%d" % (n % 2)
                    small = pb[7]
                    MM(small[:, 0:8], Mc, g_t, ["tri", "gdec"], [PB[7]], inc=False)
                    MM(small[:, 8:16], Mrest, g_t, ["tri", "gdec"], [PB[7]], inc=False)
                    MM(small[:, 16:24], ONES, g_t, ["tri", "gdec"], [PB[7]])
                    ACTF(eg[:], small[:, 0:16], AF.Exp, [PB[7]], [egn])
                    ACTF(egl2[0:64, :], small[0:64, 16:20], AF.Exp, [PB[7]], [egl2n])
                    ACTF(egl2[64:128, :], small[64:128, 20:24], AF.Exp, [PB[7]], [egl2n])
                    TT("pool", rg[:], bc_h(Mr), bc_i(g_t, 128), ALU.mult, ["tri", "gdec"], ["rgEs"])
                    yield
                    for hh in range(2):
                        MM(pb[4 + hh][:], Ml, rg[:, hh * 4:(hh + 1) * 4, :].rearrange("p h i -> p (h i)"), ["tri", "rgEs"], [PB[4 + hh]])
                    ACTF(DT[:], ps2(4), AF.Exp, [PB[4], PB[5]], ["DT"])
                    TT("dve", DT[:], DT[:], bc_h(Mi), ALU.mult, ["DT", "tri"], ["DT"])
                    yield
                    TT("dve", E_[:], DT[:], bc_i(b_t, 128), ALU.mult, ["DT", "beta"], ["E_"])
                    TT("pool", Es[:], E_[:], bc_h(negoff[:]), ALU.mult, ["E_", "negoff"], ["rgEs"])
                    yield
                    for h in range(8):
                        pr, hp = (h % 2) * 64, h // 2
                        kT_h = dnkT[pr:pr + 64, hp, tok]
                        MM(psall[:, 4 + h % 2, hp * 128:(hp + 1) * 128], kT_h, kT_h, ["dnkT%d" % hp], [PB[4 + h % 2]], inc=(h >= 6))
                    if lat:
                        for h in range(8):
                            pr, hp = (h % 2) * 64, h // 2
                            MM(psall[:, 6 + h % 2, hp * 128:(hp + 1) * 128], dnkT[pr:pr + 64, hp, tok],
                               dnqT[pr:pr + 64, hp, tok], ["dnkT%d" % hp, "dnqT%d" % hp], [PB[6 + h % 2]], inc=(h >= 6))
                    TT("dve", R0b[:], ps2(4), Es[:], ALU.mult, [PB[4], PB[5], "rgEs"], [R0n])
                    if lat:
                        TT("dve", Aqk[:], ps2(6), E_[:], ALU.mult, [PB[6], PB[7], "E_"], [Aqkn])
                    yield
                    pT = pb[6][:].bitcast(BF16).rearrange("p (k t) -> p k t", k=8)
                    for q in range(8):
                        TR(pT[:, q, :], R0b[:, q, :], ident[:], [R0n, "ident"], [PB[6]], inc=(q == 7))
                    CP("act", P0b[:], pT, [PB[6]], [P0n])
                    yield

                def stream_I(d, n, tt):
                    R0b, R0n = R0S[n % 2], "R0b%d" % (n % 2)
                    P0b, P0n = P0S[n % 2], "P0b%d" % (n % 2)
                    XT, XTn = XTS[n % 2], "XT%d" % (n % 2)
                    for w in range(2):
                        sl = slice(w * 4, (w + 1) * 4)
                        TT(os.environ.get("BA_ENG", "pool"), BA[w][0][:, :, 0, :], R0b[:, sl, :], bcm(hm[:, 0, :]), ALU.mult, [R0n, "hm"], ["BA%d0" % w])
                        TT(os.environ.get("BB_ENG", "dve"), BB[w][0][:, :, 0, :], P0b[:, sl, :], bcm(hm[:, 0, :]), ALU.mult, [P0n, "hm"], ["BB%d0" % w])
                        TT(os.environ.get("BA_ENG", "pool"), BA[w][1][:, :, 1, :], BA[w][0][:, :, 0, :], bcm(ident[:]), ALU.add, ["BA%d0" % w, "ident"], ["BA%d1" % w])
                    yield
                    CmAll = Cmb + Cfb
                    for li in range(4):
                        mnat = hm[:, (1 + li) if d == 0 else (5 + li), :]
                        TT("pool", CmAll[li][:], P0b[:], bc_h(mnat), ALU.mult, [P0n, "hm"], ["CmAll%d" % li])
                    for w in range(2):
                        b0 = 2 * w
                        for i in range(4):
                            MM(A4(b0)[:, i, :], BA[w][0][:, i, 0, :], BB[w][0][:, i, 0, :], ["BA%d0" % w, "BB%d0" % w], [PB[b0]], inc=False)
                            MM(A4(b0 + 1)[:, i, :], BB[w][0][:, i, 0, :], BA[w][0][:, i, 0, :], ["BA%d0" % w, "BB%d0" % w], [PB[b0 + 1]],
                               inc=(i == 3))
                        evac_copy(BB[w][1][:, :, 0, :], A4(b0), [PB[b0]], ["BB%d1" % w])
                        evac_copy(BA[w][1][:, :, 0, :], A4(b0 + 1), [PB[b0 + 1]], ["BA%d1" % w])
                    yield
                    for w in range(2):
                        b0 = 2 * w
                        for i in range(4):
                            MM(A8(b0)[:, i, :], BB[w][1][:, i, 0, :], BA[w][1][:, i, :, :].rearrange("p a c -> p (a c)"),
                               ["BA%d1" % w, "BB%d1" % w], [PB[b0 + i // 2]], start=True, stop=False, inc=False)
                            MM(A8(b0)[:, i, 128:256], ident[:], BA[w][1][:, i, 1, :], ["ident", "BA%d1" % w], [PB[b0 + i // 2]],
                               start=False, stop=True, inc=(i == 3))
                        evac_copy(BA[w][0][:].rearrange("p i a c -> p i (a c)"), A8(b0), [PB[b0], PB[b0 + 1]], ["BA%d0" % w])
                    for w in range(2):
                        b0 = 4 + 2 * w
                        for i in range(4):
                            MM(A4(b0)[:, i, :], BA[w][1][:, i, 0, :], BB[w][1][:, i, 0, :], ["BA%d1" % w, "BB%d1" % w], [PB[b0]], inc=(i == 3))
                        evac_copy(BB[w][0][:, :, 0, :], A4(b0), [PB[b0]], ["BB%d0" % w])
                    yield
                    for w in range(2):
                        b0 = 2 * w
                        for i in range(4):
                            MM(A4(b0)[:, i, :], BB[w][0][:, i, 0, :], BA[w][0][:, i, 1, :], ["BA%d0" % w, "BB%d0" % w], [PB[b0]],
                               start=True, stop=False, inc=False)
                            MM(A4(b0)[:, i, :], ident[:], BA[w][0][:, i, 1, :], ["ident", "BA%d0" % w], [PB[b0]], start=False, stop=True,
                               inc=(i == 3))
                        evac_copy(Wb[w][0][:], A4(b0), [PB[b0]], ["Wb%d0" % w])
                    yield
                    for li in range(4):
                        cur, nxt = li % 2, (li + 1) % 2
                        last = (li == 3)
                        Cm, Cmn = CmAll[li], "CmAll%d" % li
                        for w in range(2):
                            b0 = 2 * w
                            Wc, Wcn, Dc, Dcn = Wb[w][cur], "Wb%d%d" % (w, cur), Db[w][cur], "Db%d%d" % (w, cur)
                            pTw = pb[b0 + 1][:].bitcast(BF16).rearrange("p (k t) -> p k t", k=8)
                            for i in range(4):
                                q = w * 4 + i
                                MM(A4(b0)[:, i, :], Cm[:, q, :], Wc[:, i, :], [Cmn, Wcn], [PB[b0]], inc=False)
                                TR(pTw[:, i, :], Wc[:, i, :], ident[:], [Wcn, "ident"], [PB[b0 + 1]], inc=(i == 3))
                            evac_copy(Yb[w][:], A4(b0), [PB[b0]], ["Yb%d" % w])
                            evac_copy(Dc[:], pTw[:, 0:4, :], [PB[b0 + 1]], [Dcn])
                        yield
                        for w in range(2):
                            b0 = 2 * w
                            Wc, Wcn, Dc, Dcn = Wb[w][cur], "Wb%d%d" % (w, cur), Db[w][cur], "Db%d%d" % (w, cur)
                            for i in range(4):
                                MM(A4(b0)[:, i, :], Dc[:, i, :], Yb[w][:, i, :], [Dcn, "Yb%d" % w], [PB[b0]], start=True, stop=False, inc=False)
                                MM(A4(b0)[:, i, :], ident[:], Wc[:, i, :], ["ident", Wcn], [PB[b0]], start=False, stop=True, inc=(i == 3))
                            if last:
                                evac_copy(XT[:, w * 4:(w + 1) * 4, :], A4(b0), [PB[b0]], [XTn])
                            else:
                                evac_copy(Wb[w][nxt][:], A4(b0), [PB[b0]], ["Wb%d%d" % (w, nxt)])
                        yield

                def stream_C(d, n, tt):
                    lat = tt >= 2
                    lt = tt - 2
                    tok = slice(tt * 128, (tt + 1) * 128)
                    b_t = beta[:, tt, d * 8:(d + 1) * 8]
                    eg, egn = egS[n % 3], "eg%d" % (n % 3)
                    egl2, egl2n = egl2S[n % 3], "egl2%d" % (n % 3)
                    Aqk, Aqkn = AqkS[n % 3], "Aqk%d" % (n % 3)
                    XT, XTn = XTS[n % 2], "XT%d" % (n % 2)
                    ktn = k_tok[:, tt, :, :].rearrange("p (hp par) e -> p par hp e", par=2)
                    TT("pool", v4q(kg[:]), ktn, bc4(eg[:, 0:8], 64), ALU.mult, ["k_tok", egn], ["kg"])
                    TT("pool", eb[:], eg[:, 8:16], b_t, ALU.mult, [egn, "beta"], ["eb"])
                    TT("pool", v4q(kdb[:]), ktn, bc4(eb[:], 64), ALU.mult, ["k_tok", "eb"], ["kdb"])
                    up_ps = pb[4].rearrange("p (q e) -> p q e", e=64)
                    for h in range(8):
                        q = (h % 2) * 4 + h // 2
                        MM(up_ps[:, q, :], XT[:, q, :], v_tok[:, tt, h, :], [XTn, "v_tok"], [PB[4]], inc=(h == 7))
                    for h in range(8):
                        par, hp = h % 2, h // 2
                        q = par * 4 + hp
                        MM(psall[par * 64:(par + 1) * 64, 5, hp * 128:(hp + 1) * 128], kg[:, q, :], XT[:, q, :], ["kg", XTn],
                           [PB[5]], inc=(h == 7))
                    CP("act", up[:], up_ps, [PB[4]], ["up"])
                    CP("act", wT[:], pb[5].rearrange("p (a i) -> p a i", i=128), [PB[5]], ["wT"])
                    yield
                    wS_ps = psall[:, 6:8, 0:256].rearrange("p b (hp e) -> p b hp e", e=64)
                    for h in range(8):
                        par, hp = h % 2, h // 2
                        pr = par * 64
                        MM(wS_ps[:, par, hp, :], wT[pr:pr + 64, hp, :], Sbf[pr:pr + 64, hp, :], ["wT", "Sbf"], [PB[6 + par]], inc=(h >= 6))
                    TT("dve", v4q(vt[:]), v4q(up[:]), wS_ps, ALU.subtract, ["up", PB[6], PB[7]], ["vt"])
                    yield
                    if lat:
                        qS_ps = psall[:, 4:6, 0:256].rearrange("p b (hp e) -> p b hp e", e=64)
                        Av_ps = pb[6].rearrange("p (q e) -> p q e", e=64)
                        for h in range(8):
                            par, hp = h % 2, h // 2
                            pr = par * 64
                            MM(qS_ps[:, par, hp, :], dnqT[pr:pr + 64, hp, tok], Sbf[pr:pr + 64, hp, :], ["dnqT%d" % hp, "Sbf"],
                               [PB[4 + par]], inc=(h >= 6))
                        for q in range(8):
                            MM(Av_ps[:, q, :], Aqk[:, q, :], vt[:, q, :], [Aqkn, "vt"], [PB[6]], inc=(q == 7))
                    for h in range(8):
                        par, hp = h % 2, h // 2
                        q = par * 4 + hp
                        MM(psall[par * 64:(par + 1) * 64, 7, hp * 64:(hp + 1) * 64], kdb[:, q, :], vt[:, q, :], ["kdb", "vt"],
                           [PB[7]], inc=(h == 7))
                    TT("dve", S32[:], S32[:], egl2[:].unsqueeze(2).to_broadcast([128, 4, 64]), ALU.mult, ["S32", egl2n], ["S32"])
                    TT("dve", S32[:], S32[:], pb[7][:, 0:256].rearrange("p (a e) -> p a e", e=64), ALU.add, ["S32", PB[7]], ["S32"])
                    CP("act", Sbf[:], S32[:], ["S32"], ["Sbf"])
                    if lat:
                        TT("dve", v4q(otmp[:]), qS_ps, bc4(eg[:, 0:8], 64), ALU.mult, [PB[4], PB[5], egn], ["otmp"])
                        if d == 0:
                            TT("dve", o_acc[:, lt, :, :], otmp[:], Av_ps, ALU.add, ["otmp", PB[6]], ["o_acc%d" % lt])
                        else:
                            TT("dve", ofin[:], otmp[:], Av_ps, ALU.add, ["otmp", PB[6]], ["ofin"])
                    if os.environ.get("DN_TAP") == "%d,%d" % (d, tt):
                        tap("eg", eg[:], [128, 16], F32, [egn])
                        tap("XT", XT[:], [128, 8, 128], BF16, [XTn])
                        tap("up", up[:], [128, 8, 64], F32, ["up"])
                        tap("wT", wT[:], [128, 4, 128], BF16, ["wT"])
                        tap("vt", vt[:], [128, 8, 64], BF16, ["vt"])
                        tap("S32", S32[:], [128, 4, 64], F32, ["S32"])
                        tap("kg", kg[:], [128, 8, 64], BF16, ["kg"])
                        tap("R0", R0S[n % 2][:], [128, 8, 128], BF16, ["R0b%d" % (n % 2)])
                    yield
                    if lat:
                        if d == 1:
                            TT("pool", ofin[:], ofin[:], o_acc[:, lt, :, :], ALU.add, ["ofin", "o_acc%d" % lt], ["ofin"])
                            TT("pool", otmp[:], ofin[:], ofin[:], ALU.mult, ["ofin"], ["otmp"])
                            op("dve", lambda e: e.tensor_reduce(out=oss[:], in_=otmp[:], axis=AX.X, op=ALU.add), ["otmp"], ["oss"])
                            TS("dve", oss[:], oss[:], 1.0 / 64, EPS, ALU.mult, ALU.add, ["oss"], ["oss"])
                            ACTF(oss[:], oss[:], AF.Ln, ["oss"], ["oss"])
                            ACTF(oss[:], oss[:], AF.Exp, ["oss"], ["oss"], scale=-0.5)
                            TT("dve", ofin[:], ofin[:], bc_i(oss[:], 64), ALU.mult, ["ofin", "oss"], ["ofin"])
                            TT("dve", onb[:].rearrange("p (hp par) e -> p par hp e", par=2), v4q(ofin[:]),
                               dnwB[:].unsqueeze(1).unsqueeze(1).to_broadcast([128, 2, 4, 64]), ALU.mult, ["ofin", "dnwB"], ["onb"])
                            yield
                            pT = pb[5][:].bitcast(BF16).rearrange("p (k t) -> p k t", k=8)
                            onv = onb[:].rearrange("p (a b) e -> p a (b e)", b=2)
                            for hp in range(4):
                                TR(pT[:, 4 + hp, :], onv[:, hp, :], ident[:], ["onb", "ident"], [PB[5]], inc=(hp == 3))
                            CP("act", odT[:, :, lt * 128:(lt + 1) * 128], pT[:, 4:8, :], [PB[5]], ["odT"])
                    yield

                def run_streams(gens, periods=None):
                    items = [(g, (periods[k] if periods else 1)) for k, g in enumerate(gens) if g is not None]
                    rnd = 0
                    while items:
                        for it in list(items):
                            g, per = it
                            if rnd % per:
                                continue
                            try:
                                next(g)
                            except StopIteration:
                                items.remove(it)
                        rnd += 1

                for d in range(int(os.environ.get("DN_PASSES", "2"))):
                    order = list(range(NT)) if d == 0 else [1, 0] + list(range(NT - 1, 1, -1))
                    order = order[:DN_NT]
                    op("pool", lambda e: e.memset(S32[:], 0.0), (), ["S32"])
                    op("pool", lambda e: e.memset(Sbf[:], 0.0), (), ["Sbf"])
                    nn = len(order)
                    run_streams([stream_P(d, 0, order[0])])
                    for n in range(nn):
                        run_streams([stream_I(d, n, order[n]),
                                     stream_P(d, n + 1, order[n + 1]) if n + 1 < nn else None,
                                     stream_C(d, n - 1, order[n - 1]) if n >= 1 else None],
                                    periods=[int(os.environ.get("PER_I", "1")), int(os.environ.get("PER_P", "1")), int(os.environ.get("PER_C", "1"))])
                    run_streams([stream_C(d, nn - 1, order[nn - 1])])
                tap("odT", odT[:], [128, 4, T_LAT], BF16, ["odT"])
                fw.barrier()
            dn_era.close()

        if stop_after not in ("C1", "D", "A"):
            omT = fw.sbuf("omT", [128, 4, T_LAT], BF16)
            hT = fw.sbuf("hT2", [128, 8, T_ALL], BF16)
            for kt in range(8):
                dma("sp", hT[:, kt, :], hT_spill[:, kt, :], reads=["hT_spill"], writes=["hT%d" % t for t in range(NT)], slot="fill%d" % kt)
            mla_era = ExitStack()
            qT_all = fw.sbuf("qT_all", [128, 8, T_LAT], BF16, mla_era)
            kT_all = fw.sbuf("kT_all", [128, 8, T_ALL], BF16, mla_era)
            V_all = fw.sbuf("V_all", [128, NT, 8, 65], BF16, mla_era)
            negC = fw.sbuf("negC", [128, 1], F32, mla_era)
            with ExitStack() as c2:
                A_ = lambda name, shape, dt: fw.sbuf(name, shape, dt, c2)
                wtok = A_("wtok", [128, 8, 672], BF16)
                wuq = A_("wuq", [128, 3, 768], BF16)
                wukv = A_("wukv", [128, 2, 1024], BF16)
                qnwT = A_("qnwT", [128, 3], F32)
                kvnwT = A_("kvnwT", [128, 2], F32)
                qhwB = A_("qhwB", [128, 96], F32)
                khwB = A_("khwB", [128, 96], F32)
                cosT = A_("cosT", [128, 16, 16], F32)
                sinT = A_("sinT", [128, 16, 16], F32)
                invn = A_("invn", [128, 2], F32)
                ssA = A_("ssA", [128, 4], F32)
                rs2 = A_("rs2", [128, 2], F32)
                junk2 = A_("junk2", [128, 384], BF16)
                cqn = A_("cqn", [128, 384], BF16)
                ckvn = A_("ckvn", [128, 256], BF16)
                cqnT = A_("cqnT", [128, 3, 128], BF16)
                ckvnT = A_("ckvnT", [128, 2, 128], BF16)
                sqq = A_("sqq", [128, 8, 96], F32)
                ss16 = A_("ss16", [128, 16], F32)
                q_fin = A_("q_fin", [128, 8, 96], BF16)
                k_fin = A_("k_fin", [128, 8, 96], BF16)
                tl = A_("tl", [128, 8, 32], F32)
                ra = A_("ra", [128, 8, 2, 8], F32)
                rb = A_("rb", [128, 8, 2, 8], F32)
                cmx = A_("cmx", [128, 4], F32)
                dma("pool", wtok[:], winv[:, :, 0:672], writes=["wtok"])
                dma("pool", wuq[:], w_uq.rearrange("(kt p) n -> p kt n", p=128), writes=["wuq"])
                dma("pool", wukv[:], w_ukv.rearrange("(kt p) n -> p kt n", p=128), writes=["wukv"])
                with nc.allow_non_contiguous_dma(reason="small vector column layouts"):
                    dma("sp", qnwT[:], q_norm_w.rearrange("(kt p) -> p kt", p=128), writes=["qnwT"])
                    dma("sp", kvnwT[:], kv_norm_w.rearrange("(kt p) -> p kt", p=128), writes=["kvnwT"])
                    dma("sp", cosT[:], cos_d.rearrange("(t p) c -> p t c", p=128), writes=["cosT"])
                    dma("sp", sinT[:], sin_d.rearrange("(t p) c -> p t c", p=128), writes=["sinT"])
                dma("sp", qhwB[:], qh_w.partition_broadcast(128), writes=["qhwB"])
                dma("sp", khwB[:], kh_w.partition_broadcast(128), writes=["khwB"])
                op("pool", lambda e: e.memset(ssA[:], 1.0), (), ["ssA"])
                op("pool", lambda e: e.memset(invn[:, 0:1], 1.0 / 384), (), ["invn"])
                op("pool", lambda e: e.memset(invn[:, 1:2], 1.0 / 256), (), ["invn"])
                op("pool", lambda e: e.memset(V_all[:, :, :, 64:65], 1.0), (), ["V_all"])
                TT("dve", sqq[:, 0, :], qhwB[:], qhwB[:], ALU.mult, ["qhwB"], ["sqq"])
                TT("dve", sqq[:, 1, :], khwB[:], khwB[:], ALU.mult, ["khwB"], ["sqq"])
                op("dve", lambda e: e.tensor_reduce(out=cmx[:, 0:2], in_=sqq[:, 0:2, :], axis=AX.X, op=ALU.max), ["sqq"], ["cmx"])
                TT("dve", cmx[:, 2:3], cmx[:, 0:1], cmx[:, 1:2], ALU.mult, ["cmx"], ["cmx"])
                ACTF(cmx[:, 2:3], cmx[:, 2:3], AF.Ln, ["cmx"], ["cmx"])
                ACTF(cmx[:, 3:4], cmx[:, 2:3], AF.Exp, ["cmx"], ["cmx"], scale=0.5)
                TS("dve", negC[:], cmx[:, 3:4], -math.sqrt(96.0), None, ALU.mult, None, ["cmx"], ["negC"])
                v8 = lambda bk: psall[:, bk:bk + 2, :].rearrange("p b (h c) -> p (b h) c", c=128)
                qfS = [A_("qfS%d" % i, [128, 8, 96], F32) for i in range(2)]
                kvfS = [A_("kvfS%d" % i, [128, 8, 128], F32) for i in range(2)]
                krsS = [A_("krsS%d" % i, [128, 32], F32) for i in range(2)]
                skrS = [A_("skrS%d" % i, [128, 1], F32) for i in range(2)]
                bq = lambda v_, n: v_.unsqueeze(2).to_broadcast([128, 8, n])
                bh = lambda v_, n: v_.unsqueeze(1).to_broadcast([128, 8, n])
                r4 = lambda t_, b_: t_.rearrange("p h (a b f) -> p h a b f", a=2, b=2)[:, :, :, b_, :]

                def stream_X(tt):
                    lat = tt >= 2
                    tok = slice(tt * 128, (tt + 1) * 128)
                    par = tt % 2
                    hb = ["hT%d" % tt]
                    qf_, qfn = qfS[par], "qfS%d" % par
                    kvf, kvfn = kvfS[par], "kvfS%d" % par
                    krs_, krsn = krsS[par], "krsS%d" % par
                    skr, skrn = skrS[par], "skrS%d" % par
                    p_cq, p_kv = pb[0][:, 0:384], pb[1][:, 0:288]
                    for kt in range(8):
                        if lat:
                            MM(p_cq, hT[:, kt, tok], wtok[:, kt, 0:384], hb + ["wtok"], [PB[0]], start=(kt == 0), stop=(kt == 7), inc=False)
                        MM(p_kv, hT[:, kt, tok], wtok[:, kt, 384:672], hb + ["wtok"], [PB[1]], start=(kt == 0), stop=(kt == 7), inc=(kt == 7))
                    if lat:
                        op("act", lambda e: e.activation(out=junk2[:], in_=p_cq, func=AF.Square, accum_out=ssA[:, 0:1]), [PB[0]], ["junk2", "ssA"])
                    op("act", lambda e: e.activation(out=junk2[:, 0:256], in_=p_kv[:, 0:256], func=AF.Square, accum_out=ssA[:, 1:2]),
                       [PB[1]], ["junk2", "ssA"])
                    op("act", lambda e: e.activation(out=junk2[:, 0:32], in_=p_kv[:, 256:288], func=AF.Square, accum_out=skr[:]),
                       [PB[1]], ["junk2", skrn])
                    TT("dve", rs2[:], ssA[:, 0:2], invn[:], ALU.mult, ["ssA", "invn"], ["rs2"])
                    TS("dve", rs2[:], rs2[:], EPS, None, ALU.add, None, ["rs2"], ["rs2"])
                    ACTF(rs2[:], rs2[:], AF.Ln, ["rs2"], ["rs2"])
                    ACTF(rs2[:], rs2[:], AF.Exp, ["rs2"], ["rs2"], scale=-0.5)
                    yield
                    if lat:
                        ACTF(cqn[:], p_cq, AF.Identity, [PB[0], "rs2"], ["cqn"], scale=rs2[:, 0:1])
                    TS("dve", ckvn[:], p_kv[:, 0:256], rs2[:, 1:2], None, ALU.mult, None, [PB[1], "rs2"], ["ckvn"])
                    CP("act", krs_[:], p_kv[:, 256:288], [PB[1]], [krsn])
                    pT = pb[2][:].bitcast(BF16).rearrange("p (k t) -> p k t", k=8)
                    if lat:
                        for i in range(3):
                            TR(pT[:, i, :], cqn[:, i * 128:(i + 1) * 128], ident[:], ["cqn", "ident"], [PB[2]], inc=False)
                    for i in range(2):
                        TR(pT[:, 3 + i, :], ckvn[:, i * 128:(i + 1) * 128], ident[:], ["ckvn", "ident"], [PB[2]], inc=(i == 1))
                    yield
                    if lat:
                        TT("dve", cqnT[:], pT[:, 0:3, :], qnwT[:].unsqueeze(2).to_broadcast([128, 3, 128]), ALU.mult, [PB[2], "qnwT"], ["cqnT"])
                    TT("dve", ckvnT[:], pT[:, 3:5, :], kvnwT[:].unsqueeze(2).to_broadcast([128, 2, 128]), ALU.mult, [PB[2], "kvnwT"], ["ckvnT"])
                    if lat:
                        for (c0, c1, bk) in ((0, 512, 3), (512, 768, 4)):
                            for kt in range(3):
                                MM(pb[bk][:, 0:c1 - c0], cqnT[:, kt, :], wuq[:, kt, c0:c1], ["cqnT", "wuq"], [PB[bk]],
                                   start=(kt == 0), stop=(kt == 2), inc=(kt == 2))
                    for nh in range(2):
                        for kt in range(2):
                            MM(pb[5 + nh][:], ckvnT[:, kt, :], wukv[:, kt, nh * 512:(nh + 1) * 512], ["ckvnT", "wukv"], [PB[5 + nh]],
                               start=(kt == 0), stop=(kt == 1), inc=(kt == 1))
                    yield
                    qff = qf_[:].rearrange("p h c -> p (h c)")
                    kvff = kvf[:].rearrange("p h c -> p (h c)")
                    if lat:
                        evac_copy(qff[:, 0:512], pb[3][:], [PB[3]], [qfn])
                        evac_copy(qff[:, 512:768], pb[4][:, 0:256], [PB[4]], [qfn])
                    evac_copy(kvff[:, 0:512], pb[5][:], [PB[5]], [kvfn])
                    evac_copy(kvff[:, 512:1024], pb[6][:], [PB[6]], [kvfn])
                    yield

                def stream_Y(tt):
                    sqk = sqq[:, :, 0:64]
                    lat = tt >= 2
                    lt = tt - 2
                    tok = slice(tt * 128, (tt + 1) * 128)
                    par = tt % 2
                    qf, qfn = qfS[par], "qfS%d" % par
                    kvv, kvfn = kvfS[par], "kvfS%d" % par
                    krs, krsn = krsS[par], "krsS%d" % par
                    skr, skrn = skrS[par], "skrS%d" % par
                    if lat:
                        TT("pool", sqq[:], qf[:], qf[:], ALU.mult, [qfn], ["sqq"])
                        op("dve", lambda e: e.tensor_reduce(out=ss16[:, 0:8], in_=sqq[:], axis=AX.X, op=ALU.add), ["sqq"], ["ss16"])
                    else:
                        op("pool", lambda e: e.memset(ss16[:, 0:8], 1.0), (), ["ss16"])
                    TT("pool", sqk, kvv[:, :, 0:64], kvv[:, :, 0:64], ALU.mult, [kvfn], ["sqq"])
                    op("dve", lambda e: e.tensor_reduce(out=ss16[:, 8:16], in_=sqk, axis=AX.X, op=ALU.add), ["sqq"], ["ss16"])
                    TS("dve", ss16[:, 8:16], ss16[:, 8:16], skr[:], None, ALU.add, None, ["ss16", skrn], ["ss16"])
                    TS("dve", ss16[:], ss16[:], 1.0 / 96, EPS, ALU.mult, ALU.add, ["ss16"], ["ss16"])
                    ACTF(ss16[:], ss16[:], AF.Ln, ["ss16"], ["ss16"])
                    ACTF(ss16[:], ss16[:], AF.Exp, ["ss16"], ["ss16"], scale=-0.5)
                    yield

                    def rope(src_t, dst_fin, cs, sn):
                        cB = cs.rearrange("p (a f) -> p a f", a=2).unsqueeze(1).to_broadcast([128, 8, 2, 8])
                        sB = sn.rearrange("p (a f) -> p a f", a=2).unsqueeze(1).to_broadcast([128, 8, 2, 8])
                        t1, t2 = r4(src_t, 0), r4(src_t, 1)
                        o1, o2 = r4(dst_fin, 0), r4(dst_fin, 1)
                        TT("dve", ra[:], t1, cB, ALU.mult, ["tl", "cosT"], ["ra"])
                        TT("pool", rb[:], t2, sB, ALU.mult, ["tl", "sinT"], ["rb"])
                        TT("dve", o1, ra[:], rb[:], ALU.subtract, ["ra", "rb"], ["fin"])
                        TT("dve", ra[:], t1, sB, ALU.mult, ["tl", "sinT"], ["ra"])
                        TT("pool", rb[:], t2, cB, ALU.mult, ["tl", "cosT"], ["rb"])
                        TT("dve", o2, ra[:], rb[:], ALU.add, ["ra", "rb"], ["fin"])

                    if lat:
                        TT("dve", qf[:], qf[:], bq(ss16[:, 0:8], 96), ALU.mult, [qfn, "ss16"], [qfn])
                        TT("dve", q_fin[:, :, 0:64], qf[:, :, 0:64], bh(qhwB[:, 0:64], 64), ALU.mult, [qfn, "qhwB"], ["fin"])
                        TT("dve", tl[:], qf[:, :, 64:96], bh(qhwB[:, 64:96], 32), ALU.mult, [qfn, "qhwB"], ["tl"])
                        rope(tl[:], q_fin[:, :, 64:96], cosT[:, lt, :], sinT[:, lt, :])
                        pq = pb[7][:].bitcast(BF16).rearrange("p (k t) -> p k t", k=8)
                        for h in range(8):
                            TR(pq[0:96, h, :], q_fin[:, h, :], ident[:], ["fin", "ident"], [PB[7]], inc=(h == 7))
                        evac_copy(qT_all[0:96, :, lt * 128:(lt + 1) * 128], pq[0:96, :, :], [PB[7]], ["qT_all"])
                    yield
                    TT("dve", sqk, kvv[:, :, 0:64], bq(ss16[:, 8:16], 64), ALU.mult, [kvfn, "ss16"], ["sqq"])
                    TT("dve", k_fin[:, :, 0:64], sqk, bh(khwB[:, 0:64], 64), ALU.mult, ["sqq", "khwB"], ["fin"])
                    TT("dve", tl[:], bh(krs[:], 32), bq(ss16[:, 8:16], 32), ALU.mult, [krsn, "ss16"], ["tl"])
                    TT("dve", tl[:], tl[:], bh(khwB[:, 64:96], 32), ALU.mult, ["tl", "khwB"], ["tl"])
                    if lat:
                        rope(tl[:], k_fin[:, :, 64:96], cosT[:, lt, :], sinT[:, lt, :])
                    else:
                        CP("act", k_fin[:, :, 64:96], tl[:], ["tl"], ["fin"])
                    CP("act", V_all[:, tt, :, 0:64], kvv[:, :, 64:128], [kvfn], ["V_all"])
                    pk = pb[7][:].bitcast(BF16).rearrange("p (k t) -> p k t", k=8)
                    for h in range(8):
                        TR(pk[0:96, h, :], k_fin[:, h, :], ident[:], ["fin", "ident"], [PB[7]], inc=(h == 7))
                    evac_copy(kT_all[0:96, :, tok], pk[0:96, :, :], [PB[7]], ["kT_all"])
                    yield

                def run2(gens):
                    gens = [g for g in gens if g is not None]
                    while gens:
                        for g in list(gens):
                            try:
                                next(g)
                            except StopIteration:
                                gens.remove(g)

                run2([stream_X(0)])
                for tt in range(NT):
                    run2([stream_Y(tt), stream_X(tt + 1) if tt + 1 < NT else None])
                tap("qT_all", qT_all[0:96, :, :], [96, 8, T_LAT], BF16, ["qT_all"])
                tap("kT_all", kT_all[0:96, :, :], [96, 8, T_ALL], BF16, ["kT_all"])
                tap("V_all", V_all[:], [128, NT, 8, 65], BF16, ["V_all"])
                fw.barrier()

            if stop_after != "C2":
                with ExitStack() as pe_:
                    PT = [fw.sbuf("PT%d" % i, [128, 2, 512], BF16, pe_) for i in range(2)]
                    o_tok = fw.sbuf("o_tok", [128, 4, 512], BF16, pe_)
                    rec = fw.sbuf("rec", [128, 4], F32, pe_)
                    SCALE = 96.0 ** -0.5
                    steps = [(g, h, jp) for g in range(4) for h in range(8) for jp in range(9)]

                    def acc_of(g, h):
                        ab = 4 + ((g * 8 + h) % 2)
                        return ab, pb[ab][:, 0:260].rearrange("p (q c) -> p q c", c=65)

                    def scores(k):
                        g, h, jp = steps[k]
                        sb = 2 * (k % 2)
                        for t in range(2):
                            kt_ = 2 * jp + t
                            MM(pb[sb + t][:], kT_all[0:96, h, kt_ * 128:(kt_ + 1) * 128], qT_all[0:96, h, g * 512:(g + 1) * 512],
                               ["kT_all", "qT_all"], [PB[sb + t]], inc=(t == 1))

                    def expo(k):
                        sb = 2 * (k % 2)
                        Pt, Ptn = PT[k % 2], "PT%d" % (k % 2)
                        op("act", lambda e: e.activation(out=Pt[:], in_=psall[:, sb:sb + 2, :], func=AF.Exp, scale=SCALE,
                                                         bias=negC[:]), [PB[sb], PB[sb + 1], "negC"], [Ptn])

                    def pv(k):
                        g, h, jp = steps[k]
                        ab, acc = acc_of(g, h)
                        Pt, Ptn = PT[k % 2], "PT%d" % (k % 2)
                        for t in range(2):
                            kt_ = 2 * jp + t
                            for qs in range(4):
                                first = (jp == 0 and t == 0 and qs == 0)
                                lastmm = (jp == 8 and t == 1 and qs == 3)
                                op("pe", lambda e: e.matmul(acc[:, qs, :], lhsT=Pt[:, t, qs * 128:(qs + 1) * 128],
                                                            rhs=V_all[:, kt_, h, :], start=first, stop=lastmm,
                                                            skip_group_check=True),
                                   [Ptn, "V_all"], [PB[ab]], inc=(t == 1 and qs == 3))

                    scores(0)
                    for k in range(len(steps)):
                        g, h, jp = steps[k]
                        expo(k)
                        if k + 1 < len(steps):
                            scores(k + 1)
                        pv(k)
                        if jp != 8:
                            continue
                        ab, acc = acc_of(g, h)
                        qs_tok = slice(g * 512, (g + 1) * 512)
                        op("dve", lambda e: e.reciprocal(out=rec[:], in_=acc[:, :, 64]), [PB[ab]], ["rec"])
                        TT("dve", o_tok[:, :, h * 64:(h + 1) * 64], acc[:, :, 0:64], rec[:].unsqueeze(2).to_broadcast([128, 4, 64]),
                           ALU.mult, [PB[ab], "rec"], ["o_tok"])
                        if h != 7:
                            continue
                        for half in range(2):
                            bk = 6 + half
                            pT = pb[bk][:].bitcast(BF16).rearrange("p (k t) -> p k t", k=8)
                            for qq in range(2):
                                qs = half * 2 + qq
                                for c4 in range(4):
                                    TR(pT[:, qq * 4 + c4, :], o_tok[:, qs, c4 * 128:(c4 + 1) * 128], ident[:], ["o_tok", "ident"], [PB[bk]],
                                       inc=(qq == 1 and c4 == 3))
                            dst = omT[:, :, qs_tok].rearrange("p c (qs t) -> p qs c t", qs=4)[:, half * 2:half * 2 + 2, :, :]
                            evac_copy(dst, pT.rearrange("p (qq c) t -> p qq c t", qq=2), [PB[bk]], ["omT"])
                    tap("omT", omT[:], [128, 4, T_LAT], BF16, ["omT"])
                    fw.barrier()
            mla_era.close()

            if stop_after not in ("C2", "E"):
                with ExitStack() as pf:
                    A_ = lambda name, shape, dt: fw.sbuf(name, shape, dt, pf)
                    wz = A_("wz", [128, 8, 3072], BF16)
                    wmo = A_("wmo", [128, 4, D], BF16)
                    wdo = A_("wdo", [128, 4, D], BF16)
                    wo = A_("wo", [128, 8, D], BF16)
                    sg = A_("sg", [128, 16, 512], BF16)
                    szm = A_("szm", [128, 512], BF16)
                    om_s = A_("om_s", [128, 4, 512], BF16)
                    od_s = A_("od_s", [128, 4, 512], BF16)
                    mT = A_("mT", [128, 8, 512], BF16)
                    t1S = [A_("t1%d" % i, [128, 512], F32) for i in range(2)]
                    t2S = [A_("t2%d" % i, [128, 512], F32) for i in range(2)]
                    xt = [A_("xt%d" % i, [128, D], F32) for i in range(2)]
                    ot = [A_("ot%d" % i, [128, D], F32) for i in range(1)]
                    dma("pool", wz[:, :, 0:512], winv[:, :, 672:1184], writes=["wz_m"])
                    dma("pool", wz[:, :, 512:1024], winv[:, :, 2720:3232], writes=["wz_d"])
                    for i in range(4):
                        dma("pool", wz[:, :, 1024 + i * 512:1536 + i * 512], winv[:, :, 3264 + i * 512:3776 + i * 512], writes=["wz_g%d" % i])
                    dma("pool", wmo[:], mla_w_o.rearrange("(c p) n -> p c n", p=128), writes=["wmo"])
                    dma("pool", wdo[:], dn_w_o.rearrange("(c p) n -> p c n", p=128), writes=["wdo"])
                    dma("pool", wo[:], w_out.rearrange("(c p) n -> p c n", p=128), writes=["wo"])
                    rr = 0
                    for g in range(4):
                        lt0 = g * 4
                        ltok = slice(g * 512, (g + 1) * 512)
                        htok = slice(256 + g * 512, 256 + (g + 1) * 512)
                        hb = hT_bufs(256 + g * 512, 512)
                        for (which, src_T, srcn, dst_s, dsn, wn) in ((0, omT, "omT", om_s, "om_s", "wz_m"), (1, odT, "odT", od_s, "od_s", "wz_d")):
                            for f in range(4):
                                bk = rr % 4
                                rr += 1
                                for kt in range(8):
                                    MM(pb[bk][:], wz[:, kt, which * 512 + f * 128:which * 512 + (f + 1) * 128], hT[:, kt, htok], hb + [wn],
                                       [PB[bk]], start=(kt == 0), stop=(kt == 7), inc=(kt == 7))
                                ACTF(szm[:], pb[bk][:], AF.Silu, [PB[bk]], ["szm"])
                                TT("dve", dst_s[:, f, :], src_T[:, f, ltok], szm[:], ALU.mult, [srcn, "szm"], [dsn])
                        for c in range(16):
                            bk = rr % 4
                            rr += 1
                            for kt in range(8):
                                MM(pb[bk][:], wz[:, kt, 1024 + c * 128:1024 + (c + 1) * 128], hT[:, kt, htok], hb + ["wz_g%d" % (c // 4)],
                                   [PB[bk]], start=(kt == 0), stop=(kt == 7), inc=(kt == 7))
                            ACTF(sg[:, c, :], pb[bk][:], AF.Sigmoid, [PB[bk]], ["sg"])
                        for f8 in range(8):
                            ba, bb_ = (4, 5) if f8 % 2 == 0 else (6, 7)
                            t1_, t1n = t1S[f8 % 2], "t1%d" % (f8 % 2)
                            t2_, t2n = t2S[f8 % 2], "t2%d" % (f8 % 2)
                            for c4 in range(4):
                                MM(pb[ba][:], wmo[:, c4, f8 * 128:(f8 + 1) * 128], om_s[:, c4, :], ["wmo", "om_s"], [PB[ba]],
                                   start=(c4 == 0), stop=(c4 == 3), inc=(c4 == 3))
                            for c4 in range(4):
                                MM(pb[bb_][:], wdo[:, c4, f8 * 128:(f8 + 1) * 128], od_s[:, c4, :], ["wdo", "od_s"], [PB[bb_]],
                                   start=(c4 == 0), stop=(c4 == 3), inc=(c4 == 3))
                            TT("dve", t1_[:], pb[ba][:], sg[:, f8, :], ALU.mult, [PB[ba], "sg"], [t1n])
                            TT("dve", t2_[:], pb[bb_][:], sg[:, 8 + f8, :], ALU.mult, [PB[bb_], "sg"], [t2n])
                            TT("pool", mT[:, f8, :], t1_[:], t2_[:], ALU.add, [t1n, t2n], ["mT"])
                        for qs in range(4):
                            lt = lt0 + qs
                            xb_, xbn = xt[lt % 2], "xt%d" % (lt % 2)
                            ob_, obn = ot[0], "ot0"
                            if lt == 0:
                                dma("sp", xb_[:], x[0:128, :], writes=[xbn])
                            if lt + 1 < 16:
                                dma("sp", xt[(lt + 1) % 2][:], x[(lt + 1) * 128:(lt + 2) * 128, :], writes=["xt%d" % ((lt + 1) % 2)])
                            for nh in range(2):
                                bk = 2 * (qs % 2) + nh
                                for f8 in range(8):
                                    MM(pb[bk][:], mT[:, f8, qs * 128:(qs + 1) * 128], wo[:, f8, nh * 512:(nh + 1) * 512], ["mT", "wo"], [PB[bk]],
                                       start=(f8 == 0), stop=(f8 == 7), inc=(f8 == 7))
                                cs = slice(nh * 512, (nh + 1) * 512)
                                TT("dve", ob_[:, cs], pb[bk][:], gateB[:, cs], ALU.mult, [PB[bk], "gateB"], [obn])
                                TT("pool", xb_[:, cs], ob_[:, cs], xb_[:, cs], ALU.add, [obn, xbn], [xbn])
                            dma("sp", out[lt * 128:(lt + 1) * 128, :], xb_[:], reads=[xbn], writes=["out_dram"], slot="out%d" % (lt % 2))
                    fw.wait_bufs("sp", ["out_dram"])
                    fw.barrier()

        fw.wait_bufs("sp", ["tap_" + n for n in tap_out] + ([] if stop_after else []))
        fw.barrier()
        print("[build] ops=%d waits=%d sems=%d" % (fw.n_ops, fw.n_waits, fw.nsem))
        print("[build] per-engine incs:", {n: e.cnt for n, e in fw.engs.items()})
        stuck = fw.simulate()
        print("[build] deadlock check:", "OK" if not stuck else "STUCK %s" % stuck)
    return nc, tap_out


def _in_maps(inputs):
    cst = _host_consts()
    maps = []
    f = lambda a: np.ascontiguousarray(np.asarray(a, dtype=np.float32))
    for b in range(8):
        m = {
            "x": f(inputs["x"][b]), "ctx": f(inputs["ctx"][b]), "c": f(inputs["c"][b]), "c_ctx": f(inputs["c_ctx"]),
            "w_mod": f(inputs["w_mod"][0]), "b_mod": f(inputs["b_mod"][0]), "norm_w": f(inputs["norm_w"][0]),
            "w_in": f(inputs["w_in"][0]), "dn_conv_w": f(inputs["dn_conv_w"][0]),
            "dn_a_log": f(inputs["dn_a_log"][0]).reshape(16), "dn_dt_bias": f(inputs["dn_dt_bias"][0]).reshape(16),
            "dn_out_norm_w": f(inputs["dn_out_norm_w"][0]),
            "mla_q_norm_w": f(inputs["mla_q_norm_w"][0]), "mla_w_uq": f(inputs["mla_w_uq"][0]),
            "mla_kv_norm_w": f(inputs["mla_kv_norm_w"][0]), "mla_w_ukv": f(inputs["mla_w_ukv"][0]),
            "mla_q_head_norm_w": f(inputs["mla_q_head_norm_w"][0]), "mla_k_head_norm_w": f(inputs["mla_k_head_norm_w"][0]),
            "mla_w_o": f(inputs["mla_w_o"][0]), "dn_w_o": f(inputs["dn_w_o"][0]), "w_out": f(inputs["w_out"][0]),
        }
        m.update(cst)
        maps.append(m)
    return maps


def kernel(**inputs):
    nc, _ = build_program()
    res = run_bass_kernel_spmd(nc, _in_maps(inputs), core_ids=list(range(8)))
    return np.stack([r["out"] for r in res.results], axis=0).astype(np.float32)
```

```python
import math
import os
from contextlib import ExitStack

import numpy as np
import concourse.bass as bass
import concourse.mybir as mybir
from concourse.bass_utils import run_bass_kernel_spmd

F32 = mybir.dt.float32
BF16 = mybir.dt.bfloat16
ALU = mybir.AluOpType
AF = mybir.ActivationFunctionType
AX = mybir.AxisListType

D = 1024
T_LAT = 2048
T_CTX = 256
T_ALL = 2304
NT = 18
IN_DIM = 5312
EPS = 1e-6


class Buf:
    __slots__ = ("name", "lw", "rd")

    def __init__(self, name):
        self.name = name
        self.lw = None
        self.rd = {}


class Eng:
    def __init__(self, name, be, sem, same_raw):
        self.name = name
        self.be = be
        self.sem = sem
        self.cnt = 0
        self.waited = {}
        self.same_raw = same_raw


class FW:
    def __init__(self, nc, stack):
        self.nc = nc
        self.stack = stack
        self.nsem = 0
        self.engs = {}
        for name, be, same_raw in (("pe", nc.tensor, False), ("act", nc.scalar, True),
                                   ("dve", nc.vector, True), ("pool", nc.gpsimd, True),
                                   ("sp", nc.sync, False)):
            self.engs[name] = Eng(name, be, self.new_sem("e_" + name), same_raw)
        self.dma_sems = {}
        self.n_ops = 0
        self.n_waits = 0
        self.bufs = {}
        self.trace = {n: [] for n in self.engs}

    def new_sem(self, name):
        self.nsem += 1
        return self.stack.enter_context(self.nc.semaphore(name))

    def sbuf(self, name, shape, dtype, stack=None):
        self.nalloc = getattr(self, "nalloc", 0) + 1
        return (stack or self.stack).enter_context(self.nc.sbuf_tensor("s%d_%s" % (self.nalloc, name), list(shape), dtype))

    def psum(self, name, shape, dtype=F32):
        return self.stack.enter_context(self.nc.psum_tensor("p_" + name, list(shape), dtype))

    def B(self, name):
        b = self.bufs.get(name)
        if b is None:
            b = self.bufs[name] = Buf(name)
        return b

    def _bl(self, xs):
        return [self.B(x) if isinstance(x, str) else x for x in xs]

    def _deps(self, eng, reads, writes):
        deps = {}

        def add(sem, val):
            k = id(sem)
            if k not in deps or deps[k][1] < val:
                deps[k] = (sem, val)

        for b in reads:
            if b.lw is not None:
                s, v = b.lw
                if s is eng.sem and not eng.same_raw:
                    continue
                add(s, v)
        for b in writes:
            if b.lw is not None:
                s, v = b.lw
                if s is not eng.sem or eng.same_raw:
                    add(s, v)
            for k, (s, v) in b.rd.items():
                if s is not eng.sem or eng.same_raw:
                    add(s, v)
        for k, (s, v) in deps.items():
            if eng.waited.get(k, 0) < v:
                eng.be.wait_ge(s, v)
                eng.waited[k] = v
                self.n_waits += 1
                self.trace[eng.name].append(("w", k, v))

    def _commit(self, sem, val, reads, writes):
        k = id(sem)
        for b in reads:
            if k not in b.rd or b.rd[k][1] < val:
                b.rd[k] = (sem, val)
        for b in writes:
            b.lw = (sem, val)
            b.rd = {}

    def op(self, ename, fn, reads=(), writes=(), inc=True):
        eng = self.engs[ename]
        reads = self._bl(reads)
        writes = self._bl(writes)
        self._deps(eng, reads, writes)
        ins = fn(eng.be)
        if inc:
            eng.cnt += 1
            ins.then_inc(eng.sem, 1)
            self._commit(eng.sem, eng.cnt, reads, writes)
            self.trace[ename].append(("i", id(eng.sem), 1))
        else:
            self._commit(eng.sem, eng.cnt + 1, reads, writes)
            self.trace[ename].append(("n", 0, 0))
        self.n_ops += 1
        return ins

    def dma(self, qname, out, in_, reads=(), writes=(), slot=None, **kw):
        eng = self.engs[qname]
        reads = self._bl(reads)
        writes = self._bl(writes)
        self._deps(eng, reads, writes)
        if slot is None:
            slot = writes[0].name if writes else reads[0].name
        if slot not in self.dma_sems:
            self.dma_sems[slot] = [self.new_sem("d%d" % len(self.dma_sems)), 0]
        ent = self.dma_sems[slot]
        ins = eng.be.dma_start(out=out, in_=in_, **kw)
        ent[1] += 16
        ins.then_inc(ent[0], 16)
        self.trace[qname].append(("i", id(ent[0]), 16))
        self._commit(ent[0], ent[1], reads, writes)
        self.n_ops += 1
        return ins

    def wait_bufs(self, ename, bufs):
        self._deps(self.engs[ename], self._bl(bufs), ())

    def simulate(self):
        sem = {}
        ptr = {n: 0 for n in self.trace}
        progress = True
        while progress:
            progress = False
            for n, tr in self.trace.items():
                while ptr[n] < len(tr):
                    kind, k, v = tr[ptr[n]]
                    if kind == "w":
                        if sem.get(k, 0) >= v:
                            ptr[n] += 1
                            progress = True
                        else:
                            break
                    else:
                        if kind == "i":
                            sem[k] = sem.get(k, 0) + v
                        ptr[n] += 1
                        progress = True
        stuck = {n: (ptr[n], len(tr)) for n, tr in self.trace.items() if ptr[n] < len(tr)}
        return stuck

    def barrier(self):
        targets = [(e.sem, e.cnt) for e in self.engs.values() if e.cnt > 0]
        targets += [(s, v) for (s, v) in self.dma_sems.values() if v > 0]
        for e in self.engs.values():
            for s, v in targets:
                if s is e.sem:
                    continue
                if e.waited.get(id(s), 0) < v:
                    e.be.wait_ge(s, v)
                    e.waited[id(s)] = v
                    self.n_waits += 1
                    self.trace[e.name].append(("w", id(s), v))


def _host_consts():
    c = {}
    c["ident"] = np.eye(128, dtype=np.float32)
    m = np.arange(128)[:, None]
    i = np.arange(128)[None, :]
    c["tri"] = np.stack([(m <= i), (m < i), (m >= i), (m > i), np.ones((128, 128), bool),
                         (m // 64 == i // 64)]).astype(np.float32)
    c["negoff"] = -(m != i).astype(np.float32)
    hm = [(m // 8 == i // 8)]
    for s_ in (8, 16, 32, 64):
        hm.append((m // (2 * s_) == i // (2 * s_)) & (m % (2 * s_) >= s_) & (i % (2 * s_) < s_))
    for s_ in (8, 16, 32, 64):
        hm.append((m // (2 * s_) == i // (2 * s_)) & (i % (2 * s_) >= s_) & (m % (2 * s_) < s_))
    c["hmask"] = np.stack(hm).astype(np.float32)
    t = np.arange(T_LAT)
    row = (t // 64).astype(np.float32)
    col = (t % 64).astype(np.float32)
    inv = (10000.0 ** (-np.arange(0, 16, 2, dtype=np.float32) / 16)).astype(np.float32)
    ang = np.concatenate([row[:, None] * inv, col[:, None] * inv], axis=-1).astype(np.float32)
    c["rope_cos"] = np.cos(ang).astype(np.float32)
    c["rope_sin"] = np.sin(ang).astype(np.float32)
    return c


def build_program(taps=(), stop_after=None):
    nc = bass.Bass("TRN2", target_bir_lowering=False)

    def din(name, shape):
        return nc.dram_tensor(name, list(shape), F32, kind="ExternalInput").ap()

    x = din("x", [T_LAT, D])
    ctx = din("ctx", [T_CTX, D])
    cvec = din("c", [D])
    cctx = din("c_ctx", [D])
    w_mod = din("w_mod", [D, 3 * D])
    b_mod = din("b_mod", [3 * D])
    norm_w = din("norm_w", [D])
    ident_d = din("ident", [128, 128])
    tri_d = din("tri", [6, 128, 128])
    negoff_d = din("negoff", [128, 128])
    hmask_d = din("hmask", [9, 128, 128])
    dn_w = din("dn_out_norm_w", [64])
    cos_d = din("rope_cos", [T_LAT, 16])
    sin_d = din("rope_sin", [T_LAT, 16])
    q_norm_w = din("mla_q_norm_w", [384])
    w_uq = din("mla_w_uq", [384, 768])
    kv_norm_w = din("mla_kv_norm_w", [256])
    w_ukv = din("mla_w_ukv", [256, 1024])
    qh_w = din("mla_q_head_norm_w", [96])
    kh_w = din("mla_k_head_norm_w", [96])
    mla_w_o = din("mla_w_o", [512, D])
    dn_w_o = din("dn_w_o", [512, D])
    w_out = din("w_out", [D, D])
    w_in = din("w_in", [D, IN_DIM])
    conv_w = din("dn_conv_w", [5, 1536])
    a_log = din("dn_a_log", [16])
    dt_bias = din("dn_dt_bias", [16])
    out = nc.dram_tensor("out", [T_LAT, D], F32, kind="ExternalOutput").ap()
    tap_out = {}

    with ExitStack() as st:
        fw = FW(nc, st)
        op, dma = fw.op, fw.dma

        def tap(name, sb_ap, shape, dtype, reads):
            if name not in taps:
                return
            t = nc.dram_tensor("tap_" + name, list(shape), dtype, kind="ExternalOutput").ap()
            tap_out[name] = t
            dma("sp", t, sb_ap, reads=reads, writes=["tap_" + name])

        ident = fw.sbuf("ident", [128, 128], BF16)
        identf = fw.sbuf("identf", [128, 128], F32)
        gateB = fw.sbuf("gateB", [128, D], F32)
        odT = fw.sbuf("odT", [128, 4, T_LAT], BF16)
        modT = fw.sbuf("modT", [128, 3, 8, 2], F32)
        gam = fw.sbuf("gam", [128, 8, 2], F32)
        shf = fw.sbuf("shf", [128, 8, 2], F32)
        psall = fw.psum("psall", [128, 8, 512], F32)
        pb = [psall[:, i, :] for i in range(8)]
        PB = [fw.B("pb%d" % i) for i in range(8)]

        dma("pool", ident[:], ident_d, writes=["ident"])
        dma("sp", identf[:], ident_d, writes=["identf"])

        with ExitStack() as pa:
            cf = fw.sbuf("cf", [128, 2, 8], F32, pa)
            csil = fw.sbuf("csil", [128, 2, 8], BF16, pa)
            csilB = fw.sbuf("csilB", [128, 8, 128], BF16, pa)
            bmodT = fw.sbuf("bmodT", [128, 3, 8], F32, pa)
            nwT = fw.sbuf("nwT", [128, 8], F32, pa)
            bgB = fw.sbuf("bgB", [128, D], F32, pa)
            wm = [fw.sbuf("wm%d" % i, [128, 8, 512], BF16, pa) for i in range(2)]
            with nc.allow_non_contiguous_dma(reason="small vector column layouts"):
                dma("sp", cf[:, 0, :], cvec.rearrange("(kt p) -> p kt", p=128), writes=["cf"])
                dma("sp", cf[:, 1, :], cctx.rearrange("(kt p) -> p kt", p=128), writes=["cf"], slot="cf2")
                for s in range(3):
                    dma("sp", bmodT[:, s, :], b_mod[s * D:(s + 1) * D].rearrange("(kt p) -> p kt", p=128),
                        writes=["bmodT"], slot="bmodT%d" % s)
                dma("sp", nwT[:], norm_w.rearrange("(kt p) -> p kt", p=128), writes=["nwT"])
            dma("sp", bgB[:], b_mod[2 * D:3 * D].partition_broadcast(128), writes=["bgB"])
            op("act", lambda e: e.activation(out=csil[:], in_=cf[:], func=AF.Silu), reads=["cf"], writes=["csil"])
            op("dve", lambda e: e.tensor_copy(out=csilB[:], in_=csil[:, 0, :].unsqueeze(2).to_broadcast([128, 8, 128])),
               reads=["csil"], writes=["csilB"])
            wmv = w_mod.rearrange("(kt p) n -> p kt n", p=128)
            mod_ps = pb[0][:, 0:48].rearrange("p (s k v) -> p s k v", s=3, k=8)
            for j in range(6):
                sec, half = j // 2, j % 2
                wmj = wm[j % 2]
                wname = "wm%d" % (j % 2)
                dma("pool", wmj[:], wmv[:, :, j * 512:(j + 1) * 512], writes=[wname])
                for q in range(4):
                    kto = half * 4 + q
                    for kti in range(8):
                        op("pe", (lambda q=q, kti=kti, kto=kto, sec=sec, wmj=wmj: lambda e: e.matmul(
                            mod_ps[:, sec, kto, :], lhsT=wmj[:, kti, q * 128:(q + 1) * 128], rhs=csil[:, :, kti],
                            start=(kti == 0), stop=(kti == 7)))(),
                           reads=[wname, "csil"], writes=[PB[0]], inc=(kti == 7))
                if sec == 2:
                    for kti in range(8):
                        op("pe", (lambda kti=kti, half=half, wmj=wmj: lambda e: e.matmul(
                            pb[1 + half][:], lhsT=csilB[:, kti, :], rhs=wmj[:, kti, :],
                            start=(kti == 0), stop=(kti == 7)))(),
                           reads=[wname, "csilB"], writes=[PB[1 + half]], inc=(kti == 7))
            op("dve", lambda e: e.tensor_tensor(out=modT[:], in0=mod_ps,
                                                in1=bmodT[:].unsqueeze(3).to_broadcast([128, 3, 8, 2]), op=ALU.add),
               reads=[PB[0], "bmodT"], writes=["modT"])
            op("dve", lambda e: e.tensor_scalar(out=gam[:], in0=modT[:, 1, :, :], scalar1=1.0, scalar2=None, op0=ALU.add),
               reads=["modT"], writes=["gam"])
            op("dve", lambda e: e.tensor_tensor(out=gam[:], in0=gam[:], in1=nwT[:].unsqueeze(2).to_broadcast([128, 8, 2]),
                                                op=ALU.mult), reads=["gam", "nwT"], writes=["gam"])
            op("dve", lambda e: e.tensor_copy(out=shf[:], in_=modT[:, 0, :, :]), reads=["modT"], writes=["shf"])
            for half in range(2):
                op("dve", (lambda half=half: lambda e: e.tensor_tensor(
                    out=gateB[:, half * 512:(half + 1) * 512], in0=pb[1 + half][:], in1=bgB[:, half * 512:(half + 1) * 512],
                    op=ALU.add))(), reads=[PB[1 + half], "bgB"], writes=["gateB"])
            tap("modT", modT[:], [128, 3, 8, 2], F32, ["modT"])
            tap("gateB", gateB[:], [128, D], F32, ["gateB"])
            fw.barrier()

        def phase_B(hT):
            with ExitStack() as pbk:
                xs = [fw.sbuf("xs%d" % i, [128, D], F32, pbk) for i in range(3)]
                junk = fw.sbuf("junkB", [128, D], BF16, pbk)
                xn = [fw.sbuf("xn%d" % i, [128, D], BF16, pbk) for i in range(2)]
                ssq = fw.sbuf("ssq", [128, NT], F32, pbk)
                def stats(tt):
                    xt, xtn = xs[tt % 3], "xs%d" % (tt % 3)
                    src = ctx[tt * 128:(tt + 1) * 128, :] if tt < 2 else x[(tt - 2) * 128:(tt - 1) * 128, :]
                    dma("sp", xt[:], src, writes=[xtn])
                    sc = ssq[:, tt:tt + 1]
                    op("act", lambda e: e.activation(out=junk[:], in_=xt[:], func=AF.Square, accum_out=sc), [xtn], ["junkB", "ssq%d" % tt])
                    op("dve", lambda e: e.tensor_scalar(out=sc, in0=sc, scalar1=1.0 / D, scalar2=EPS, op0=ALU.mult, op1=ALU.add),
                       ["ssq%d" % tt], ["ssq%d" % tt])
                    op("act", lambda e: e.activation(out=sc, in_=sc, func=AF.Ln), ["ssq%d" % tt], ["ssq%d" % tt])
                    op("act", lambda e: e.activation(out=sc, in_=sc, func=AF.Exp, scale=-0.5), ["ssq%d" % tt], ["ssq%d" % tt])

                stats(0)
                stats(1)
                for tt in range(NT):
                    v = 1 if tt < 2 else 0
                    xt, xtn = xs[tt % 3], "xs%d" % (tt % 3)
                    sc = ssq[:, tt:tt + 1]
                    if tt + 2 < NT:
                        stats(tt + 2)
                    xb, xbn = xn[tt % 2], "xn%d" % (tt % 2)
                    bank = 2 + (tt % 2)
                    pT = pb[bank][:].bitcast(BF16).rearrange("p (k t) -> p k t", k=8)
                    op("dve", lambda e: e.tensor_scalar(out=xb[:], in0=xt[:], scalar1=sc, scalar2=None, op0=ALU.mult),
                       [xtn, "ssq%d" % tt], [xbn])
                    for kt in range(8):
                        op("pe", lambda e: e.transpose(out=pT[:, kt, :], in_=xb[:, kt * 128:(kt + 1) * 128], identity=ident[:]),
                           [xbn, "ident"], [PB[bank]], inc=(kt == 7))
                    for kt in range(8):
                        dst = hT[:, kt, tt * 128:(tt + 1) * 128]
                        if kt % 2 == 0:
                            op("act", lambda e: e.activation(out=dst, in_=pT[:, kt, :], func=AF.Identity, scale=gam[:, kt, v:v + 1],
                                                             bias=shf[:, kt, v:v + 1]), [PB[bank], "gam", "shf"], ["hT%d" % tt])
                        else:
                            op("dve", lambda e: e.tensor_scalar(out=dst, in0=pT[:, kt, :], scalar1=gam[:, kt, v:v + 1],
                                                                scalar2=shf[:, kt, v:v + 1], op0=ALU.mult, op1=ALU.add),
                               [PB[bank], "gam", "shf"], ["hT%d" % tt])
                fw.barrier()

        def MM(out_, lhsT, rhs, reads, writes, start=True, stop=True, inc=True):
            return op("pe", lambda e: e.matmul(out_, lhsT=lhsT, rhs=rhs, start=start, stop=stop), reads, writes, inc)

        def TR(out_, in_, idn, reads, writes, inc=True):
            return op("pe", lambda e: e.transpose(out=out_, in_=in_, identity=idn), reads, writes, inc)

        def ACTF(out_, in_, func, reads, writes, **kw):
            return op("act", lambda e: e.activation(out=out_, in_=in_, func=func, **kw), reads, writes)

        def TT(en, out_, in0, in1, alu, reads, writes):
            return op(en, lambda e: e.tensor_tensor(out=out_, in0=in0, in1=in1, op=alu), reads, writes)

        def TS(en, out_, in0, s1, s2, op0, op1, reads, writes):
            if s2 is None:
                return op(en, lambda e: e.tensor_scalar(out=out_, in0=in0, scalar1=s1, scalar2=None, op0=op0), reads, writes)
            return op(en, lambda e: e.tensor_scalar(out=out_, in0=in0, scalar1=s1, scalar2=s2, op0=op0, op1=op1), reads, writes)

        def STT(en, out_, in0, scalar, in1, op0, op1, reads, writes):
            return op(en, lambda e: e.scalar_tensor_tensor(out=out_, in0=in0, scalar=scalar, in1=in1, op0=op0, op1=op1),
                      reads, writes)

        def CP(en, out_, in_, reads, writes):
            if en == "act":
                return op("act", lambda e: e.activation(out=out_, in_=in_, func=AF.Copy), reads, writes)
            return op(en, lambda e: e.tensor_copy(out=out_, in_=in_), reads, writes)

        def hT_bufs(t0, n):
            return ["hT%d" % t for t in range(t0 // 128, (t0 + n + 127) // 128)]

        winv = w_in.rearrange("(kt p) n -> p kt n", p=128)
        GROUPS = [(0, 256)] + [(256 + 512 * i, 512) for i in range(4)]
        evac_rr = [0]

        EV_N = int(os.environ.get("EV_N", "3"))
        EV_A = int(os.environ.get("EV_A", "2"))

        def evac_copy(out_, in_, reads, writes):
            evac_rr[0] += 1
            return CP("act" if (evac_rr[0] % EV_N) < EV_A else "dve", out_, in_, reads, writes)


        dn_era = ExitStack()
        dnqT = fw.sbuf("dnqT", [128, 4, T_ALL], BF16, dn_era)
        dnkT = fw.sbuf("dnkT", [128, 4, T_ALL], BF16, dn_era)
        k_tok = fw.sbuf("k_tok", [128, NT, 8, 64], BF16, dn_era)
        v_tok = fw.sbuf("v_tok", [128, NT, 8, 64], BF16, dn_era)
        beta = fw.sbuf("beta", [128, NT, 16], F32, dn_era)
        gdec = fw.sbuf("gdec", [128, NT, 16], F32, dn_era)
        tri = fw.sbuf("tri", [128, 6, 128], F32, dn_era)
        onesbd = fw.sbuf("onesbd", [128, 128], BF16, dn_era)
        negoff = fw.sbuf("negoff", [128, 128], F32, dn_era)
        hm = fw.sbuf("hm", [128, 9, 128], BF16, dn_era)
        dma("sp", tri[:], tri_d.rearrange("k p i -> p k i"), writes=["tri"])
        dma("pool", onesbd[:], tri_d[5], writes=["onesbd"])
        dma("sp", negoff[:], negoff_d, writes=["negoff"])
        dma("pool", hm[:], hmask_d.rearrange("k p i -> p k i"), writes=["hm"])
        h_era = ExitStack()
        hT = fw.sbuf("hT", [128, 8, T_ALL], BF16, h_era)
        phase_B(hT)
        tap("hT", hT[:], [128, 8, T_ALL], BF16, ["hT%d" % t for t in range(NT)])
        hT_spill = nc.dram_tensor("hT_spill", [128, 8, T_ALL], BF16).ap()
        for kt in range(8):
            dma("sp", hT_spill[:, kt, :], hT[:, kt, :], reads=["hT%d" % t for t in range(NT)], writes=["hT_spill"], slot="spill%d" % kt)

        with ExitStack() as c1:
            wdnb = [fw.sbuf("wdn%d" % i, [128, 8, 512], BF16, c1) for i in range(2)]
            wbg = fw.sbuf("wbg", [128, 8, 32], BF16, c1)
            convT = fw.sbuf("convT", [128, 12, 5], F32, c1)
            diagwb = [fw.sbuf("diagw%d" % i, [128, 5, 128], BF16, c1) for i in range(2)]
            xc = [fw.sbuf("xc%d" % i, [128, 2312], BF16, c1) for i in range(2)]
            ys = [fw.sbuf("ys%d" % i, [128, T_ALL], F32, c1) for i in range(2)]
            sqS = [fw.sbuf("sq%d" % i, [128, T_ALL], BF16, c1) for i in range(2)]
            rsS = [fw.sbuf("rs%d" % i, [128, 512], F32, c1) for i in range(2)]
            zz = fw.sbuf("zz", [128, NT, 16], F32, c1)
            zm_ = fw.sbuf("zm_", [128, NT, 16], F32, c1)
            ze = fw.sbuf("ze", [128, NT, 16], F32, c1)
            dtbB = fw.sbuf("dtbB", [128, 16], F32, c1)
            negA = fw.sbuf("negA", [128, 16], F32, c1)
            wbg_n = fw.sbuf("wbg_n", [128, 8, 32], BF16, c1)
            dma("pool", wbg_n[:], winv[:, :, 3232:3264], writes=["wbg_n"])
            CP("dve", wbg[:].rearrange("p k (a par hp) -> p (k a) par hp", par=2, hp=4),
               wbg_n[:].rearrange("p k (a hp par) -> p (k a) par hp", par=2, hp=4), ["wbg_n"], ["wbg"])
            with nc.allow_non_contiguous_dma(reason="small conv weight column layout"):
                for j in range(5):
                    dma("sp", convT[:, :, j], conv_w[j, :].rearrange("(c p) -> p c", p=128), writes=["convT"], slot="convT%d" % j)
            dtb_n = fw.sbuf("dtb_n", [128, 16], F32, c1)
            alog_n = fw.sbuf("alog_n", [128, 16], F32, c1)
            dma("sp", dtb_n[:], dt_bias.partition_broadcast(128), writes=["dtb_n"])
            dma("sp", alog_n[:], a_log.partition_broadcast(128), writes=["alog_n"])
            CP("dve", dtbB[:].rearrange("p (a par hp) -> p a par hp", par=2, hp=4),
               dtb_n[:].rearrange("p (a hp par) -> p a par hp", par=2, hp=4), ["dtb_n"], ["dtbB"])
            CP("dve", negA[:].rearrange("p (a par hp) -> p a par hp", par=2, hp=4),
               alog_n[:].rearrange("p (a hp par) -> p a par hp", par=2, hp=4), ["alog_n"], ["negA"])
            for i in range(2):
                for lo, hi in ((0, 2), (258, 262), (2310, 2312)):
                    op("pool", lambda e: e.memset(xc[i][:, lo:hi], 0.0), (), ["xc%d" % i])
            zps = psall[:, 4:6, 0:288].rearrange("p b (t c) -> p b t c", c=32)
            for tt in range(NT):
                for kt in range(8):
                    MM(zps[:, tt // 9, tt % 9, :], hT[:, kt, tt * 128:(tt + 1) * 128], wbg[:, kt, :], ["hT%d" % tt, "wbg"],
                       [PB[4 + tt // 9]], start=(kt == 0), stop=(kt == 7), inc=(kt == 7))
            v4 = lambda t: t[:].rearrange("p (b t) c -> p b t c", b=2)
            ACTF(v4(beta), zps[:, :, :, 0:16], AF.Sigmoid, [PB[4], PB[5]], ["beta"])
            TT("dve", v4(zz), zps[:, :, :, 16:32], dtbB[:].unsqueeze(1).unsqueeze(1).to_broadcast([128, 2, 9, 16]), ALU.add,
               [PB[4], PB[5], "dtbB"], ["zz"])
            TS("dve", zm_[:], zz[:], 0.0, None, ALU.max, None, ["zz"], ["zm_"])
            STT("dve", ze[:], zm_[:], -2.0, zz[:], ALU.mult, ALU.add, ["zm_", "zz"], ["ze"])
            ACTF(ze[:], ze[:], AF.Exp, ["ze"], ["ze"])
            ACTF(ze[:], ze[:], AF.Ln, ["ze"], ["ze"], bias=1.0)
            ACTF(negA[:], negA[:], AF.Exp, ["negA"], ["negA"])
            TS("dve", negA[:], negA[:], -1.0, None, ALU.mult, None, ["negA"], ["negA"])
            TT("dve", ze[:], ze[:], zm_[:], ALU.add, ["ze", "zm_"], ["ze"])
            TT("dve", gdec[:], ze[:], negA[:].unsqueeze(1).to_broadcast([128, NT, 16]), ALU.mult, ["ze", "negA"], ["gdec"])
            def chunk_stream(c):
                which, hc = c // 4, c % 4
                par = c % 2
                xcb, xcn = xc[par], "xc%d" % par
                ysc, ysn = ys[par], "ys%d" % par
                sqc, sqn = sqS[par], "sq%d" % par
                wdn, wn = wdnb[which % 2], "wdn%d" % (which % 2)
                if hc == 0:
                    dma("pool", wdn[:], winv[:, :, 1184 + which * 512:1184 + (which + 1) * 512], writes=[wn])
                diagw, dgn = diagwb[par], "diagw%d" % par
                for j in range(5):
                    TS("dve" if j % 2 else "pool", diagw[:, j, :], identf[:], convT[:, c, j:j + 1], None, ALU.mult, None,
                       ["identf", "convT"], [dgn])
                for gi, (t0, n) in enumerate(GROUPS):
                    bk = gi % 2
                    col = t0 + 2 if t0 < 256 else t0 + 6
                    for kt in range(8):
                        MM(pb[bk][:, 0:n], wdn[:, kt, hc * 128:(hc + 1) * 128], hT[:, kt, t0:t0 + n], [wn] + hT_bufs(t0, n), [PB[bk]],
                           start=(kt == 0), stop=(kt == 7), inc=(kt == 7))
                    evac_copy(xcb[:, col:col + n], pb[bk][:, 0:n], [PB[bk]], [xcn])
                    if gi in (1, 3):
                        yield
                yield
                for gi, (t0, n) in enumerate(GROUPS):
                    bk = 2 + gi % 2
                    col = t0 + 2 if t0 < 256 else t0 + 6
                    for j in range(5):
                        MM(pb[bk][:, 0:n], diagw[:, j, :], xcb[:, col + j - 2:col + j - 2 + n], [dgn, xcn], [PB[bk]],
                           start=(j == 0), stop=(j == 4), inc=(j == 4))
                    if which == 2:
                        ACTF(sqc[:, t0:t0 + n], pb[bk][:, 0:n], AF.Silu, [PB[bk]], [sqn])
                    else:
                        ACTF(ysc[:, t0:t0 + n], pb[bk][:, 0:n], AF.Silu, [PB[bk]], [ysn])
                yield
                if which < 2:
                    TT("pool", sqc[:], ysc[:], ysc[:], ALU.mult, [ysn], [sqn])
                    yield
                    dst = (dnqT if which == 0 else dnkT)
                    dn_ = ("dnqT%d" if which == 0 else "dnkT%d") % hc
                    for gi, (t0, n) in enumerate(GROUPS):
                        bk = 4 + gi % 2
                        rsg, rsn = rsS[gi % 2], "rs%d" % (gi % 2)
                        MM(pb[bk][:, 0:n], onesbd[:], sqc[:, t0:t0 + n], ["onesbd", sqn], [PB[bk]])
                        ACTF(rsg[:, 0:n], pb[bk][:, 0:n], AF.Ln, [PB[bk]], [rsn], bias=EPS)
                        ACTF(rsg[:, 0:n], rsg[:, 0:n], AF.Exp, [rsn], [rsn], scale=-0.5)
                        if which == 0:
                            STT("dve", dst[:, hc, t0:t0 + n], ysc[:, t0:t0 + n], 0.125, rsg[:, 0:n], ALU.mult, ALU.mult, [ysn, rsn], [dn_])
                        else:
                            TT("dve", dst[:, hc, t0:t0 + n], ysc[:, t0:t0 + n], rsg[:, 0:n], ALU.mult, [ysn, rsn], [dn_])
                    yield
                if which >= 1:
                    src = dnkT[:, hc, :] if which == 1 else sqc[:]
                    srcn = ("dnkT%d" % hc) if which == 1 else sqn
                    dtok = k_tok if which == 1 else v_tok
                    dtn = "k_tok" if which == 1 else "v_tok"
                    for bi, (tt0, ntl) in enumerate(((0, 8), (8, 8), (16, 2))):
                        bk = 6 + bi % 2
                        pT = pb[bk][:].bitcast(BF16).rearrange("p (k t) -> p k t", k=8)
                        for i in range(ntl):
                            tt = tt0 + i
                            TR(pT[:, i, :], src[:, tt * 128:(tt + 1) * 128], ident[:], [srcn, "ident"], [PB[bk]], inc=(i == ntl - 1))
                        evac_copy(dtok[:, tt0:tt0 + ntl, 2 * hc:2 * hc + 2, :].rearrange("p t h d -> p t (h d)"), pT[:, 0:ntl, :],
                                  [PB[bk]], [dtn])
                    yield

            NCH = int(os.environ.get("C1_NCH", "12"))
            active = []
            nxt_c = 0
            while nxt_c < NCH or active:
                if nxt_c < NCH and len(active) < 2:
                    active.append(chunk_stream(nxt_c))
                    nxt_c += 1
                for g in list(active):
                    try:
                        next(g)
                    except StopIteration:
                        active.remove(g)
            tap("dnqT", dnqT[:], [128, 4, T_ALL], BF16, ["dnqT%d" % i for i in range(4)])
            tap("dnkT", dnkT[:], [128, 4, T_ALL], BF16, ["dnkT%d" % i for i in range(4)])
            tap("k_tok", k_tok[:], [128, NT, 8, 64], BF16, ["k_tok"])
            tap("v_tok", v_tok[:], [128, NT, 8, 64], BF16, ["v_tok"])
            tap("beta", beta[:], [128, NT, 16], F32, ["beta"])
            tap("gdec", gdec[:], [128, NT, 16], F32, ["gdec"])
            fw.barrier()
        h_era.close()
        if stop_after == "C1":
            dn_era.close()

        if stop_after != "C1":
            with ExitStack() as dd:
                A_ = lambda name, shape, dt: fw.sbuf(name, shape, dt, dd)
                rg = A_("rgEs", [128, 8, 128], F32)
                Es = rg
                DT = A_("DT", [128, 8, 128], F32)
                E_ = A_("E_", [128, 8, 128], F32)
                AqkS = [A_("Aqk%d" % i, [128, 8, 128], BF16) for i in range(3)]
                R0S = [A_("R0b%d" % i, [128, 8, 128], BF16) for i in range(2)]
                P0S = [A_("P0b%d" % i, [128, 8, 128], BF16) for i in range(2)]
                BA = [[A_("BA%d%d" % (w, i), [128, 4, 2, 128], BF16) for i in range(2)] for w in range(2)]
                BB = [[A_("BB%d%d" % (w, i), [128, 4, 2, 128], BF16) for i in range(2)] for w in range(2)]
                Wb = [[A_("Wb%d%d" % (w, i), [128, 4, 128], BF16) for i in range(2)] for w in range(2)]
                Db = [[A_("Db%d%d" % (w, i), [128, 4, 128], BF16) for i in range(2)] for w in range(2)]
                Yb = [A_("Yb%d" % w, [128, 4, 128], BF16) for w in range(2)]
                Ypb = [A_("Ypb%d" % w, [128, 4, 128], BF16) for w in range(2)]
                Cmb = [A_("Cmb%d" % i, [128, 8, 128], BF16) for i in range(2)]
                Cfb = [A_("Cfb%d" % i, [128, 8, 128], BF16) for i in range(2)]
                XTS = [A_("XT%d" % i, [128, 8, 128], BF16) for i in range(2)]
                kg = A_("kg", [128, 8, 64], BF16)
                kdb = A_("kdb", [128, 8, 64], BF16)
                up = A_("up", [128, 8, 64], F32)
                wT = A_("wT", [128, 4, 128], BF16)
                vt = A_("vt", [128, 8, 64], BF16)
                S32 = A_("S32", [128, 4, 64], F32)
                Sbf = A_("Sbf", [128, 4, 64], BF16)
                egS = [A_("eg%d" % i, [128, 16], F32) for i in range(3)]
                egl2S = [A_("egl2%d" % i, [128, 4], F32) for i in range(3)]
                eb = A_("eb", [128, 8], F32)
                otmp = A_("otmp", [128, 8, 64], F32)
                ofin = A_("ofin", [128, 8, 64], F32)
                onb = A_("onb", [128, 8, 64], BF16)
                oss = A_("oss", [128, 8], F32)
                dnwB = A_("dnwB", [128, 64], F32)
                o_acc = A_("o_acc", [128, 16, 8, 64], BF16)
                dma("sp", dnwB[:], dn_w.partition_broadcast(128), writes=["dnwB"])
                LE, LT_, GE, GT_, ONES = (tri[:, i, :] for i in range(5))

                def bc_h(m):
                    return m.unsqueeze(1).to_broadcast([128, 8, 128])

                def bc_i(v, n):
                    return v.unsqueeze(2).to_broadcast([128, 8, n])

                ps2 = lambda b0: psall[:, b0:b0 + 2, :].rearrange("p b (i c) -> p (b i) c", c=128)
                v4q = lambda t: t.rearrange("p (par hp) e -> p par hp e", par=2)
                bc4 = lambda v, n: v.rearrange("p (par hp) -> p par hp", par=2).unsqueeze(3).to_broadcast([128, 2, 4, n])
                bcm = lambda m_: m_.unsqueeze(1).to_broadcast([128, 4, 128])
                A4 = lambda bk: pb[bk].rearrange("p (i c) -> p i c", c=128)
                A8 = lambda bk: psall[:, bk:bk + 2, :].rearrange("p b (i c) -> p (b i) c", c=256)

                DN_NT = int(os.environ.get("DN_NT", "18"))

                def stream_P(d, n, tt):
                    (Mc, Mrest, Mr, Ml, Mi) = (LE, GT_, LE, GT_, LE) if d == 0 else (GE, LT_, GE, LT_, GE)
                    lat = tt >= 2
                    tok = slice(tt * 128, (tt + 1) * 128)
                    g_t = gdec[:, tt, d * 8:(d + 1) * 8]
                    b_t = beta[:, tt, d * 8:(d + 1) * 8]
                    eg, egn = egS[n % 3], "eg%d" % (n % 3)
                    egl2, egl2n = egl2S[n % 3], "egl2%d" % (n % 3)
                    Aqk, Aqkn = AqkS[n % 3], "Aqk%d" % (n % 3)
                    R0b, R0n = R0S[n % 2], "R0b%d" % (n % 2)
                    P0b, P0n = P0S[n % 2], "P0b%d" % (n % 2)
                    small = pb[7]
                    MM(small[:, 0:8], Mc, g_t, ["tri", "gdec"], [PB[7]], inc=False)
                    MM(small[:, 8:16], Mrest, g_t, ["tri", "gdec"], [PB[7]], inc=False)
                    MM(small[:, 16:24], ONES, g_t, ["tri", "gdec"], [PB[7]])
                    ACTF(eg[:], small[:, 0:16], AF.Exp, [PB[7]], [egn])
                    ACTF(egl2[0:64, :], small[0:64, 16:20], AF.Exp, [PB[7]], [egl2n])
                    ACTF(egl2[64:128, :], small[64:128, 20:24], AF.Exp, [PB[7]], [egl2n])
                    TT("pool", rg[:], bc_h(Mr), bc_i(g_t, 128), ALU.mult, ["tri", "gdec"], ["rgEs"])
                    yield
                    for hh in range(2):
                        MM(pb[4 + hh][:], Ml, rg[:, hh * 4:(hh + 1) * 4, :].rearrange("p h i -> p (h i)"), ["tri", "rgEs"], [PB[4 + hh]])
                    ACTF(DT[:], ps2(4), AF.Exp, [PB[4], PB[5]], ["DT"])
                    TT("dve", DT[:], DT[:], bc_h(Mi), ALU.mult, ["DT", "tri"], ["DT"])
                    yield
                    TT("dve", E_[:], DT[:], bc_i(b_t, 128), ALU.mult, ["DT", "beta"], ["E_"])
                    TT("pool", Es[:], E_[:], bc_h(negoff[:]), ALU.mult, ["E_", "negoff"], ["rgEs"])
                    yield
                    for h in range(8):
                        pr, hp = (h % 2) * 64, h // 2
                        kT_h = dnkT[pr:pr + 64, hp, tok]
                        MM(psall[:, 4 + h % 2, hp * 128:(hp + 1) * 128], kT_h, kT_h, ["dnkT%d" % hp], [PB[4 + h % 2]], inc=(h >= 6))
                    if lat:
                        for h in range(8):
                            pr, hp = (h % 2) * 64, h // 2
                            MM(psall[:, 6 + h % 2, hp * 128:(hp + 1) * 128], dnkT[pr:pr + 64, hp, tok],
                               dnqT[pr:pr + 64, hp, tok], ["dnkT%d" % hp, "dnqT%d" % hp], [PB[6 + h % 2]], inc=(h >= 6))
                    TT("dve", R0b[:], ps2(4), Es[:], ALU.mult, [PB[4], PB[5], "rgEs"], [R0n])
                    if lat:
                        TT("dve", Aqk[:], ps2(6), E_[:], ALU.mult, [PB[6], PB[7], "E_"], [Aqkn])
                    yield
                    pT = pb[6][:].bitcast(BF16).rearrange("p (k t) -> p k t", k=8)
                    for q in range(8):
                        TR(pT[:, q, :], R0b[:, q, :], ident[:], [R0n, "ident"], [PB[6]], inc=(q == 7))
                    CP("act", P0b[:], pT, [PB[6]], [P0n])
                    yield

                def stream_I(d, n, tt):
                    R0b, R0n = R0S[n % 2], "R0b%d" % (n % 2)
                    P0b, P0n = P0S[n % 2], "P0b%d" % (n % 2)
                    XT, XTn = XTS[n % 2], "XT%d" % (n % 2)
                    for w in range(2):
                        sl = slice(w * 4, (w + 1) * 4)
                        TT(os.environ.get("BA_ENG", "pool"), BA[w][0][:, :, 0, :], R0b[:, sl, :], bcm(hm[:, 0, :]), ALU.mult, [R0n, "hm"], ["BA%d0" % w])
                        TT(os.environ.get("BB_ENG", "dve"), BB[w][0][:, :, 0, :], P0b[:, sl, :], bcm(hm[:, 0, :]), ALU.mult, [P0n, "hm"], ["BB%d0" % w])
                        TT(os.environ.get("BA_ENG", "pool"), BA[w][1][:, :, 1, :], BA[w][0][:, :, 0, :], bcm(ident[:]), ALU.add, ["BA%d0" % w, "ident"], ["BA%d1" % w])
                    yield
                    CmAll = Cmb + Cfb
                    for li in range(4):
                        mnat = hm[:, (1 + li) if d == 0 else (5 + li), :]
                        TT("pool", CmAll[li][:], P0b[:], bc_h(mnat), ALU.mult, [P0n, "hm"], ["CmAll%d" % li])
                    for w in range(2):
                        b0 = 2 * w
                        for i in range(4):
                            MM(A4(b0)[:, i, :], BA[w][0][:, i, 0, :], BB[w][0][:, i, 0, :], ["BA%d0" % w, "BB%d0" % w], [PB[b0]], inc=False)
                            MM(A4(b0 + 1)[:, i, :], BB[w][0][:, i, 0, :], BA[w][0][:, i, 0, :], ["BA%d0" % w, "BB%d0" % w], [PB[b0 + 1]],
                               inc=(i == 3))
                        evac_copy(BB[w][1][:, :, 0, :], A4(b0), [PB[b0]], ["BB%d1" % w])
                        evac_copy(BA[w][1][:, :, 0, :], A4(b0 + 1), [PB[b0 + 1]], ["BA%d1" % w])
                    yield
                    for w in range(2):
                        b0 = 2 * w
                        for i in range(4):
                            MM(A8(b0)[:, i, :], BB[w][1][:, i, 0, :], BA[w][1][:, i, :, :].rearrange("p a c -> p (a c)"),
                               ["BA%d1" % w, "BB%d1" % w], [PB[b0 + i // 2]], start=True, stop=False, inc=False)
                            MM(A8(b0)[:, i, 128:256], ident[:], BA[w][1][:, i, 1, :], ["ident", "BA%d1" % w], [PB[b0 + i // 2]],
                               start=False, stop=True, inc=(i == 3))
                        evac_copy(BA[w][0][:].rearrange("p i a c -> p i (a c)"), A8(b0), [PB[b0], PB[b0 + 1]], ["BA%d0" % w])
                    for w in range(2):
                        b0 = 4 + 2 * w
                        for i in range(4):
                            MM(A4(b0)[:, i, :], BA[w][1][:, i, 0, :], BB[w][1][:, i, 0, :], ["BA%d1" % w, "BB%d1" % w], [PB[b0]], inc=(i == 3))
                        evac_copy(BB[w][0][:, :, 0, :], A4(b0), [PB[b0]], ["BB%d0" % w])
                    yield
                    for w in range(2):
                        b0 = 2 * w
                        for i in range(4):
                            MM(A4(b0)[:, i, :], BB[w][0][:, i, 0, :], BA[w][0][:, i, 1, :], ["BA%d0" % w, "BB%d0" % w], [PB[b0]],
                               start=True, stop=False, inc=False)
                            MM(A4(b0)[:, i, :], ident[:], BA[w][0][:, i, 1, :], ["ident", "BA%d0" % w], [PB[b0]], start=False, stop=True,
                               inc=(i == 3))
                        evac_copy(Wb[w][0][:], A4(b0), [PB[b0]], ["Wb%d0" % w])
                    yield
                    for li in range(4):
                        cur, nxt = li % 2, (li + 1) % 2
                        last = (li == 3)
                        Cm, Cmn = CmAll[li], "CmAll%d" % li
                        for w in range(2):
                            b0 = 2 * w
                            Wc, Wcn, Dc, Dcn = Wb[w][cur], "Wb%d%d" % (w, cur), Db[w][cur], "Db%d%d" % (w, cur)
                            pTw = pb[b0 + 1][:].bitcast(BF16).rearrange("p (k t) -> p k t", k=8)
                            for i in range(4):
                                q = w * 4 + i
                                MM(A4(b0)[:, i, :], Cm[:, q, :], Wc[:, i, :], [Cmn, Wcn], [PB[b0]], inc=False)
                                TR(pTw[:, i, :], Wc[:, i, :], ident[:], [Wcn, "ident"], [PB[b0 + 1]], inc=(i == 3))
                            evac_copy(Yb[w][:], A4(b0), [PB[b0]], ["Yb%d" % w])
                            evac_copy(Dc[:], pTw[:, 0:4, :], [PB[b0 + 1]], [Dcn])
                        yield
                        for w in range(2):
                            b0 = 2 * w
                            Wc, Wcn, Dc, Dcn = Wb[w][cur], "Wb%d%d" % (w, cur), Db[w][cur], "Db%d%d" % (w, cur)
                            for i in range(4):
                                MM(A4(b0)[:, i, :], Dc[:, i, :], Yb[w][:, i, :], [Dcn, "Yb%d" % w], [PB[b0]], start=True, stop=False, inc=False)
                                MM(A4(b0)[:, i, :], ident[:], Wc[:, i, :], ["ident", Wcn], [PB[b0]], start=False, stop=True, inc=(i == 3))
                            if last:
                                evac_copy(XT[:, w * 4:(w + 1) * 4, :], A4(b0), [PB[b0]], [XTn])
                            else:
                                evac_copy(Wb[w][nxt][:], A4(b0), [PB[b0]], ["Wb%d%d" % (w, nxt)])
                        yield

                def stream_C(d, n, tt):
                    lat = tt >= 2
                    lt = tt - 2
                    tok = slice(tt * 128, (tt + 1) * 128)
                    b_t = beta[:, tt, d * 8:(d + 1) * 8]
                    eg, egn = egS[n % 3], "eg%d" % (n % 3)
                    egl2, egl2n = egl2S[n % 3], "egl2%d" % (n % 3)
                    Aqk, Aqkn = AqkS[n % 3], "Aqk%d" % (n % 3)
                    XT, XTn = XTS[n % 2], "XT%d" % (n % 2)
                    ktn = k_tok[:, tt, :, :].rearrange("p (hp par) e -> p par hp e", par=2)
                    TT("pool", v4q(kg[:]), ktn, bc4(eg[:, 0:8], 64), ALU.mult, ["k_tok", egn], ["kg"])
                    TT("pool", eb[:], eg[:, 8:16], b_t, ALU.mult, [egn, "beta"], ["eb"])
                    TT("pool", v4q(kdb[:]), ktn, bc4(eb[:], 64), ALU.mult, ["k_tok", "eb"], ["kdb"])
                    up_ps = pb[4].rearrange("p (q e) -> p q e", e=64)
                    for h in range(8):
                        q = (h % 2) * 4 + h // 2
                        MM(up_ps[:, q, :], XT[:, q, :], v_tok[:, tt, h, :], [XTn, "v_tok"], [PB[4]], inc=(h == 7))
                    for h in range(8):
                        par, hp = h % 2, h // 2
                        q = par * 4 + hp
                        MM(psall[par * 64:(par + 1) * 64, 5, hp * 128:(hp + 1) * 128], kg[:, q, :], XT[:, q, :], ["kg", XTn],
                           [PB[5]], inc=(h == 7))
                    CP("act", up[:], up_ps, [PB[4]], ["up"])
                    CP("act", wT[:], pb[5].rearrange("p (a i) -> p a i", i=128), [PB[5]], ["wT"])
                    yield
                    wS_ps = psall[:, 6:8, 0:256].rearrange("p b (hp e) -> p b hp e", e=64)
                    for h in range(8):
                        par, hp = h % 2, h // 2
                        pr = par * 64
                        MM(wS_ps[:, par, hp, :], wT[pr:pr + 64, hp, :], Sbf[pr:pr + 64, hp, :], ["wT", "Sbf"], [PB[6 + par]], inc=(h >= 6))
                    TT("dve", v4q(vt[:]), v4q(up[:]), wS_ps, ALU.subtract, ["up", PB[6], PB[7]], ["vt"])
                    yield
                    if lat:
                        qS_ps = psall[:, 4:6, 0:256].rearrange("p b (hp e) -> p b hp e", e=64)
                        Av_ps = pb[6].rearrange("p (q e) -> p q e", e=64)
                        for h in range(8):
                            par, hp = h % 2, h // 2
                            pr = par * 64
                            MM(qS_ps[:, par, hp, :], dnqT[pr:pr + 64, hp, tok], Sbf[pr:pr + 64, hp, :], ["dnqT%d" % hp, "Sbf"],
                               [PB[4 + par]], inc=(h >= 6))
                        for q in range(8):
                            MM(Av_ps[:, q, :], Aqk[:, q, :], vt[:, q, :], [Aqkn, "vt"], [PB[6]], inc=(q == 7))
                    for h in range(8):
                        par, hp = h % 2, h // 2
                        q = par * 4 + hp
                        MM(psall[par * 64:(par + 1) * 64, 7, hp * 64:(hp + 1) * 64], kdb[:, q, :], vt[:, q, :], ["kdb", "vt"],
                           [PB[7]], inc=(h == 7))
                    TT("dve", S32[:], S32[:], egl2[:].unsqueeze(2).to_broadcast([128, 4, 64]), ALU.mult, ["S32", egl2n], ["S32"])
                    TT("dve", S32[:], S32[:], pb[7][:, 0:256].rearrange("p (a e) -> p a e", e=64), ALU.add, ["S32", PB[7]], ["S32"])
                    CP("act", Sbf[:], S32[:], ["S32"], ["Sbf"])
                    if lat:
                        TT("dve", v4q(otmp[:]), qS_ps, bc4(eg[:, 0:8], 64), ALU.mult, [PB[4], PB[5], egn], ["otmp"])
                        if d == 0:
                            TT("dve", o_acc[:, lt, :, :], otmp[:], Av_ps, ALU.add, ["otmp", PB[6]], ["o_acc%d" % lt])
                        else:
                            TT("dve", ofin[:], otmp[:], Av_ps, ALU.add, ["otmp", PB[6]], ["ofin"])
                    if os.environ.get("DN_TAP") == "%d,%d" % (d, tt):
                        tap("eg", eg[:], [128, 16], F32, [egn])
                        tap("XT", XT[:], [128, 8, 128], BF16, [XTn])
                        tap("up", up[:], [128, 8, 64], F32, ["up"])
                        tap("wT", wT[:], [128, 4, 128], BF16, ["wT"])
                        tap("vt", vt[:], [128, 8, 64], BF16, ["vt"])
                        tap("S32", S32[:], [128, 4, 64], F32, ["S32"])
                        tap("kg", kg[:], [128, 8, 64], BF16, ["kg"])
                        tap("R0", R0S[n % 2][:], [128, 8, 128], BF16, ["R0b%d" % (n % 2)])
                    yield
                    if lat:
                        if d == 1:
                            TT("pool", ofin[:], ofin[:], o_acc[:, lt, :, :], ALU.add, ["ofin", "o_acc%d" % lt], ["ofin"])
                            TT("pool", otmp[:], ofin[:], ofin[:], ALU.mult, ["ofin"], ["otmp"])
                            op("dve", lambda e: e.tensor_reduce(out=oss[:], in_=otmp[:], axis=AX.X, op=ALU.add), ["otmp"], ["oss"])
                            TS("dve", oss[:], oss[:], 1.0 / 64, EPS, ALU.mult, ALU.add, ["oss"], ["oss"])
                            ACTF(oss[:], oss[:], AF.Ln, ["oss"], ["oss"])
                            ACTF(oss[:], oss[:], AF.Exp, ["oss"], ["oss"], scale=-0.5)
                            TT("dve", ofin[:], ofin[:], bc_i(oss[:], 64), ALU.mult, ["ofin", "oss"], ["ofin"])
                            TT("dve", onb[:].rearrange("p (hp par) e -> p par hp e", par=2), v4q(ofin[:]),
                               dnwB[:].unsqueeze(1).unsqueeze(1).to_broadcast([128, 2, 4, 64]), ALU.mult, ["ofin", "dnwB"], ["onb"])
                            yield
                            pT = pb[5][:].bitcast(BF16).rearrange("p (k t) -> p k t", k=8)
                            onv = onb[:].rearrange("p (a b) e -> p a (b e)", b=2)
                            for hp in range(4):
                                TR(pT[:, 4 + hp, :], onv[:, hp, :], ident[:], ["onb", "ident"], [PB[5]], inc=(hp == 3))
                            CP("act", odT[:, :, lt * 128:(lt + 1) * 128], pT[:, 4:8, :], [PB[5]], ["odT"])
                    yield

                def run_streams(gens, periods=None):
                    items = [(g, (periods[k] if periods else 1)) for k, g in enumerate(gens) if g is not None]
                    rnd = 0
                    while items:
                        for it in list(items):
                            g, per = it
                            if rnd % per:
                                continue
                            try:
                                next(g)
                            except StopIteration:
                                items.remove(it)
                        rnd += 1

                for d in range(int(os.environ.get("DN_PASSES", "2"))):
                    order = list(range(NT)) if d == 0 else [1, 0] + list(range(NT - 1, 1, -1))
                    order = order[:DN_NT]
                    op("pool", lambda e: e.memset(S32[:], 0.0), (), ["S32"])
                    op("pool", lambda e: e.memset(Sbf[:], 0.0), (), ["Sbf"])
                    nn = len(order)
                    run_streams([stream_P(d, 0, order[0])])
                    for n in range(nn):
                        run_streams([stream_I(d, n, order[n]),
                                     stream_P(d, n + 1, order[n + 1]) if n + 1 < nn else None,
                                     stream_C(d, n - 1, order[n - 1]) if n >= 1 else None],
                                    periods=[int(os.environ.get("PER_I", "1")), int(os.environ.get("PER_P", "1")), int(os.environ.get("PER_C", "1"))])
                    run_streams([stream_C(d, nn - 1, order[nn - 1])])
                tap("odT", odT[:], [128, 4, T_LAT], BF16, ["odT"])
                fw.barrier()
            dn_era.close()

        if stop_after not in ("C1", "D", "A"):
            omT = fw.sbuf("omT", [128, 4, T_LAT], BF16)
            hT = fw.sbuf("hT2", [128, 8, T_ALL], BF16)
            for kt in range(8):
                dma("sp", hT[:, kt, :], hT_spill[:, kt, :], reads=["hT_spill"], writes=["hT%d" % t for t in range(NT)], slot="fill%d" % kt)
            mla_era = ExitStack()
            qT_all = fw.sbuf("qT_all", [128, 8, T_LAT], BF16, mla_era)
            kT_all = fw.sbuf("kT_all", [128, 8, T_ALL], BF16, mla_era)
            V_all = fw.sbuf("V_all", [128, NT, 8, 65], BF16, mla_era)
            negC = fw.sbuf("negC", [128, 1], F32, mla_era)
            with ExitStack() as c2:
                A_ = lambda name, shape, dt: fw.sbuf(name, shape, dt, c2)
                wtok = A_("wtok", [128, 8, 672], BF16)
                wuq = A_("wuq", [128, 3, 768], BF16)
                wukv = A_("wukv", [128, 2, 1024], BF16)
                qnwT = A_("qnwT", [128, 3], F32)
                kvnwT = A_("kvnwT", [128, 2], F32)
                qhwB = A_("qhwB", [128, 96], F32)
                khwB = A_("khwB", [128, 96], F32)
                cosT = A_("cosT", [128, 16, 16], F32)
                sinT = A_("sinT", [128, 16, 16], F32)
                invn = A_("invn", [128, 2], F32)
                ssA = A_("ssA", [128, 4], F32)
                rs2 = A_("rs2", [128, 2], F32)
                junk2 = A_("junk2", [128, 384], BF16)
                cqn = A_("cqn", [128, 384], BF16)
                ckvn = A_("ckvn", [128, 256], BF16)
                cqnT = A_("cqnT", [128, 3, 128], BF16)
                ckvnT = A_("ckvnT", [128, 2, 128], BF16)
                sqq = A_("sqq", [128, 8, 96], F32)
                ss16 = A_("ss16", [128, 16], F32)
                q_fin = A_("q_fin", [128, 8, 96], BF16)
                k_fin = A_("k_fin", [128, 8, 96], BF16)
                tl = A_("tl", [128, 8, 32], F32)
                ra = A_("ra", [128, 8, 2, 8], F32)
                rb = A_("rb", [128, 8, 2, 8], F32)
                cmx = A_("cmx", [128, 4], F32)
                dma("pool", wtok[:], winv[:, :, 0:672], writes=["wtok"])
                dma("pool", wuq[:], w_uq.rearrange("(kt p) n -> p kt n", p=128), writes=["wuq"])
                dma("pool", wukv[:], w_ukv.rearrange("(kt p) n -> p kt n", p=128), writes=["wukv"])
                with nc.allow_non_contiguous_dma(reason="small vector column layouts"):
                    dma("sp", qnwT[:], q_norm_w.rearrange("(kt p) -> p kt", p=128), writes=["qnwT"])
                    dma("sp", kvnwT[:], kv_norm_w.rearrange("(kt p) -> p kt", p=128), writes=["kvnwT"])
                    dma("sp", cosT[:], cos_d.rearrange("(t p) c -> p t c", p=128), writes=["cosT"])
                    dma("sp", sinT[:], sin_d.rearrange("(t p) c -> p t c", p=128), writes=["sinT"])
                dma("sp", qhwB[:], qh_w.partition_broadcast(128), writes=["qhwB"])
                dma("sp", khwB[:], kh_w.partition_broadcast(128), writes=["khwB"])
                op("pool", lambda e: e.memset(ssA[:], 1.0), (), ["ssA"])
                op("pool", lambda e: e.memset(invn[:, 0:1], 1.0 / 384), (), ["invn"])
                op("pool", lambda e: e.memset(invn[:, 1:2], 1.0 / 256), (), ["invn"])
                op("pool", lambda e: e.memset(V_all[:, :, :, 64:65], 1.0), (), ["V_all"])
                TT("dve", sqq[:, 0, :], qhwB[:], qhwB[:], ALU.mult, ["qhwB"], ["sqq"])
                TT("dve", sqq[:, 1, :], khwB[:], khwB[:], ALU.mult, ["khwB"], ["sqq"])
                op("dve", lambda e: e.tensor_reduce(out=cmx[:, 0:2], in_=sqq[:, 0:2, :], axis=AX.X, op=ALU.max), ["sqq"], ["cmx"])
                TT("dve", cmx[:, 2:3], cmx[:, 0:1], cmx[:, 1:2], ALU.mult, ["cmx"], ["cmx"])
                ACTF(cmx[:, 2:3], cmx[:, 2:3], AF.Ln, ["cmx"], ["cmx"])
                ACTF(cmx[:, 3:4], cmx[:, 2:3], AF.Exp, ["cmx"], ["cmx"], scale=0.5)
                TS("dve", negC[:], cmx[:, 3:4], -math.sqrt(96.0), None, ALU.mult, None, ["cmx"], ["negC"])
                v8 = lambda bk: psall[:, bk:bk + 2, :].rearrange("p b (h c) -> p (b h) c", c=128)
                qfS = [A_("qfS%d" % i, [128, 8, 96], F32) for i in range(2)]
                kvfS = [A_("kvfS%d" % i, [128, 8, 128], F32) for i in range(2)]
                krsS = [A_("krsS%d" % i, [128, 32], F32) for i in range(2)]
                skrS = [A_("skrS%d" % i, [128, 1], F32) for i in range(2)]
                bq = lambda v_, n: v_.unsqueeze(2).to_broadcast([128, 8, n])
                bh = lambda v_, n: v_.unsqueeze(1).to_broadcast([128, 8, n])
                r4 = lambda t_, b_: t_.rearrange("p h (a b f) -> p h a b f", a=2, b=2)[:, :, :, b_, :]

                def stream_X(tt):
                    lat = tt >= 2
                    tok = slice(tt * 128, (tt + 1) * 128)
                    par = tt % 2
                    hb = ["hT%d" % tt]
                    qf_, qfn = qfS[par], "qfS%d" % par
                    kvf, kvfn = kvfS[par], "kvfS%d" % par
                    krs_, krsn = krsS[par], "krsS%d" % par
                    skr, skrn = skrS[par], "skrS%d" % par
                    p_cq, p_kv = pb[0][:, 0:384], pb[1][:, 0:288]
                    for kt in range(8):
                        if lat:
                            MM(p_cq, hT[:, kt, tok], wtok[:, kt, 0:384], hb + ["wtok"], [PB[0]], start=(kt == 0), stop=(kt == 7), inc=False)
                        MM(p_kv, hT[:, kt, tok], wtok[:, kt, 384:672], hb + ["wtok"], [PB[1]], start=(kt == 0), stop=(kt == 7), inc=(kt == 7))
                    if lat:
                        op("act", lambda e: e.activation(out=junk2[:], in_=p_cq, func=AF.Square, accum_out=ssA[:, 0:1]), [PB[0]], ["junk2", "ssA"])
                    op("act", lambda e: e.activation(out=junk2[:, 0:256], in_=p_kv[:, 0:256], func=AF.Square, accum_out=ssA[:, 1:2]),
                       [PB[1]], ["junk2", "ssA"])
                    op("act", lambda e: e.activation(out=junk2[:, 0:32], in_=p_kv[:, 256:288], func=AF.Square, accum_out=skr[:]),
                       [PB[1]], ["junk2", skrn])
                    TT("dve", rs2[:], ssA[:, 0:2], invn[:], ALU.mult, ["ssA", "invn"], ["rs2"])
                    TS("dve", rs2[:], rs2[:], EPS, None, ALU.add, None, ["rs2"], ["rs2"])
                    ACTF(rs2[:], rs2[:], AF.Ln, ["rs2"], ["rs2"])
                    ACTF(rs2[:], rs2[:], AF.Exp, ["rs2"], ["rs2"], scale=-0.5)
                    yield
                    if lat:
                        ACTF(cqn[:], p_cq, AF.Identity, [PB[0], "rs2"], ["cqn"], scale=rs2[:, 0:1])
                    TS("dve", ckvn[:], p_kv[:, 0:256], rs2[:, 1:2], None, ALU.mult, None, [PB[1], "rs2"], ["ckvn"])
                    CP("act", krs_[:], p_kv[:, 256:288], [PB[1]], [krsn])
                    pT = pb[2][:].bitcast(BF16).rearrange("p (k t) -> p k t", k=8)
                    if lat:
                        for i in range(3):
                            TR(pT[:, i, :], cqn[:, i * 128:(i + 1) * 128], ident[:], ["cqn", "ident"], [PB[2]], inc=False)
                    for i in range(2):
                        TR(pT[:, 3 + i, :], ckvn[:, i * 128:(i + 1) * 128], ident[:], ["ckvn", "ident"], [PB[2]], inc=(i == 1))
                    yield
                    if lat:
                        TT("dve", cqnT[:], pT[:, 0:3, :], qnwT[:].unsqueeze(2).to_broadcast([128, 3, 128]), ALU.mult, [PB[2], "qnwT"], ["cqnT"])
                    TT("dve", ckvnT[:], pT[:, 3:5, :], kvnwT[:].unsqueeze(2).to_broadcast([128, 2, 128]), ALU.mult, [PB[2], "kvnwT"], ["ckvnT"])
                    if lat:
                        for (c0, c1, bk) in ((0, 512, 3), (512, 768, 4)):
                            for kt in range(3):
                                MM(pb[bk][:, 0:c1 - c0], cqnT[:, kt, :], wuq[:, kt, c0:c1], ["cqnT", "wuq"], [PB[bk]],
                                   start=(kt == 0), stop=(kt == 2), inc=(kt == 2))
                    for nh in range(2):
                        for kt in range(2):
                            MM(pb[5 + nh][:], ckvnT[:, kt, :], wukv[:, kt, nh * 512:(nh + 1) * 512], ["ckvnT", "wukv"], [PB[5 + nh]],
                               start=(kt == 0), stop=(kt == 1), inc=(kt == 1))
                    yield
                    qff = qf_[:].rearrange("p h c -> p (h c)")
                    kvff = kvf[:].rearrange("p h c -> p (h c)")
                    if lat:
                        evac_copy(qff[:, 0:512], pb[3][:], [PB[3]], [qfn])
                        evac_copy(qff[:, 512:768], pb[4][:, 0:256], [PB[4]], [qfn])
                    evac_copy(kvff[:, 0:512], pb[5][:], [PB[5]], [kvfn])
                    evac_copy(kvff[:, 512:1024], pb[6][:], [PB[6]], [kvfn])
                    yield

                def stream_Y(tt):
                    sqk = sqq[:, :, 0:64]
                    lat = tt >= 2
                    lt = tt - 2
                    tok = slice(tt * 128, (tt + 1) * 128)
                    par = tt % 2
                    qf, qfn = qfS[par], "qfS%d" % par
                    kvv, kvfn = kvfS[par], "kvfS%d" % par
                    krs, krsn = krsS[par], "krsS%d" % par
                    skr, skrn = skrS[par], "skrS%d" % par
                    if lat:
                        TT("pool", sqq[:], qf[:], qf[:], ALU.mult, [qfn], ["sqq"])
                        op("dve", lambda e: e.tensor_reduce(out=ss16[:, 0:8], in_=sqq[:], axis=AX.X, op=ALU.add), ["sqq"], ["ss16"])
                    else:
                        op("pool", lambda e: e.memset(ss16[:, 0:8], 1.0), (), ["ss16"])
                    TT("pool", sqk, kvv[:, :, 0:64], kvv[:, :, 0:64], ALU.mult, [kvfn], ["sqq"])
                    op("dve", lambda e: e.tensor_reduce(out=ss16[:, 8:16], in_=sqk, axis=AX.X, op=ALU.add), ["sqq"], ["ss16"])
                    TS("dve", ss16[:, 8:16], ss16[:, 8:16], skr[:], None, ALU.add, None, ["ss16", skrn], ["ss16"])
                    TS("dve", ss16[:], ss16[:], 1.0 / 96, EPS, ALU.mult, ALU.add, ["ss16"], ["ss16"])
                    ACTF(ss16[:], ss16[:], AF.Ln, ["ss16"], ["ss16"])
                    ACTF(ss16[:], ss16[:], AF.Exp, ["ss16"], ["ss16"], scale=-0.5)
                    yield

                    def rope(src_t, dst_fin, cs, sn):
                        cB = cs.rearrange("p (a f) -> p a f", a=2).unsqueeze(1).to_broadcast([128, 8, 2, 8])
                        sB = sn.rearrange("p (a f) -> p a f", a=2).unsqueeze(1).to_broadcast([128, 8, 2, 8])
                        t1, t2 = r4(src_t, 0), r4(src_t, 1)
                        o1, o2 = r4(dst_fin, 0), r4(dst_fin, 1)
                        TT("dve", ra[:], t1, cB, ALU.mult, ["tl", "cosT"], ["ra"])
                        TT("pool", rb[:], t2, sB, ALU.mult, ["tl", "sinT"], ["rb"])
                        TT("dve", o1, ra[:], rb[:], ALU.subtract, ["ra", "rb"], ["fin"])
                        TT("dve", ra[:], t1, sB, ALU.mult, ["tl", "sinT"], ["ra"])
                        TT("pool", rb[:], t2, cB, ALU.mult, ["tl", "cosT"], ["rb"])
                        TT("dve", o2, ra[:], rb[:], ALU.add, ["ra", "rb"], ["fin"])

                    if lat:
                        TT("dve", qf[:], qf[:], bq(ss16[:, 0:8], 96), ALU.mult, [qfn, "ss16"], [qfn])
                        TT("dve", q_fin[:, :, 0:64], qf[:, :, 0:64], bh(qhwB[:, 0:64], 64), ALU.mult, [qfn, "qhwB"], ["fin"])
                        TT("dve", tl[:], qf[:, :, 64:96], bh(qhwB[:, 64:96], 32), ALU.mult, [qfn, "qhwB"], ["tl"])
                        rope(tl[:], q_fin[:, :, 64:96], cosT[:, lt, :], sinT[:, lt, :])
                        pq = pb[7][:].bitcast(BF16).rearrange("p (k t) -> p k t", k=8)
                        for h in range(8):
                            TR(pq[0:96, h, :], q_fin[:, h, :], ident[:], ["fin", "ident"], [PB[7]], inc=(h == 7))
                        evac_copy(qT_all[0:96, :, lt * 128:(lt + 1) * 128], pq[0:96, :, :], [PB[7]], ["qT_all"])
                    yield
                    TT("dve", sqk, kvv[:, :, 0:64], bq(ss16[:, 8:16], 64), ALU.mult, [kvfn, "ss16"], ["sqq"])
                    TT("dve", k_fin[:, :, 0:64], sqk, bh(khwB[:, 0:64], 64), ALU.mult, ["sqq", "khwB"], ["fin"])
                    TT("dve", tl[:], bh(krs[:], 32), bq(ss16[:, 8:16], 32), ALU.mult, [krsn, "ss16"], ["tl"])
                    TT("dve", tl[:], tl[:], bh(khwB[:, 64:96], 32), ALU.mult, ["tl", "khwB"], ["tl"])
                    if lat:
                        rope(tl[:], k_fin[:, :, 64:96], cosT[:, lt, :], sinT[:, lt, :])
                    else:
                        CP("act", k_fin[:, :, 64:96], tl[:], ["tl"], ["fin"])
                    CP("act", V_all[:, tt, :, 0:64], kvv[:, :, 64:128], [kvfn], ["V_all"])
                    pk = pb[7][:].bitcast(BF16).rearrange("p (k t) -> p k t", k=8)
                    for h in range(8):
                        TR(pk[0:96, h, :], k_fin[:, h, :], ident[:], ["fin", "ident"], [PB[7]], inc=(h == 7))
                    evac_copy(kT_all[0:96, :, tok], pk[0:96, :, :], [PB[7]], ["kT_all"])
                    yield

                def run2(gens):
                    gens = [g for g in gens if g is not None]
                    while gens:
                        for g in list(gens):
                            try:
                                next(g)
                            except StopIteration:
                                gens.remove(g)

                run2([stream_X(0)])
                for tt in range(NT):
                    run2([stream_Y(tt), stream_X(tt + 1) if tt + 1 < NT else None])
                tap("qT_all", qT_all[0:96, :, :], [96, 8, T_LAT], BF16, ["qT_all"])
                tap("kT_all", kT_all[0:96, :, :], [96, 8, T_ALL], BF16, ["kT_all"])
                tap("V_all", V_all[:], [128, NT, 8, 65], BF16, ["V_all"])
                fw.barrier()

            if stop_after != "C2":
                with ExitStack() as pe_:
                    PT = [fw.sbuf("PT%d" % i, [128, 2, 512], BF16, pe_) for i in range(2)]
                    o_tok = fw.sbuf("o_tok", [128, 4, 512], BF16, pe_)
                    rec = fw.sbuf("rec", [128, 4], F32, pe_)
                    SCALE = 96.0 ** -0.5
                    steps = [(g, h, jp) for g in range(4) for h in range(8) for jp in range(9)]

                    def acc_of(g, h):
                        ab = 4 + ((g * 8 + h) % 2)
                        return ab, pb[ab][:, 0:260].rearrange("p (q c) -> p q c", c=65)

                    def scores(k):
                        g, h, jp = steps[k]
                        sb = 2 * (k % 2)
                        for t in range(2):
                            kt_ = 2 * jp + t
                            MM(pb[sb + t][:], kT_all[0:96, h, kt_ * 128:(kt_ + 1) * 128], qT_all[0:96, h, g * 512:(g + 1) * 512],
                               ["kT_all", "qT_all"], [PB[sb + t]], inc=(t == 1))

                    def expo(k):
                        sb = 2 * (k % 2)
                        Pt, Ptn = PT[k % 2], "PT%d" % (k % 2)
                        op("act", lambda e: e.activation(out=Pt[:], in_=psall[:, sb:sb + 2, :], func=AF.Exp, scale=SCALE,
                                                         bias=negC[:]), [PB[sb], PB[sb + 1], "negC"], [Ptn])

                    def pv(k):
                        g, h, jp = steps[k]
                        ab, acc = acc_of(g, h)
                        Pt, Ptn = PT[k % 2], "PT%d" % (k % 2)
                        for t in range(2):
                            kt_ = 2 * jp + t
                            for qs in range(4):
                                first = (jp == 0 and t == 0 and qs == 0)
                                lastmm = (jp == 8 and t == 1 and qs == 3)
                                op("pe", lambda e: e.matmul(acc[:, qs, :], lhsT=Pt[:, t, qs * 128:(qs + 1) * 128],
                                                            rhs=V_all[:, kt_, h, :], start=first, stop=lastmm,
                                                            skip_group_check=True),
                                   [Ptn, "V_all"], [PB[ab]], inc=(t == 1 and qs == 3))

                    scores(0)
                    for k in range(len(steps)):
                        g, h, jp = steps[k]
                        expo(k)
                        if k + 1 < len(steps):
                            scores(k + 1)
                        pv(k)
                        if jp != 8:
                            continue
                        ab, acc = acc_of(g, h)
                        qs_tok = slice(g * 512, (g + 1) * 512)
                        op("dve", lambda e: e.reciprocal(out=rec[:], in_=acc[:, :, 64]), [PB[ab]], ["rec"])
                        TT("dve", o_tok[:, :, h * 64:(h + 1) * 64], acc[:, :, 0:64], rec[:].unsqueeze(2).to_broadcast([128, 4, 64]),
                           ALU.mult, [PB[ab], "rec"], ["o_tok"])
                        if h != 7:
                            continue
                        for half in range(2):
                            bk = 6 + half
                            pT = pb[bk][:].bitcast(BF16).rearrange("p (k t) -> p k t", k=8)
                            for qq in range(2):
                                qs = half * 2 + qq
                                for c4 in range(4):
                                    TR(pT[:, qq * 4 + c4, :], o_tok[:, qs, c4 * 128:(c4 + 1) * 128], ident[:], ["o_tok", "ident"], [PB[bk]],
                                       inc=(qq == 1 and c4 == 3))
                            dst = omT[:, :, qs_tok].rearrange("p c (qs t) -> p qs c t", qs=4)[:, half * 2:half * 2 + 2, :, :]
                            evac_copy(dst, pT.rearrange("p (qq c) t -> p qq c t", qq=2), [PB[bk]], ["omT"])
                    tap("omT", omT[:], [128, 4, T_LAT], BF16, ["omT"])
                    fw.barrier()
            mla_era.close()

            if stop_after not in ("C2", "E"):
                with ExitStack() as pf:
                    A_ = lambda name, shape, dt: fw.sbuf(name, shape, dt, pf)
                    wz = A_("wz", [128, 8, 3072], BF16)
                    wmo = A_("wmo", [128, 4, D], BF16)
                    wdo = A_("wdo", [128, 4, D], BF16)
                    wo = A_("wo", [128, 8, D], BF16)
                    sg = A_("sg", [128, 16, 512], BF16)
                    szm = A_("szm", [128, 512], BF16)
                    om_s = A_("om_s", [128, 4, 512], BF16)
                    od_s = A_("od_s", [128, 4, 512], BF16)
                    mT = A_("mT", [128, 8, 512], BF16)
                    t1S = [A_("t1%d" % i, [128, 512], F32) for i in range(2)]
                    t2S = [A_("t2%d" % i, [128, 512], F32) for i in range(2)]
                    xt = [A_("xt%d" % i, [128, D], F32) for i in range(2)]
                    ot = [A_("ot%d" % i, [128, D], F32) for i in range(1)]
                    dma("pool", wz[:, :, 0:512], winv[:, :, 672:1184], writes=["wz_m"])
                    dma("pool", wz[:, :, 512:1024], winv[:, :, 2720:3232], writes=["wz_d"])
                    for i in range(4):
                        dma("pool", wz[:, :, 1024 + i * 512:1536 + i * 512], winv[:, :, 3264 + i * 512:3776 + i * 512], writes=["wz_g%d" % i])
                    dma("pool", wmo[:], mla_w_o.rearrange("(c p) n -> p c n", p=128), writes=["wmo"])
                    dma("pool", wdo[:], dn_w_o.rearrange("(c p) n -> p c n", p=128), writes=["wdo"])
                    dma("pool", wo[:], w_out.rearrange("(c p) n -> p c n", p=128), writes=["wo"])
                    rr = 0
                    for g in range(4):
                        lt0 = g * 4
                        ltok = slice(g * 512, (g + 1) * 512)
                        htok = slice(256 + g * 512, 256 + (g + 1) * 512)
                        hb = hT_bufs(256 + g * 512, 512)
                        for (which, src_T, srcn, dst_s, dsn, wn) in ((0, omT, "omT", om_s, "om_s", "wz_m"), (1, odT, "odT", od_s, "od_s", "wz_d")):
                            for f in range(4):
                                bk = rr % 4
                                rr += 1
                                for kt in range(8):
                                    MM(pb[bk][:], wz[:, kt, which * 512 + f * 128:which * 512 + (f + 1) * 128], hT[:, kt, htok], hb + [wn],
                                       [PB[bk]], start=(kt == 0), stop=(kt == 7), inc=(kt == 7))
                                ACTF(szm[:], pb[bk][:], AF.Silu, [PB[bk]], ["szm"])
                                TT("dve", dst_s[:, f, :], src_T[:, f, ltok], szm[:], ALU.mult, [srcn, "szm"], [dsn])
                        for c in range(16):
                            bk = rr % 4
                            rr += 1
                            for kt in range(8):
                                MM(pb[bk][:], wz[:, kt, 1024 + c * 128:1024 + (c + 1) * 128], hT[:, kt, htok], hb + ["wz_g%d" % (c // 4)],
                                   [PB[bk]], start=(kt == 0), stop=(kt == 7), inc=(kt == 7))
                            ACTF(sg[:, c, :], pb[bk][:], AF.Sigmoid, [PB[bk]], ["sg"])
                        for f8 in range(8):
                            ba, bb_ = (4, 5) if f8 % 2 == 0 else (6, 7)
                            t1_, t1n = t1S[f8 % 2], "t1%d" % (f8 % 2)
                            t2_, t2n = t2S[f8 % 2], "t2%d" % (f8 % 2)
                            for c4 in range(4):
                                MM(pb[ba][:], wmo[:, c4, f8 * 128:(f8 + 1) * 128], om_s[:, c4, :], ["wmo", "om_s"], [PB[ba]],
                                   start=(c4 == 0), stop=(c4 == 3), inc=(c4 == 3))
                            for c4 in range(4):
                                MM(pb[bb_][:], wdo[:, c4, f8 * 128:(f8 + 1) * 128], od_s[:, c4, :], ["wdo", "od_s"], [PB[bb_]],
                                   start=(c4 == 0), stop=(c4 == 3), inc=(c4 == 3))
                            TT("dve", t1_[:], pb[ba][:], sg[:, f8, :], ALU.mult, [PB[ba], "sg"], [t1n])
                            TT("dve", t2_[:], pb[bb_][:], sg[:, 8 + f8, :], ALU.mult, [PB[bb_], "sg"], [t2n])
                            TT("pool", mT[:, f8, :], t1_[:], t2_[:], ALU.add, [t1n, t2n], ["mT"])
                        for qs in range(4):
                            lt = lt0 + qs
                            xb_, xbn = xt[lt % 2], "xt%d" % (lt % 2)
                            ob_, obn = ot[0], "ot0"
                            if lt == 0:
                                dma("sp", xb_[:], x[0:128, :], writes=[xbn])
                            if lt + 1 < 16:
                                dma("sp", xt[(lt + 1) % 2][:], x[(lt + 1) * 128:(lt + 2) * 128, :], writes=["xt%d" % ((lt + 1) % 2)])
                            for nh in range(2):
                                bk = 2 * (qs % 2) + nh
                                for f8 in range(8):
                                    MM(pb[bk][:], mT[:, f8, qs * 128:(qs + 1) * 128], wo[:, f8, nh * 512:(nh + 1) * 512], ["mT", "wo"], [PB[bk]],
                                       start=(f8 == 0), stop=(f8 == 7), inc=(f8 == 7))
                                cs = slice(nh * 512, (nh + 1) * 512)
                                TT("dve", ob_[:, cs], pb[bk][:], gateB[:, cs], ALU.mult, [PB[bk], "gateB"], [obn])
                                TT("pool", xb_[:, cs], ob_[:, cs], xb_[:, cs], ALU.add, [obn, xbn], [xbn])
                            dma("sp", out[lt * 128:(lt + 1) * 128, :], xb_[:], reads=[xbn], writes=["out_dram"], slot="out%d" % (lt % 2))
                    fw.wait_bufs("sp", ["out_dram"])
                    fw.barrier()

        fw.wait_bufs("sp", ["tap_" + n for n in tap_out] + ([] if stop_after else []))
        fw.barrier()
        print("[build] ops=%d waits=%d sems=%d" % (fw.n_ops, fw.n_waits, fw.nsem))
        print("[build] per-engine incs:", {n: e.cnt for n, e in fw.engs.items()})
        stuck = fw.simulate()
        print("[build] deadlock check:", "OK" if not stuck else "STUCK %s" % stuck)
    return nc, tap_out


def _in_maps(inputs):
    cst = _host_consts()
    maps = []
    f = lambda a: np.ascontiguousarray(np.asarray(a, dtype=np.float32))
    for b in range(8):
        m = {
            "x": f(inputs["x"][b]), "ctx": f(inputs["ctx"][b]), "c": f(inputs["c"][b]), "c_ctx": f(inputs["c_ctx"]),
            "w_mod": f(inputs["w_mod"][0]), "b_mod": f(inputs["b_mod"][0]), "norm_w": f(inputs["norm_w"][0]),
            "w_in": f(inputs["w_in"][0]), "dn_conv_w": f(inputs["dn_conv_w"][0]),
            "dn_a_log": f(inputs["dn_a_log"][0]).reshape(16), "dn_dt_bias": f(inputs["dn_dt_bias"][0]).reshape(16),
            "dn_out_norm_w": f(inputs["dn_out_norm_w"][0]),
            "mla_q_norm_w": f(inputs["mla_q_norm_w"][0]), "mla_w_uq": f(inputs["mla_w_uq"][0]),
            "mla_kv_norm_w": f(inputs["mla_kv_norm_w"][0]), "mla_w_ukv": f(inputs["mla_w_ukv"][0]),
            "mla_q_head_norm_w": f(inputs["mla_q_head_norm_w"][0]), "mla_k_head_norm_w": f(inputs["mla_k_head_norm_w"][0]),
            "mla_w_o": f(inputs["mla_w_o"][0]), "dn_w_o": f(inputs["dn_w_o"][0]), "w_out": f(inputs["w_out"][0]),
        }
        m.update(cst)
        maps.append(m)
    return maps


def kernel(**inputs):
    nc, _ = build_program()
    res = run_bass_kernel_spmd(nc, _in_maps(inputs), core_ids=list(range(8)))
    return np.stack([r["out"] for r in res.results], axis=0).astype(np.float32)
```

```python
import math
import os
from contextlib import ExitStack

import numpy as np
import concourse.bass as bass
import concourse.mybir as mybir
from concourse.bass_utils import run_bass_kernel_spmd

F32 = mybir.dt.float32
BF16 = mybir.dt.bfloat16
ALU = mybir.AluOpType
AF = mybir.ActivationFunctionType
AX = mybir.AxisListType

D = 1024
T_LAT = 2048
T_CTX = 256
T_ALL = 2304
NT = 18
IN_DIM = 5312
EPS = 1e-6


class Buf:
    __slots__ = ("name", "lw", "rd")

    def __init__(self, name):
        self.name = name
        self.lw = None
        self.rd = {}


class Eng:
    def __init__(self, name, be, sem, same_raw):
        self.name = name
        self.be = be
        self.sem = sem
        self.cnt = 0
        self.waited = {}
        self.same_raw = same_raw


class FW:
    def __init__(self, nc, stack):
        self.nc = nc
        self.stack = stack
        self.nsem = 0
        self.engs = {}
        for name, be, same_raw in (("pe", nc.tensor, False), ("act", nc.scalar, True),
                                   ("dve", nc.vector, True), ("pool", nc.gpsimd, True),
                                   ("sp", nc.sync, False)):
            self.engs[name] = Eng(name, be, self.new_sem("e_" + name), same_raw)
        self.dma_sems = {}
        self.n_ops = 0
        self.n_waits = 0
        self.bufs = {}
        self.trace = {n: [] for n in self.engs}

    def new_sem(self, name):
        self.nsem += 1
        return self.stack.enter_context(self.nc.semaphore(name))

    def sbuf(self, name, shape, dtype, stack=None):
        self.nalloc = getattr(self, "nalloc", 0) + 1
        return (stack or self.stack).enter_context(self.nc.sbuf_tensor("s%d_%s" % (self.nalloc, name), list(shape), dtype))

    def psum(self, name, shape, dtype=F32):
        return self.stack.enter_context(self.nc.psum_tensor("p_" + name, list(shape), dtype))

    def B(self, name):
        b = self.bufs.get(name)
        if b is None:
            b = self.bufs[name] = Buf(name)
        return b

    def _bl(self, xs):
        return [self.B(x) if isinstance(x, str) else x for x in xs]

    def _deps(self, eng, reads, writes):
        deps = {}

        def add(sem, val):
            k = id(sem)
            if k not in deps or deps[k][1] < val:
                deps[k] = (sem, val)

        for b in reads:
            if b.lw is not None:
                s, v = b.lw
                if s is eng.sem and not eng.same_raw:
                    continue
                add(s, v)
        for b in writes:
            if b.lw is not None:
                s, v = b.lw
                if s is not eng.sem or eng.same_raw:
                    add(s, v)
            for k, (s, v) in b.rd.items():
                if s is not eng.sem or eng.same_raw:
                    add(s, v)
        for k, (s, v) in deps.items():
            if eng.waited.get(k, 0) < v:
                eng.be.wait_ge(s, v)
                eng.waited[k] = v
                self.n_waits += 1
                self.trace[eng.name].append(("w", k, v))

    def _commit(self, sem, val, reads, writes):
        k = id(sem)
        for b in reads:
            if k not in b.rd or b.rd[k][1] < val:
                b.rd[k] = (sem, val)
        for b in writes:
            b.lw = (sem, val)
            b.rd = {}

    def op(self, ename, fn, reads=(), writes=(), inc=True):
        eng = self.engs[ename]
        reads = self._bl(reads)
        writes = self._bl(writes)
        self._deps(eng, reads, writes)
        ins = fn(eng.be)
        if inc:
            eng.cnt += 1
            ins.then_inc(eng.sem, 1)
            self._commit(eng.sem, eng.cnt, reads, writes)
            self.trace[ename].append(("i", id(eng.sem), 1))
        else:
            self._commit(eng.sem, eng.cnt + 1, reads, writes)
            self.trace[ename].append(("n", 0, 0))
        self.n_ops += 1
        return ins

    def dma(self, qname, out, in_, reads=(), writes=(), slot=None, **kw):
        eng = self.engs[qname]
        reads = self._bl(reads)
        writes = self._bl(writes)
        self._deps(eng, reads, writes)
        if slot is None:
            slot = writes[0].name if writes else reads[0].name
        if slot not in self.dma_sems:
            self.dma_sems[slot] = [self.new_sem("d%d" % len(self.dma_sems)), 0]
        ent = self.dma_sems[slot]
        ins = eng.be.dma_start(out=out, in_=in_, **kw)
        ent[1] += 16
        ins.then_inc(ent[0], 16)
        self.trace[qname].append(("i", id(ent[0]), 16))
        self._commit(ent[0], ent[1], reads, writes)
        self.n_ops += 1
        return ins

    def wait_bufs(self, ename, bufs):
        self._deps(self.engs[ename], self._bl(bufs), ())

    def simulate(self):
        sem = {}
        ptr = {n: 0 for n in self.trace}
        progress = True
        while progress:
            progress = False
            for n, tr in self.trace.items():
                while ptr[n] < len(tr):
                    kind, k, v = tr[ptr[n]]
                    if kind == "w":
                        if sem.get(k, 0) >= v:
                            ptr[n] += 1
                            progress = True
                        else:
                            break
                    else:
                        if kind == "i":
                            sem[k] = sem.get(k, 0) + v
                        ptr[n] += 1
                        progress = True
        stuck = {n: (ptr[n], len(tr)) for n, tr in self.trace.items() if ptr[n] < len(tr)}
        return stuck

    def barrier(self):
        targets = [(e.sem, e.cnt) for e in self.engs.values() if e.cnt > 0]
        targets += [(s, v) for (s, v) in self.dma_sems.values() if v > 0]
        for e in self.engs.values():
            for s, v in targets:
                if s is e.sem:
                    continue
                if e.waited.get(id(s), 0) < v:
                    e.be.wait_ge(s, v)
                    e.waited[id(s)] = v
                    self.n_waits += 1
                    self.trace[e.name].append(("w", id(s), v))


def _host_consts():
    c = {}
    c["ident"] = np.eye(128, dtype=np.float32)
    m = np.arange(128)[:, None]
    i = np.arange(128)[None, :]
    c["tri"] = np.stack([(m <= i), (m < i), (m >= i), (m > i), np.ones((128, 128), bool),
                         (m // 64 == i // 64)]).astype(np.float32)
    c["negoff"] = -(m != i).astype(np.float32)
    hm = [(m // 8 == i // 8)]
    for s_ in (8, 16, 32, 64):
        hm.append((m // (2 * s_) == i // (2 * s_)) & (m % (2 * s_) >= s_) & (i % (2 * s_) < s_))
    for s_ in (8, 16, 32, 64):
        hm.append((m // (2 * s_) == i // (2 * s_)) & (i % (2 * s_) >= s_) & (m % (2 * s_) < s_))
    c["hmask"] = np.stack(hm).astype(np.float32)
    t = np.arange(T_LAT)
    row = (t // 64).astype(np.float32)
    col = (t % 64).astype(np.float32)
    inv = (10000.0 ** (-np.arange(0, 16, 2, dtype=np.float32) / 16)).astype(np.float32)
    ang = np.concatenate([row[:, None] * inv, col[:, None] * inv], axis=-1).astype(np.float32)
    c["rope_cos"] = np.cos(ang).astype(np.float32)
    c["rope_sin"] = np.sin(ang).astype(np.float32)
    return c


def build_program(taps=(), stop_after=None):
    nc = bass.Bass("TRN2", target_bir_lowering=False)

    def din(name, shape):
        return nc.dram_tensor(name, list(shape), F32, kind="ExternalInput").ap()

    x = din("x", [T_LAT, D])
    ctx = din("ctx", [T_CTX, D])
    cvec = din("c", [D])
    cctx = din("c_ctx", [D])
    w_mod = din("w_mod", [D, 3 * D])
    b_mod = din("b_mod", [3 * D])
    norm_w = din("norm_w", [D])
    ident_d = din("ident", [128, 128])
    tri_d = din("tri", [6, 128, 128])
    negoff_d = din("negoff", [128, 128])
    hmask_d = din("hmask", [9, 128, 128])
    dn_w = din("dn_out_norm_w", [64])
    cos_d = din("rope_cos", [T_LAT, 16])
    sin_d = din("rope_sin", [T_LAT, 16])
    q_norm_w = din("mla_q_norm_w", [384])
    w_uq = din("mla_w_uq", [384, 768])
    kv_norm_w = din("mla_kv_norm_w", [256])
    w_ukv = din("mla_w_ukv", [256, 1024])
    qh_w = din("mla_q_head_norm_w", [96])
    kh_w = din("mla_k_head_norm_w", [96])
    mla_w_o = din("mla_w_o", [512, D])
    dn_w_o = din("dn_w_o", [512, D])
    w_out = din("w_out", [D, D])
    w_in = din("w_in", [D, IN_DIM])
    conv_w = din("dn_conv_w", [5, 1536])
    a_log = din("dn_a_log", [16])
    dt_bias = din("dn_dt_bias", [16])
    out = nc.dram_tensor("out", [T_LAT, D], F32, kind="ExternalOutput").ap()
    tap_out = {}

    with ExitStack() as st:
        fw = FW(nc, st)
        op, dma = fw.op, fw.dma

        def tap(name, sb_ap, shape, dtype, reads):
            if name not in taps:
                return
            t = nc.dram_tensor("tap_" + name, list(shape), dtype, kind="ExternalOutput").ap()
            tap_out[name] = t
            dma("sp", t, sb_ap, reads=reads, writes=["tap_" + name])

        ident = fw.sbuf("ident", [128, 128], BF16)
        identf = fw.sbuf("identf", [128, 128], F32)
        gateB = fw.sbuf("gateB", [128, D], F32)
        odT = fw.sbuf("odT", [128, 4, T_LAT], BF16)
        modT = fw.sbuf("modT", [128, 3, 8, 2], F32)
        gam = fw.sbuf("gam", [128, 8, 2], F32)
        shf = fw.sbuf("shf", [128, 8, 2], F32)
        psall = fw.psum("psall", [128, 8, 512], F32)
        pb = [psall[:, i, :] for i in range(8)]
        PB = [fw.B("pb%d" % i) for i in range(8)]

        dma("pool", ident[:], ident_d, writes=["ident"])
        dma("sp", identf[:], ident_d, writes=["identf"])

        with ExitStack() as pa:
            cf = fw.sbuf("cf", [128, 2, 8], F32, pa)
            csil = fw.sbuf("csil", [128, 2, 8], BF16, pa)
            csilB = fw.sbuf("csilB", [128, 8, 128], BF16, pa)
            bmodT = fw.sbuf("bmodT", [128, 3, 8], F32, pa)
            nwT = fw.sbuf("nwT", [128, 8], F32, pa)
            bgB = fw.sbuf("bgB", [128, D], F32, pa)
            wm = [fw.sbuf("wm%d" % i, [128, 8, 512], BF16, pa) for i in range(2)]
            with nc.allow_non_contiguous_dma(reason="small vector column layouts"):
                dma("sp", cf[:, 0, :], cvec.rearrange("(kt p) -> p kt", p=128), writes=["cf"])
                dma("sp", cf[:, 1, :], cctx.rearrange("(kt p) -> p kt", p=128), writes=["cf"], slot="cf2")
                for s in range(3):
                    dma("sp", bmodT[:, s, :], b_mod[s * D:(s + 1) * D].rearrange("(kt p) -> p kt", p=128),
                        writes=["bmodT"], slot="bmodT%d" % s)
                dma("sp", nwT[:], norm_w.rearrange("(kt p) -> p kt", p=128), writes=["nwT"])
            dma("sp", bgB[:], b_mod[2 * D:3 * D].partition_broadcast(128), writes=["bgB"])
            op("act", lambda e: e.activation(out=csil[:], in_=cf[:], func=AF.Silu), reads=["cf"], writes=["csil"])
            op("dve", lambda e: e.tensor_copy(out=csilB[:], in_=csil[:, 0, :].unsqueeze(2).to_broadcast([128, 8, 128])),
               reads=["csil"], writes=["csilB"])
            wmv = w_mod.rearrange("(kt p) n -> p kt n", p=128)
            mod_ps = pb[0][:, 0:48].rearrange("p (s k v) -> p s k v", s=3, k=8)
            for j in range(6):
                sec, half = j // 2, j % 2
                wmj = wm[j % 2]
                wname = "wm%d" % (j % 2)
                dma("pool", wmj[:], wmv[:, :, j * 512:(j + 1) * 512], writes=[wname])
                for q in range(4):
                    kto = half * 4 + q
                    for kti in range(8):
                        op("pe", (lambda q=q, kti=kti, kto=kto, sec=sec, wmj=wmj: lambda e: e.matmul(
                            mod_ps[:, sec, kto, :], lhsT=wmj[:, kti, q * 128:(q + 1) * 128], rhs=csil[:, :, kti],
                            start=(kti == 0), stop=(kti == 7)))(),
                           reads=[wname, "csil"], writes=[PB[0]], inc=(kti == 7))
                if sec == 2:
                    for kti in range(8):
                        op("pe", (lambda kti=kti, half=half, wmj=wmj: lambda e: e.matmul(
                            pb[1 + half][:], lhsT=csilB[:, kti, :], rhs=wmj[:, kti, :],
                            start=(kti == 0), stop=(kti == 7)))(),
                           reads=[wname, "csilB"], writes=[PB[1 + half]], inc=(kti == 7))
            op("dve", lambda e: e.tensor_tensor(out=modT[:], in0=mod_ps,
                                                in1=bmodT[:].unsqueeze(3).to_broadcast([128, 3, 8, 2]), op=ALU.add),
               reads=[PB[0], "bmodT"], writes=["modT"])
            op("dve", lambda e: e.tensor_scalar(out=gam[:], in0=modT[:, 1, :, :], scalar1=1.0, scalar2=None, op0=ALU.add),
               reads=["modT"], writes=["gam"])
            op("dve", lambda e: e.tensor_tensor(out=gam[:], in0=gam[:], in1=nwT[:].unsqueeze(2).to_broadcast([128, 8, 2]),
                                                op=ALU.mult), reads=["gam", "nwT"], writes=["gam"])
            op("dve", lambda e: e.tensor_copy(out=shf[:], in_=modT[:, 0, :, :]), reads=["modT"], writes=["shf"])
            for half in range(2):
                op("dve", (lambda half=half: lambda e: e.tensor_tensor(
                    out=gateB[:, half * 512:(half + 1) * 512], in0=pb[1 + half][:], in1=bgB[:, half * 512:(half + 1) * 512],
                    op=ALU.add))(), reads=[PB[1 + half], "bgB"], writes=["gateB"])
            tap("modT", modT[:], [128, 3, 8, 2], F32, ["modT"])
            tap("gateB", gateB[:], [128, D], F32, ["gateB"])
            fw.barrier()

        def phase_B(hT):
            with ExitStack() as pbk:
                xs = [fw.sbuf("xs%d" % i, [128, D], F32, pbk) for i in range(3)]
                junk = fw.sbuf("junkB", [128, D], BF16, pbk)
                xn = [fw.sbuf("xn%d" % i, [128, D], BF16, pbk) for i in range(2)]
                ssq = fw.sbuf("ssq", [128, NT], F32, pbk)
                def stats(tt):
                    xt, xtn = xs[tt % 3], "xs%d" % (tt % 3)
                    src = ctx[tt * 128:(tt + 1) * 128, :] if tt < 2 else x[(tt - 2) * 128:(tt - 1) * 128, :]
                    dma("sp", xt[:], src, writes=[xtn])
                    sc = ssq[:, tt:tt + 1]
                    op("act", lambda e: e.activation(out=junk[:], in_=xt[:], func=AF.Square, accum_out=sc), [xtn], ["junkB", "ssq%d" % tt])
                    op("dve", lambda e: e.tensor_scalar(out=sc, in0=sc, scalar1=1.0 / D, scalar2=EPS, op0=ALU.mult, op1=ALU.add),
                       ["ssq%d" % tt], ["ssq%d" % tt])
                    op("act", lambda e: e.activation(out=sc, in_=sc, func=AF.Ln), ["ssq%d" % tt], ["ssq%d" % tt])
                    op("act", lambda e: e.activation(out=sc, in_=sc, func=AF.Exp, scale=-0.5), ["ssq%d" % tt], ["ssq%d" % tt])

                stats(0)
                for tt in range(NT):
                    v = 1 if tt < 2 else 0
                    xt, xtn = xs[tt % 3], "xs%d" % (tt % 3)
                    sc = ssq[:, tt:tt + 1]
                    if tt + 1 < NT:
                        stats(tt + 1)
                    xb, xbn = xn[tt % 2], "xn%d" % (tt % 2)
                    bank = 2 + (tt % 2)
                    pT = pb[bank][:].bitcast(BF16).rearrange("p (k t) -> p k t", k=8)
                    op("dve", lambda e: e.tensor_scalar(out=xb[:], in0=xt[:], scalar1=sc, scalar2=None, op0=ALU.mult),
                       [xtn, "ssq%d" % tt], [xbn])
                    for kt in range(8):
                        op("pe", lambda e: e.transpose(out=pT[:, kt, :], in_=xb[:, kt * 128:(kt + 1) * 128], identity=ident[:]),
                           [xbn, "ident"], [PB[bank]], inc=(kt == 7))
                    for kt in range(8):
                        dst = hT[:, kt, tt * 128:(tt + 1) * 128]
                        if kt % 2 == 0:
                            op("act", lambda e: e.activation(out=dst, in_=pT[:, kt, :], func=AF.Identity, scale=gam[:, kt, v:v + 1],
                                                             bias=shf[:, kt, v:v + 1]), [PB[bank], "gam", "shf"], ["hT%d" % tt])
                        else:
                            op("dve", lambda e: e.tensor_scalar(out=dst, in0=pT[:, kt, :], scalar1=gam[:, kt, v:v + 1],
                                                                scalar2=shf[:, kt, v:v + 1], op0=ALU.mult, op1=ALU.add),
                               [PB[bank], "gam", "shf"], ["hT%d" % tt])
                fw.barrier()

        def MM(out_, lhsT, rhs, reads, writes, start=True, stop=True, inc=True):
            return op("pe", lambda e: e.matmul(out_, lhsT=lhsT, rhs=rhs, start=start, stop=stop), reads, writes, inc)

        def TR(out_, in_, idn, reads, writes, inc=True):
            return op("pe", lambda e: e.transpose(out=out_, in_=in_, identity=idn), reads, writes, inc)

        def ACTF(out_, in_, func, reads, writes, **kw):
            return op("act", lambda e: e.activation(out=out_, in_=in_, func=func, **kw), reads, writes)

        def TT(en, out_, in0, in1, alu, reads, writes):
            return op(en, lambda e: e.tensor_tensor(out=out_, in0=in0, in1=in1, op=alu), reads, writes)

        def TS(en, out_, in0, s1, s2, op0, op1, reads, writes):
            if s2 is None:
                return op(en, lambda e: e.tensor_scalar(out=out_, in0=in0, scalar1=s1, scalar2=None, op0=op0), reads, writes)
            return op(en, lambda e: e.tensor_scalar(out=out_, in0=in0, scalar1=s1, scalar2=s2, op0=op0, op1=op1), reads, writes)

        def STT(en, out_, in0, scalar, in1, op0, op1, reads, writes):
            return op(en, lambda e: e.scalar_tensor_tensor(out=out_, in0=in0, scalar=scalar, in1=in1, op0=op0, op1=op1),
                      reads, writes)

        def CP(en, out_, in_, reads, writes):
            if en == "act":
                return op("act", lambda e: e.activation(out=out_, in_=in_, func=AF.Copy), reads, writes)
            return op(en, lambda e: e.tensor_copy(out=out_, in_=in_), reads, writes)

        def hT_bufs(t0, n):
            return ["hT%d" % t for t in range(t0 // 128, (t0 + n + 127) // 128)]

        winv = w_in.rearrange("(kt p) n -> p kt n", p=128)
        GROUPS = [(0, 256)] + [(256 + 512 * i, 512) for i in range(4)]
        evac_rr = [0]

        EV_N = int(os.environ.get("EV_N", "3"))
        EV_A = int(os.environ.get("EV_A", "2"))

        def evac_copy(out_, in_, reads, writes):
            evac_rr[0] += 1
            return CP("act" if (evac_rr[0] % EV_N) < EV_A else "dve", out_, in_, reads, writes)


        dn_era = ExitStack()
        dnqT = fw.sbuf("dnqT", [128, 4, T_ALL], BF16, dn_era)
        dnkT = fw.sbuf("dnkT", [128, 4, T_ALL], BF16, dn_era)
        k_tok = fw.sbuf("k_tok", [128, NT, 8, 64], BF16, dn_era)
        v_tok = fw.sbuf("v_tok", [128, NT, 8, 64], BF16, dn_era)
        beta = fw.sbuf("beta", [128, NT, 16], F32, dn_era)
        gdec = fw.sbuf("gdec", [128, NT, 16], F32, dn_era)
        tri = fw.sbuf("tri", [128, 6, 128], F32, dn_era)
        onesbd = fw.sbuf("onesbd", [128, 128], BF16, dn_era)
        negoff = fw.sbuf("negoff", [128, 128], F32, dn_era)
        hm = fw.sbuf("hm", [128, 9, 128], BF16, dn_era)
        dma("sp", tri[:], tri_d.rearrange("k p i -> p k i"), writes=["tri"])
        dma("pool", onesbd[:], tri_d[5], writes=["onesbd"])
        dma("sp", negoff[:], negoff_d, writes=["negoff"])
        dma("pool", hm[:], hmask_d.rearrange("k p i -> p k i"), writes=["hm"])
        h_era = ExitStack()
        hT = fw.sbuf("hT", [128, 8, T_ALL], BF16, h_era)
        phase_B(hT)
        tap("hT", hT[:], [128, 8, T_ALL], BF16, ["hT%d" % t for t in range(NT)])
        hT_spill = nc.dram_tensor("hT_spill", [128, 8, T_ALL], BF16).ap()
        for kt in range(8):
            dma("sp", hT_spill[:, kt, :], hT[:, kt, :], reads=["hT%d" % t for t in range(NT)], writes=["hT_spill"], slot="spill%d" % kt)

        with ExitStack() as c1:
            wdnb = [fw.sbuf("wdn%d" % i, [128, 8, 512], BF16, c1) for i in range(2)]
            wbg = fw.sbuf("wbg", [128, 8, 32], BF16, c1)
            convT = fw.sbuf("convT", [128, 12, 5], F32, c1)
            diagwb = [fw.sbuf("diagw%d" % i, [128, 5, 128], BF16, c1) for i in range(2)]
            xc = [fw.sbuf("xc%d" % i, [128, 2312], BF16, c1) for i in range(2)]
            ys = [fw.sbuf("ys%d" % i, [128, T_ALL], F32, c1) for i in range(2)]
            sqS = [fw.sbuf("sq%d" % i, [128, T_ALL], BF16, c1) for i in range(2)]
            rsS = [fw.sbuf("rs%d" % i, [128, 512], F32, c1) for i in range(2)]
            zz = fw.sbuf("zz", [128, NT, 16], F32, c1)
            zm_ = fw.sbuf("zm_", [128, NT, 16], F32, c1)
            ze = fw.sbuf("ze", [128, NT, 16], F32, c1)
            dtbB = fw.sbuf("dtbB", [128, 16], F32, c1)
            negA = fw.sbuf("negA", [128, 16], F32, c1)
            wbg_n = fw.sbuf("wbg_n", [128, 8, 32], BF16, c1)
            dma("pool", wbg_n[:], winv[:, :, 3232:3264], writes=["wbg_n"])
            CP("dve", wbg[:].rearrange("p k (a par hp) -> p (k a) par hp", par=2, hp=4),
               wbg_n[:].rearrange("p k (a hp par) -> p (k a) par hp", par=2, hp=4), ["wbg_n"], ["wbg"])
            with nc.allow_non_contiguous_dma(reason="small conv weight column layout"):
                for j in range(5):
                    dma("sp", convT[:, :, j], conv_w[j, :].rearrange("(c p) -> p c", p=128), writes=["convT"], slot="convT%d" % j)
            dtb_n = fw.sbuf("dtb_n", [128, 16], F32, c1)
            alog_n = fw.sbuf("alog_n", [128, 16], F32, c1)
            dma("sp", dtb_n[:], dt_bias.partition_broadcast(128), writes=["dtb_n"])
            dma("sp", alog_n[:], a_log.partition_broadcast(128), writes=["alog_n"])
            CP("dve", dtbB[:].rearrange("p (a par hp) -> p a par hp", par=2, hp=4),
               dtb_n[:].rearrange("p (a hp par) -> p a par hp", par=2, hp=4), ["dtb_n"], ["dtbB"])
            CP("dve", negA[:].rearrange("p (a par hp) -> p a par hp", par=2, hp=4),
               alog_n[:].rearrange("p (a hp par) -> p a par hp", par=2, hp=4), ["alog_n"], ["negA"])
            for i in range(2):
                for lo, hi in ((0, 2), (258, 262), (2310, 2312)):
                    op("pool", lambda e: e.memset(xc[i][:, lo:hi], 0.0), (), ["xc%d" % i])
            zps = psall[:, 4:6, 0:288].rearrange("p b (t c) -> p b t c", c=32)
            for tt in range(NT):
                for kt in range(8):
                    MM(zps[:, tt // 9, tt % 9, :], hT[:, kt, tt * 128:(tt + 1) * 128], wbg[:, kt, :], ["hT%d" % tt, "wbg"],
                       [PB[4 + tt // 9]], start=(kt == 0), stop=(kt == 7), inc=(kt == 7))
            v4 = lambda t: t[:].rearrange("p (b t) c -> p b t c", b=2)
            ACTF(v4(beta), zps[:, :, :, 0:16], AF.Sigmoid, [PB[4], PB[5]], ["beta"])
            TT("dve", v4(zz), zps[:, :, :, 16:32], dtbB[:].unsqueeze(1).unsqueeze(1).to_broadcast([128, 2, 9, 16]), ALU.add,
               [PB[4], PB[5], "dtbB"], ["zz"])
            TS("dve", zm_[:], zz[:], 0.0, None, ALU.max, None, ["zz"], ["zm_"])
            STT("dve", ze[:], zm_[:], -2.0, zz[:], ALU.mult, ALU.add, ["zm_", "zz"], ["ze"])
            ACTF(ze[:], ze[:], AF.Exp, ["ze"], ["ze"])
            ACTF(ze[:], ze[:], AF.Ln, ["ze"], ["ze"], bias=1.0)
            ACTF(negA[:], negA[:], AF.Exp, ["negA"], ["negA"])
            TS("dve", negA[:], negA[:], -1.0, None, ALU.mult, None, ["negA"], ["negA"])
            TT("dve", ze[:], ze[:], zm_[:], ALU.add, ["ze", "zm_"], ["ze"])
            TT("dve", gdec[:], ze[:], negA[:].unsqueeze(1).to_broadcast([128, NT, 16]), ALU.mult, ["ze", "negA"], ["gdec"])
            def chunk_stream(c):
                which, hc = c // 4, c % 4
                par = c % 2
                xcb, xcn = xc[par], "xc%d" % par
                ysc, ysn = ys[par], "ys%d" % par
                sqc, sqn = sqS[par], "sq%d" % par
                wdn, wn = wdnb[which % 2], "wdn%d" % (which % 2)
                if c == 0:
                    dma("pool", wdn[:], winv[:, :, 1184:1184 + 512], writes=[wn])
                if hc == 1 and which < 2:
                    nw = which + 1
                    dma("pool", wdnb[nw % 2][:], winv[:, :, 1184 + nw * 512:1184 + (nw + 1) * 512], writes=["wdn%d" % (nw % 2)])
                diagw, dgn = diagwb[par], "diagw%d" % par
                for j in range(5):
                    TS("dve" if j % 2 else "pool", diagw[:, j, :], identf[:], convT[:, c, j:j + 1], None, ALU.mult, None,
                       ["identf", "convT"], [dgn])
                for gi, (t0, n) in enumerate(GROUPS):
                    bk = gi % 2
                    col = t0 + 2 if t0 < 256 else t0 + 6
                    for kt in range(8):
                        MM(pb[bk][:, 0:n], wdn[:, kt, hc * 128:(hc + 1) * 128], hT[:, kt, t0:t0 + n], [wn] + hT_bufs(t0, n), [PB[bk]],
                           start=(kt == 0), stop=(kt == 7), inc=(kt == 7))
                    evac_copy(xcb[:, col:col + n], pb[bk][:, 0:n], [PB[bk]], [xcn])
                    if gi in (1, 3):
                        yield
                yield
                for gi, (t0, n) in enumerate(GROUPS):
                    bk = 2 + gi % 2
                    col = t0 + 2 if t0 < 256 else t0 + 6
                    for j in range(5):
                        MM(pb[bk][:, 0:n], diagw[:, j, :], xcb[:, col + j - 2:col + j - 2 + n], [dgn, xcn], [PB[bk]],
                           start=(j == 0), stop=(j == 4), inc=(j == 4))
                    if which == 2:
                        ACTF(sqc[:, t0:t0 + n], pb[bk][:, 0:n], AF.Silu, [PB[bk]], [sqn])
                    else:
                        ACTF(ysc[:, t0:t0 + n], pb[bk][:, 0:n], AF.Silu, [PB[bk]], [ysn])
                yield
                if which < 2:
                    TT("pool", sqc[:], ysc[:], ysc[:], ALU.mult, [ysn], [sqn])
                    yield
                    dst = (dnqT if which == 0 else dnkT)
                    dn_ = ("dnqT%d" if which == 0 else "dnkT%d") % hc
                    for gi, (t0, n) in enumerate(GROUPS):
                        bk = 4 + gi % 2
                        rsg, rsn = rsS[gi % 2], "rs%d" % (gi % 2)
                        MM(pb[bk][:, 0:n], onesbd[:], sqc[:, t0:t0 + n], ["onesbd", sqn], [PB[bk]])
                        ACTF(rsg[:, 0:n], pb[bk][:, 0:n], AF.Ln, [PB[bk]], [rsn], bias=EPS)
                        ACTF(rsg[:, 0:n], rsg[:, 0:n], AF.Exp, [rsn], [rsn], scale=-0.5)
                        if which == 0:
                            STT("dve", dst[:, hc, t0:t0 + n], ysc[:, t0:t0 + n], 0.125, rsg[:, 0:n], ALU.mult, ALU.mult, [ysn, rsn], [dn_])
                        else:
                            TT("dve", dst[:, hc, t0:t0 + n], ysc[:, t0:t0 + n], rsg[:, 0:n], ALU.mult, [ysn, rsn], [dn_])
                    yield
                if which >= 1:
                    src = dnkT[:, hc, :] if which == 1 else sqc[:]
                    srcn = ("dnkT%d" % hc) if which == 1 else sqn
                    dtok = k_tok if which == 1 else v_tok
                    dtn = "k_tok" if which == 1 else "v_tok"
                    for bi, (tt0, ntl) in enumerate(((0, 8), (8, 8), (16, 2))):
                        bk = 6 + bi % 2
                        pT = pb[bk][:].bitcast(BF16).rearrange("p (k t) -> p k t", k=8)
                        for i in range(ntl):
                            tt = tt0 + i
                            TR(pT[:, i, :], src[:, tt * 128:(tt + 1) * 128], ident[:], [srcn, "ident"], [PB[bk]], inc=(i == ntl - 1))
                        evac_copy(dtok[:, tt0:tt0 + ntl, 2 * hc:2 * hc + 2, :].rearrange("p t h d -> p t (h d)"), pT[:, 0:ntl, :],
                                  [PB[bk]], [dtn])
                    yield

            NCH = int(os.environ.get("C1_NCH", "12"))
            active = []
            nxt_c = 0
            while nxt_c < NCH or active:
                if nxt_c < NCH and len(active) < 2:
                    active.append(chunk_stream(nxt_c))
                    nxt_c += 1
                for g in list(active):
                    try:
                        next(g)
                    except StopIteration:
                        active.remove(g)
            tap("dnqT", dnqT[:], [128, 4, T_ALL], BF16, ["dnqT%d" % i for i in range(4)])
            tap("dnkT", dnkT[:], [128, 4, T_ALL], BF16, ["dnkT%d" % i for i in range(4)])
            tap("k_tok", k_tok[:], [128, NT, 8, 64], BF16, ["k_tok"])
            tap("v_tok", v_tok[:], [128, NT, 8, 64], BF16, ["v_tok"])
            tap("beta", beta[:], [128, NT, 16], F32, ["beta"])
            tap("gdec", gdec[:], [128, NT, 16], F32, ["gdec"])
            fw.barrier()
        h_era.close()
        if stop_after == "C1":
            dn_era.close()

        if stop_after != "C1":
            with ExitStack() as dd:
                A_ = lambda name, shape, dt: fw.sbuf(name, shape, dt, dd)
                rg = A_("rgEs", [128, 8, 128], F32)
                Es = rg
                DT = A_("DT", [128, 8, 128], F32)
                E_ = A_("E_", [128, 8, 128], F32)
                AqkS = [A_("Aqk%d" % i, [128, 8, 128], BF16) for i in range(3)]
                R0S = [A_("R0b%d" % i, [128, 8, 128], BF16) for i in range(2)]
                P0S = [A_("P0b%d" % i, [128, 8, 128], BF16) for i in range(2)]
                BA = [[A_("BA%d%d" % (w, i), [128, 4, 2, 128], BF16) for i in range(2)] for w in range(2)]
                BB = [[A_("BB%d%d" % (w, i), [128, 4, 2, 128], BF16) for i in range(2)] for w in range(2)]
                Wb = [[A_("Wb%d%d" % (w, i), [128, 4, 128], BF16) for i in range(2)] for w in range(2)]
                Db = [[A_("Db%d%d" % (w, i), [128, 4, 128], BF16) for i in range(2)] for w in range(2)]
                Yb = [A_("Yb%d" % w, [128, 4, 128], BF16) for w in range(2)]
                Ypb = [A_("Ypb%d" % w, [128, 4, 128], BF16) for w in range(2)]
                Cmb = [A_("Cmb%d" % i, [128, 8, 128], BF16) for i in range(2)]
                Cfb = [A_("Cfb%d" % i, [128, 8, 128], BF16) for i in range(2)]
                XTS = [A_("XT%d" % i, [128, 8, 128], BF16) for i in range(2)]
                kg = A_("kg", [128, 8, 64], BF16)
                kdb = A_("kdb", [128, 8, 64], BF16)
                up = A_("up", [128, 8, 64], F32)
                wT = A_("wT", [128, 4, 128], BF16)
                vt = A_("vt", [128, 8, 64], BF16)
                S32 = A_("S32", [128, 4, 64], F32)
                Sbf = A_("Sbf", [128, 4, 64], BF16)
                egS = [A_("eg%d" % i, [128, 16], F32) for i in range(3)]
                egl2S = [A_("egl2%d" % i, [128, 4], F32) for i in range(3)]
                eb = A_("eb", [128, 8], F32)
                otmp = A_("otmp", [128, 8, 64], F32)
                ofin = A_("ofin", [128, 8, 64], F32)
                onb = A_("onb", [128, 8, 64], BF16)
                oss = A_("oss", [128, 8], F32)
                dnwB = A_("dnwB", [128, 64], F32)
                o_acc = A_("o_acc", [128, 16, 8, 64], BF16)
                dma("sp", dnwB[:], dn_w.partition_broadcast(128), writes=["dnwB"])
                LE, LT_, GE, GT_, ONES = (tri[:, i, :] for i in range(5))

                def bc_h(m):
                    return m.unsqueeze(1).to_broadcast([128, 8, 128])

                def bc_i(v, n):
                    return v.unsqueeze(2).to_broadcast([128, 8, n])

                ps2 = lambda b0: psall[:, b0:b0 + 2, :].rearrange("p b (i c) -> p (b i) c", c=128)
                v4q = lambda t: t.rearrange("p (par hp) e -> p par hp e", par=2)
                bc4 = lambda v, n: v.rearrange("p (par hp) -> p par hp", par=2).unsqueeze(3).to_broadcast([128, 2, 4, n])
                bcm = lambda m_: m_.unsqueeze(1).to_broadcast([128, 4, 128])
                A4 = lambda bk: pb[bk].rearrange("p (i c) -> p i c", c=128)
                A8 = lambda bk: psall[:, bk:bk + 2, :].rearrange("p b (i c) -> p (b i) c", c=256)

                DN_NT = int(os.environ.get("DN_NT", "18"))

                def stream_P(d, n, tt):
                    (Mc, Mrest, Mr, Ml, Mi) = (LE, GT_, LE, GT_, LE) if d == 0 else (GE, LT_, GE, LT_, GE)
                    lat = tt >= 2
                    tok = slice(tt * 128, (tt + 1) * 128)
                    g_t = gdec[:, tt, d * 8:(d + 1) * 8]
                    b_t = beta[:, tt, d * 8:(d + 1) * 8]
                    eg, egn = egS[n % 3], "eg%d" % (n % 3)
                    egl2, egl2n = egl2S[n % 3], "egl2%d" % (n % 3)
                    Aqk, Aqkn = AqkS[n % 3], "Aqk%d" % (n % 3)
                    R0b, R0n = R0S[n % 2], "R0b%d" % (n % 2)
                    P0b, P0n = P0S[n % 2], "P0b%d" % (n % 2)
                    small = pb[7]
                    MM(small[:, 0:8], Mc, g_t, ["tri", "gdec"], [PB[7]], inc=False)
                    MM(small[:, 8:16], Mrest, g_t, ["tri", "gdec"], [PB[7]], inc=False)
                    MM(small[:, 16:24], ONES, g_t, ["tri", "gdec"], [PB[7]])
                    ACTF(eg[:], small[:, 0:16], AF.Exp, [PB[7]], [egn])
                    ACTF(egl2[0:64, :], small[0:64, 16:20], AF.Exp, [PB[7]], [egl2n])
                    ACTF(egl2[64:128, :], small[64:128, 20:24], AF.Exp, [PB[7]], [egl2n])
                    TT("pool", rg[:], bc_h(Mr), bc_i(g_t, 128), ALU.mult, ["tri", "gdec"], ["rgEs"])
                    yield
                    for hh in range(2):
                        MM(pb[4 + hh][:], Ml, rg[:, hh * 4:(hh + 1) * 4, :].rearrange("p h i -> p (h i)"), ["tri", "rgEs"], [PB[4 + hh]])
                    ACTF(DT[:], ps2(4), AF.Exp, [PB[4], PB[5]], ["DT"])
                    TT("dve", DT[:], DT[:], bc_h(Mi), ALU.mult, ["DT", "tri"], ["DT"])
                    yield
                    TT("dve", E_[:], DT[:], bc_i(b_t, 128), ALU.mult, ["DT", "beta"], ["E_"])
                    TT("pool", Es[:], E_[:], bc_h(negoff[:]), ALU.mult, ["E_", "negoff"], ["rgEs"])
                    yield
                    for h in range(8):
                        pr, hp = (h % 2) * 64, h // 2
                        kT_h = dnkT[pr:pr + 64, hp, tok]
                        MM(psall[:, 4 + h % 2, hp * 128:(hp + 1) * 128], kT_h, kT_h, ["dnkT%d" % hp], [PB[4 + h % 2]], inc=(h >= 6))
                    if lat:
                        for h in range(8):
                            pr, hp = (h % 2) * 64, h // 2
                            MM(psall[:, 6 + h % 2, hp * 128:(hp + 1) * 128], dnkT[pr:pr + 64, hp, tok],
                               dnqT[pr:pr + 64, hp, tok], ["dnkT%d" % hp, "dnqT%d" % hp], [PB[6 + h % 2]], inc=(h >= 6))
                    TT("dve", R0b[:], ps2(4), Es[:], ALU.mult, [PB[4], PB[5], "rgEs"], [R0n])
                    if lat:
                        TT("dve", Aqk[:], ps2(6), E_[:], ALU.mult, [PB[6], PB[7], "E_"], [Aqkn])
                    yield
                    pT = pb[6][:].bitcast(BF16).rearrange("p (k t) -> p k t", k=8)
                    for q in range(8):
                        TR(pT[:, q, :], R0b[:, q, :], ident[:], [R0n, "ident"], [PB[6]], inc=(q == 7))
                    CP("act", P0b[:], pT, [PB[6]], [P0n])
                    yield

                def stream_I(d, n, tt):
                    R0b, R0n = R0S[n % 2], "R0b%d" % (n % 2)
                    P0b, P0n = P0S[n % 2], "P0b%d" % (n % 2)
                    XT, XTn = XTS[n % 2], "XT%d" % (n % 2)
                    for w in range(2):
                        sl = slice(w * 4, (w + 1) * 4)
                        TT(os.environ.get("BA_ENG", "pool"), BA[w][0][:, :, 0, :], R0b[:, sl, :], bcm(hm[:, 0, :]), ALU.mult, [R0n, "hm"], ["BA%d0" % w])
                        TT(os.environ.get("BB_ENG", "dve"), BB[w][0][:, :, 0, :], P0b[:, sl, :], bcm(hm[:, 0, :]), ALU.mult, [P0n, "hm"], ["BB%d0" % w])
                        TT(os.environ.get("BA_ENG", "pool"), BA[w][1][:, :, 1, :], BA[w][0][:, :, 0, :], bcm(ident[:]), ALU.add, ["BA%d0" % w, "ident"], ["BA%d1" % w])
                    yield
                    CmAll = Cmb + Cfb
                    for li in range(4):
                        mnat = hm[:, (1 + li) if d == 0 else (5 + li), :]
                        TT("pool", CmAll[li][:], P0b[:], bc_h(mnat), ALU.mult, [P0n, "hm"], ["CmAll%d" % li])
                    for w in range(2):
                        b0 = 2 * w
                        for i in range(4):
                            MM(A4(b0)[:, i, :], BA[w][0][:, i, 0, :], BB[w][0][:, i, 0, :], ["BA%d0" % w, "BB%d0" % w], [PB[b0]], inc=False)
                            MM(A4(b0 + 1)[:, i, :], BB[w][0][:, i, 0, :], BA[w][0][:, i, 0, :], ["BA%d0" % w, "BB%d0" % w], [PB[b0 + 1]],
                               inc=(i == 3))
                        evac_copy(BB[w][1][:, :, 0, :], A4(b0), [PB[b0]], ["BB%d1" % w])
                        evac_copy(BA[w][1][:, :, 0, :], A4(b0 + 1), [PB[b0 + 1]], ["BA%d1" % w])
                    yield
                    for w in range(2):
                        b0 = 2 * w
                        for i in range(4):
                            MM(A8(b0)[:, i, :], BB[w][1][:, i, 0, :], BA[w][1][:, i, :, :].rearrange("p a c -> p (a c)"),
                               ["BA%d1" % w, "BB%d1" % w], [PB[b0 + i // 2]], start=True, stop=False, inc=False)
                            MM(A8(b0)[:, i, 128:256], ident[:], BA[w][1][:, i, 1, :], ["ident", "BA%d1" % w], [PB[b0 + i // 2]],
                               start=False, stop=True, inc=(i == 3))
                        evac_copy(BA[w][0][:].rearrange("p i a c -> p i (a c)"), A8(b0), [PB[b0], PB[b0 + 1]], ["BA%d0" % w])
                    for w in range(2):
                        b0 = 4 + 2 * w
                        for i in range(4):
                            MM(A4(b0)[:, i, :], BA[w][1][:, i, 0, :], BB[w][1][:, i, 0, :], ["BA%d1" % w, "BB%d1" % w], [PB[b0]], inc=(i == 3))
                        evac_copy(BB[w][0][:, :, 0, :], A4(b0), [PB[b0]], ["BB%d0" % w])
                    yield
                    for w in range(2):
                        b0 = 2 * w
                        for i in range(4):
                            MM(A4(b0)[:, i, :], BB[w][0][:, i, 0, :], BA[w][0][:, i, 1, :], ["BA%d0" % w, "BB%d0" % w], [PB[b0]],
                               start=True, stop=False, inc=False)
                            MM(A4(b0)[:, i, :], ident[:], BA[w][0][:, i, 1, :], ["ident", "BA%d0" % w], [PB[b0]], start=False, stop=True,
                               inc=(i == 3))
                        evac_copy(Wb[w][0][:], A4(b0), [PB[b0]], ["Wb%d0" % w])
                    yield
                    for li in range(4):
                        cur, nxt = li % 2, (li + 1) % 2
                        last = (li == 3)
                        Cm, Cmn = CmAll[li], "CmAll%d" % li
                        for w in range(2):
                            b0 = 2 * w
                            Wc, Wcn, Dc, Dcn = Wb[w][cur], "Wb%d%d" % (w, cur), Db[w][cur], "Db%d%d" % (w, cur)
                            pTw = pb[b0 + 1][:].bitcast(BF16).rearrange("p (k t) -> p k t", k=8)
                            for i in range(4):
                                q = w * 4 + i
                                MM(A4(b0)[:, i, :], Cm[:, q, :], Wc[:, i, :], [Cmn, Wcn], [PB[b0]], inc=False)
                                TR(pTw[:, i, :], Wc[:, i, :], ident[:], [Wcn, "ident"], [PB[b0 + 1]], inc=(i == 3))
                            evac_copy(Yb[w][:], A4(b0), [PB[b0]], ["Yb%d" % w])
                            evac_copy(Dc[:], pTw[:, 0:4, :], [PB[b0 + 1]], [Dcn])
                        yield
                        for w in range(2):
                            b0 = 2 * w
                            Wc, Wcn, Dc, Dcn = Wb[w][cur], "Wb%d%d" % (w, cur), Db[w][cur], "Db%d%d" % (w, cur)
                            for i in range(4):
                                MM(A4(b0)[:, i, :], Dc[:, i, :], Yb[w][:, i, :], [Dcn, "Yb%d" % w], [PB[b0]], start=True, stop=False, inc=False)
                                MM(A4(b0)[:, i, :], ident[:], Wc[:, i, :], ["ident", Wcn], [PB[b0]], start=False, stop=True, inc=(i == 3))
                            if last:
                                evac_copy(XT[:, w * 4:(w + 1) * 4, :], A4(b0), [PB[b0]], [XTn])
                            else:
                                evac_copy(Wb[w][nxt][:], A4(b0), [PB[b0]], ["Wb%d%d" % (w, nxt)])
                        yield

                def stream_C(d, n, tt):
                    lat = tt >= 2
                    lt = tt - 2
                    tok = slice(tt * 128, (tt + 1) * 128)
                    b_t = beta[:, tt, d * 8:(d + 1) * 8]
                    eg, egn = egS[n % 3], "eg%d" % (n % 3)
                    egl2, egl2n = egl2S[n % 3], "egl2%d" % (n % 3)
                    Aqk, Aqkn = AqkS[n % 3], "Aqk%d" % (n % 3)
                    XT, XTn = XTS[n % 2], "XT%d" % (n % 2)
                    ktn = k_tok[:, tt, :, :].rearrange("p (hp par) e -> p par hp e", par=2)
                    TT("pool", v4q(kg[:]), ktn, bc4(eg[:, 0:8], 64), ALU.mult, ["k_tok", egn], ["kg"])
                    TT("pool", eb[:], eg[:, 8:16], b_t, ALU.mult, [egn, "beta"], ["eb"])
                    TT("pool", v4q(kdb[:]), ktn, bc4(eb[:], 64), ALU.mult, ["k_tok", "eb"], ["kdb"])
                    up_ps = pb[4].rearrange("p (q e) -> p q e", e=64)
                    for h in range(8):
                        q = (h % 2) * 4 + h // 2
                        MM(up_ps[:, q, :], XT[:, q, :], v_tok[:, tt, h, :], [XTn, "v_tok"], [PB[4]], inc=(h == 7))
                    for h in range(8):
                        par, hp = h % 2, h // 2
                        q = par * 4 + hp
                        MM(psall[par * 64:(par + 1) * 64, 5, hp * 128:(hp + 1) * 128], kg[:, q, :], XT[:, q, :], ["kg", XTn],
                           [PB[5]], inc=(h == 7))
                    CP("act", up[:], up_ps, [PB[4]], ["up"])
                    CP("act", wT[:], pb[5].rearrange("p (a i) -> p a i", i=128), [PB[5]], ["wT"])
                    yield
                    wS_ps = psall[:, 6:8, 0:256].rearrange("p b (hp e) -> p b hp e", e=64)
                    for h in range(8):
                        par, hp = h % 2, h // 2
                        pr = par * 64
                        MM(wS_ps[:, par, hp, :], wT[pr:pr + 64, hp, :], Sbf[pr:pr + 64, hp, :], ["wT", "Sbf"], [PB[6 + par]], inc=(h >= 6))
                    TT("dve", v4q(vt[:]), v4q(up[:]), wS_ps, ALU.subtract, ["up", PB[6], PB[7]], ["vt"])
                    yield
                    if lat:
                        qS_ps = psall[:, 4:6, 0:256].rearrange("p b (hp e) -> p b hp e", e=64)
                        Av_ps = pb[6].rearrange("p (q e) -> p q e", e=64)
                        for h in range(8):
                            par, hp = h % 2, h // 2
                            pr = par * 64
                            MM(qS_ps[:, par, hp, :], dnqT[pr:pr + 64, hp, tok], Sbf[pr:pr + 64, hp, :], ["dnqT%d" % hp, "Sbf"],
                               [PB[4 + par]], inc=(h >= 6))
                        for q in range(8):
                            MM(Av_ps[:, q, :], Aqk[:, q, :], vt[:, q, :], [Aqkn, "vt"], [PB[6]], inc=(q == 7))
                    for h in range(8):
                        par, hp = h % 2, h // 2
                        q = par * 4 + hp
                        MM(psall[par * 64:(par + 1) * 64, 7, hp * 64:(hp + 1) * 64], kdb[:, q, :], vt[:, q, :], ["kdb", "vt"],
                           [PB[7]], inc=(h == 7))
                    TT("dve", S32[:], S32[:], egl2[:].unsqueeze(2).to_broadcast([128, 4, 64]), ALU.mult, ["S32", egl2n], ["S32"])
                    TT("dve", S32[:], S32[:], pb[7][:, 0:256].rearrange("p (a e) -> p a e", e=64), ALU.add, ["S32", PB[7]], ["S32"])
                    CP("act", Sbf[:], S32[:], ["S32"], ["Sbf"])
                    if lat:
                        TT("dve", v4q(otmp[:]), qS_ps, bc4(eg[:, 0:8], 64), ALU.mult, [PB[4], PB[5], egn], ["otmp"])
                        if d == 0:
                            TT("dve", o_acc[:, lt, :, :], otmp[:], Av_ps, ALU.add, ["otmp", PB[6]], ["o_acc%d" % lt])
                        else:
                            TT("dve", ofin[:], otmp[:], Av_ps, ALU.add, ["otmp", PB[6]], ["ofin"])
                    if os.environ.get("DN_TAP") == "%d,%d" % (d, tt):
                        tap("eg", eg[:], [128, 16], F32, [egn])
                        tap("XT", XT[:], [128, 8, 128], BF16, [XTn])
                        tap("up", up[:], [128, 8, 64], F32, ["up"])
                        tap("wT", wT[:], [128, 4, 128], BF16, ["wT"])
                        tap("vt", vt[:], [128, 8, 64], BF16, ["vt"])
                        tap("S32", S32[:], [128, 4, 64], F32, ["S32"])
                        tap("kg", kg[:], [128, 8, 64], BF16, ["kg"])
                        tap("R0", R0S[n % 2][:], [128, 8, 128], BF16, ["R0b%d" % (n % 2)])
                    yield
                    if lat:
                        if d == 1:
                            TT("pool", ofin[:], ofin[:], o_acc[:, lt, :, :], ALU.add, ["ofin", "o_acc%d" % lt], ["ofin"])
                            TT("pool", otmp[:], ofin[:], ofin[:], ALU.mult, ["ofin"], ["otmp"])
                            op("dve", lambda e: e.tensor_reduce(out=oss[:], in_=otmp[:], axis=AX.X, op=ALU.add), ["otmp"], ["oss"])
                            TS("dve", oss[:], oss[:], 1.0 / 64, EPS, ALU.mult, ALU.add, ["oss"], ["oss"])
                            ACTF(oss[:], oss[:], AF.Ln, ["oss"], ["oss"])
                            ACTF(oss[:], oss[:], AF.Exp, ["oss"], ["oss"], scale=-0.5)
                            TT("dve", ofin[:], ofin[:], bc_i(oss[:], 64), ALU.mult, ["ofin", "oss"], ["ofin"])
                            TT("dve", onb[:].rearrange("p (hp par) e -> p par hp e", par=2), v4q(ofin[:]),
                               dnwB[:].unsqueeze(1).unsqueeze(1).to_broadcast([128, 2, 4, 64]), ALU.mult, ["ofin", "dnwB"], ["onb"])
                            yield
                            pT = pb[5][:].bitcast(BF16).rearrange("p (k t) -> p k t", k=8)
                            onv = onb[:].rearrange("p (a b) e -> p a (b e)", b=2)
                            for hp in range(4):
                                TR(pT[:, 4 + hp, :], onv[:, hp, :], ident[:], ["onb", "ident"], [PB[5]], inc=(hp == 3))
                            CP("act", odT[:, :, lt * 128:(lt + 1) * 128], pT[:, 4:8, :], [PB[5]], ["odT"])
                    yield

                def run_streams(gens, periods=None):
                    items = [(g, (periods[k] if periods else 1)) for k, g in enumerate(gens) if g is not None]
                    rnd = 0
                    while items:
                        for it in list(items):
                            g, per = it
                            if rnd % per:
                                continue
                            try:
                                next(g)
                            except StopIteration:
                                items.remove(it)
                        rnd += 1

                for d in range(int(os.environ.get("DN_PASSES", "2"))):
                    order = list(range(NT)) if d == 0 else [1, 0] + list(range(NT - 1, 1, -1))
                    order = order[:DN_NT]
                    op("pool", lambda e: e.memset(S32[:], 0.0), (), ["S32"])
                    op("pool", lambda e: e.memset(Sbf[:], 0.0), (), ["Sbf"])
                    nn = len(order)
                    run_streams([stream_P(d, 0, order[0])])
                    for n in range(nn):
                        run_streams([stream_I(d, n, order[n]),
                                     stream_P(d, n + 1, order[n + 1]) if n + 1 < nn else None,
                                     stream_C(d, n - 1, order[n - 1]) if n >= 1 else None],
                                    periods=[int(os.environ.get("PER_I", "1")), int(os.environ.get("PER_P", "1")), int(os.environ.get("PER_C", "1"))])
                    run_streams([stream_C(d, nn - 1, order[nn - 1])])
                tap("odT", odT[:], [128, 4, T_LAT], BF16, ["odT"])
                fw.barrier()
            dn_era.close()

        if stop_after not in ("C1", "D", "A"):
            omT = fw.sbuf("omT", [128, 4, T_LAT], BF16)
            hT = fw.sbuf("hT2", [128, 8, T_ALL], BF16)
            for kt in range(8):
                dma("sp", hT[:, kt, :], hT_spill[:, kt, :], reads=["hT_spill"], writes=["hT%d" % t for t in range(NT)], slot="fill%d" % kt)
            mla_era = ExitStack()
            qT_all = fw.sbuf("qT_all", [128, 8, T_LAT], BF16, mla_era)
            kT_all = fw.sbuf("kT_all", [128, 8, T_ALL], BF16, mla_era)
            V_all = fw.sbuf("V_all", [128, NT, 8, 65], BF16, mla_era)
            negC = fw.sbuf("negC", [128, 1], F32, mla_era)
            with ExitStack() as c2:
                A_ = lambda name, shape, dt: fw.sbuf(name, shape, dt, c2)
                wtok = A_("wtok", [128, 8, 672], BF16)
                wuq = A_("wuq", [128, 3, 768], BF16)
                wukv = A_("wukv", [128, 2, 1024], BF16)
                qnwT = A_("qnwT", [128, 3], F32)
                kvnwT = A_("kvnwT", [128, 2], F32)
                qhwB = A_("qhwB", [128, 96], F32)
                khwB = A_("khwB", [128, 96], F32)
                cosT = A_("cosT", [128, 16, 16], F32)
                sinT = A_("sinT", [128, 16, 16], F32)
                invn = A_("invn", [128, 2], F32)
                ssA = A_("ssA", [128, 4], F32)
                rs2 = A_("rs2", [128, 2], F32)
                junk2 = A_("junk2", [128, 384], BF16)
                cqn = A_("cqn", [128, 384], BF16)
                ckvn = A_("ckvn", [128, 256], BF16)
                cqnT = A_("cqnT", [128, 3, 128], BF16)
                ckvnT = A_("ckvnT", [128, 2, 128], BF16)
                sqq = A_("sqq", [128, 8, 96], F32)
                ss16 = A_("ss16", [128, 16], F32)
                q_fin = A_("q_fin", [128, 8, 96], BF16)
                k_fin = A_("k_fin", [128, 8, 96], BF16)
                tl = A_("tl", [128, 8, 32], F32)
                ra = A_("ra", [128, 8, 2, 8], F32)
                rb = A_("rb", [128, 8, 2, 8], F32)
                cmx = A_("cmx", [128, 4], F32)
                dma("pool", wtok[:], winv[:, :, 0:672], writes=["wtok"])
                dma("pool", wuq[:], w_uq.rearrange("(kt p) n -> p kt n", p=128), writes=["wuq"])
                dma("pool", wukv[:], w_ukv.rearrange("(kt p) n -> p kt n", p=128), writes=["wukv"])
                with nc.allow_non_contiguous_dma(reason="small vector column layouts"):
                    dma("sp", qnwT[:], q_norm_w.rearrange("(kt p) -> p kt", p=128), writes=["qnwT"])
                    dma("sp", kvnwT[:], kv_norm_w.rearrange("(kt p) -> p kt", p=128), writes=["kvnwT"])
                    dma("sp", cosT[:], cos_d.rearrange("(t p) c -> p t c", p=128), writes=["cosT"])
                    dma("sp", sinT[:], sin_d.rearrange("(t p) c -> p t c", p=128), writes=["sinT"])
                dma("sp", qhwB[:], qh_w.partition_broadcast(128), writes=["qhwB"])
                dma("sp", khwB[:], kh_w.partition_broadcast(128), writes=["khwB"])
                op("pool", lambda e: e.memset(ssA[:], 1.0), (), ["ssA"])
                op("pool", lambda e: e.memset(invn[:, 0:1], 1.0 / 384), (), ["invn"])
                op("pool", lambda e: e.memset(invn[:, 1:2], 1.0 / 256), (), ["invn"])
                op("pool", lambda e: e.memset(V_all[:, :, :, 64:65], 1.0), (), ["V_all"])
                TT("dve", sqq[:, 0, :], qhwB[:], qhwB[:], ALU.mult, ["qhwB"], ["sqq"])
                TT("dve", sqq[:, 1, :], khwB[:], khwB[:], ALU.mult, ["khwB"], ["sqq"])
                op("dve", lambda e: e.tensor_reduce(out=cmx[:, 0:2], in_=sqq[:, 0:2, :], axis=AX.X, op=ALU.max), ["sqq"], ["cmx"])
                TT("dve", cmx[:, 2:3], cmx[:, 0:1], cmx[:, 1:2], ALU.mult, ["cmx"], ["cmx"])
                ACTF(cmx[:, 2:3], cmx[:, 2:3], AF.Ln, ["cmx"], ["cmx"])
                ACTF(cmx[:, 3:4], cmx[:, 2:3], AF.Exp, ["cmx"], ["cmx"], scale=0.5)
                TS("dve", negC[:], cmx[:, 3:4], -math.sqrt(96.0), None, ALU.mult, None, ["cmx"], ["negC"])
                v8 = lambda bk: psall[:, bk:bk + 2, :].rearrange("p b (h c) -> p (b h) c", c=128)
                qfS = [A_("qfS%d" % i, [128, 8, 96], F32) for i in range(2)]
                kvfS = [A_("kvfS%d" % i, [128, 8, 128], F32) for i in range(2)]
                krsS = [A_("krsS%d" % i, [128, 32], F32) for i in range(2)]
                skrS = [A_("skrS%d" % i, [128, 1], F32) for i in range(2)]
                bq = lambda v_, n: v_.unsqueeze(2).to_broadcast([128, 8, n])
                bh = lambda v_, n: v_.unsqueeze(1).to_broadcast([128, 8, n])
                r4 = lambda t_, b_: t_.rearrange("p h (a b f) -> p h a b f", a=2, b=2)[:, :, :, b_, :]

                def stream_X(tt):
                    lat = tt >= 2
                    tok = slice(tt * 128, (tt + 1) * 128)
                    par = tt % 2
                    hb = ["hT%d" % tt]
                    qf_, qfn = qfS[par], "qfS%d" % par
                    kvf, kvfn = kvfS[par], "kvfS%d" % par
                    krs_, krsn = krsS[par], "krsS%d" % par
                    skr, skrn = skrS[par], "skrS%d" % par
                    p_cq, p_kv = pb[0][:, 0:384], pb[1][:, 0:288]
                    for kt in range(8):
                        if lat:
                            MM(p_cq, hT[:, kt, tok], wtok[:, kt, 0:384], hb + ["wtok"], [PB[0]], start=(kt == 0), stop=(kt == 7), inc=False)
                        MM(p_kv, hT[:, kt, tok], wtok[:, kt, 384:672], hb + ["wtok"], [PB[1]], start=(kt == 0), stop=(kt == 7), inc=(kt == 7))
                    if lat:
                        op("act", lambda e: e.activation(out=junk2[:], in_=p_cq, func=AF.Square, accum_out=ssA[:, 0:1]), [PB[0]], ["junk2", "ssA"])
                    op("act", lambda e: e.activation(out=junk2[:, 0:256], in_=p_kv[:, 0:256], func=AF.Square, accum_out=ssA[:, 1:2]),
                       [PB[1]], ["junk2", "ssA"])
                    op("act", lambda e: e.activation(out=junk2[:, 0:32], in_=p_kv[:, 256:288], func=AF.Square, accum_out=skr[:]),
                       [PB[1]], ["junk2", skrn])
                    TT("dve", rs2[:], ssA[:, 0:2], invn[:], ALU.mult, ["ssA", "invn"], ["rs2"])
                    TS("dve", rs2[:], rs2[:], EPS, None, ALU.add, None, ["rs2"], ["rs2"])
                    ACTF(rs2[:], rs2[:], AF.Ln, ["rs2"], ["rs2"])
                    ACTF(rs2[:], rs2[:], AF.Exp, ["rs2"], ["rs2"], scale=-0.5)
                    yield
                    if lat:
                        ACTF(cqn[:], p_cq, AF.Identity, [PB[0], "rs2"], ["cqn"], scale=rs2[:, 0:1])
                    TS("dve", ckvn[:], p_kv[:, 0:256], rs2[:, 1:2], None, ALU.mult, None, [PB[1], "rs2"], ["ckvn"])
                    CP("act", krs_[:], p_kv[:, 256:288], [PB[1]], [krsn])
                    pT = pb[2][:].bitcast(BF16).rearrange("p (k t) -> p k t", k=8)
                    if lat:
                        for i in range(3):
                            TR(pT[:, i, :], cqn[:, i * 128:(i + 1) * 128], ident[:], ["cqn", "ident"], [PB[2]], inc=False)
                    for i in range(2):
                        TR(pT[:, 3 + i, :], ckvn[:, i * 128:(i + 1) * 128], ident[:], ["ckvn", "ident"], [PB[2]], inc=(i == 1))
                    yield
                    if lat:
                        TT("dve", cqnT[:], pT[:, 0:3, :], qnwT[:].unsqueeze(2).to_broadcast([128, 3, 128]), ALU.mult, [PB[2], "qnwT"], ["cqnT"])
                    TT("dve", ckvnT[:], pT[:, 3:5, :], kvnwT[:].unsqueeze(2).to_broadcast([128, 2, 128]), ALU.mult, [PB[2], "kvnwT"], ["ckvnT"])
                    if lat:
                        for (c0, c1, bk) in ((0, 512, 3), (512, 768, 4)):
                            for kt in range(3):
                                MM(pb[bk][:, 0:c1 - c0], cqnT[:, kt, :], wuq[:, kt, c0:c1], ["cqnT", "wuq"], [PB[bk]],
                                   start=(kt == 0), stop=(kt == 2), inc=(kt == 2))
                    for nh in range(2):
                        for kt in range(2):
                            MM(pb[5 + nh][:], ckvnT[:, kt, :], wukv[:, kt, nh * 512:(nh + 1) * 512], ["ckvnT", "wukv"], [PB[5 + nh]],
                               start=(kt == 0), stop=(kt == 1), inc=(kt == 1))
                    yield
                    qff = qf_[:].rearrange("p h c -> p (h c)")
                    kvff = kvf[:].rearrange("p h c -> p (h c)")
                    if lat:
                        evac_copy(qff[:, 0:512], pb[3][:], [PB[3]], [qfn])
                        evac_copy(qff[:, 512:768], pb[4][:, 0:256], [PB[4]], [qfn])
                    evac_copy(kvff[:, 0:512], pb[5][:], [PB[5]], [kvfn])
                    evac_copy(kvff[:, 512:1024], pb[6][:], [PB[6]], [kvfn])
                    yield

                def stream_Y(tt):
                    sqk = sqq[:, :, 0:64]
                    lat = tt >= 2
                    lt = tt - 2
                    tok = slice(tt * 128, (tt + 1) * 128)
                    par = tt % 2
                    qf, qfn = qfS[par], "qfS%d" % par
                    kvv, kvfn = kvfS[par], "kvfS%d" % par
                    krs, krsn = krsS[par], "krsS%d" % par
                    skr, skrn = skrS[par], "skrS%d" % par
                    if lat:
                        TT("pool", sqq[:], qf[:], qf[:], ALU.mult, [qfn], ["sqq"])
                        op("dve", lambda e: e.tensor_reduce(out=ss16[:, 0:8], in_=sqq[:], axis=AX.X, op=ALU.add), ["sqq"], ["ss16"])
                    else:
                        op("pool", lambda e: e.memset(ss16[:, 0:8], 1.0), (), ["ss16"])
                    TT("pool", sqk, kvv[:, :, 0:64], kvv[:, :, 0:64], ALU.mult, [kvfn], ["sqq"])
                    op("dve", lambda e: e.tensor_reduce(out=ss16[:, 8:16], in_=sqk, axis=AX.X, op=ALU.add), ["sqq"], ["ss16"])
                    TS("dve", ss16[:, 8:16], ss16[:, 8:16], skr[:], None, ALU.add, None, ["ss16", skrn], ["ss16"])
                    TS("dve", ss16[:], ss16[:], 1.0 / 96, EPS, ALU.mult, ALU.add, ["ss16"], ["ss16"])
                    ACTF(ss16[:], ss16[:], AF.Ln, ["ss16"], ["ss16"])
                    ACTF(ss16[:], ss16[:], AF.Exp, ["ss16"], ["ss16"], scale=-0.5)
                    yield

                    def rope(src_t, dst_fin, cs, sn):
                        cB = cs.rearrange("p (a f) -> p a f", a=2).unsqueeze(1).to_broadcast([128, 8, 2, 8])
                        sB = sn.rearrange("p (a f) -> p a f", a=2).unsqueeze(1).to_broadcast([128, 8, 2, 8])
                        t1, t2 = r4(src_t, 0), r4(src_t, 1)
                        o1, o2 = r4(dst_fin, 0), r4(dst_fin, 1)
                        TT("dve", ra[:], t1, cB, ALU.mult, ["tl", "cosT"], ["ra"])
                        TT("pool", rb[:], t2, sB, ALU.mult, ["tl", "sinT"], ["rb"])
                        TT("dve", o1, ra[:], rb[:], ALU.subtract, ["ra", "rb"], ["fin"])
                        TT("dve", ra[:], t1, sB, ALU.mult, ["tl", "sinT"], ["ra"])
                        TT("pool", rb[:], t2, cB, ALU.mult, ["tl", "cosT"], ["rb"])
                        TT("dve", o2, ra[:], rb[:], ALU.add, ["ra", "rb"], ["fin"])

                    if lat:
                        TT("dve", qf[:], qf[:], bq(ss16[:, 0:8], 96), ALU.mult, [qfn, "ss16"], [qfn])
                        TT("dve", q_fin[:, :, 0:64], qf[:, :, 0:64], bh(qhwB[:, 0:64], 64), ALU.mult, [qfn, "qhwB"], ["fin"])
                        TT("dve", tl[:], qf[:, :, 64:96], bh(qhwB[:, 64:96], 32), ALU.mult, [qfn, "qhwB"], ["tl"])
                        rope(tl[:], q_fin[:, :, 64:96], cosT[:, lt, :], sinT[:, lt, :])
                        pq = pb[7][:].bitcast(BF16).rearrange("p (k t) -> p k t", k=8)
                        for h in range(8):
                            TR(pq[0:96, h, :], q_fin[:, h, :], ident[:], ["fin", "ident"], [PB[7]], inc=(h == 7))
                        evac_copy(qT_all[0:96, :, lt * 128:(lt + 1) * 128], pq[0:96, :, :], [PB[7]], ["qT_all"])
                    yield
                    TT("dve", sqk, kvv[:, :, 0:64], bq(ss16[:, 8:16], 64), ALU.mult, [kvfn, "ss16"], ["sqq"])
                    TT("dve", k_fin[:, :, 0:64], sqk, bh(khwB[:, 0:64], 64), ALU.mult, ["sqq", "khwB"], ["fin"])
                    TT("dve", tl[:], bh(krs[:], 32), bq(ss16[:, 8:16], 32), ALU.mult, [krsn, "ss16"], ["tl"])
                    TT("dve", tl[:], tl[:], bh(khwB[:, 64:96], 32), ALU.mult, ["tl", "khwB"], ["tl"])
                    if lat:
                        rope(tl[:], k_fin[:, :, 64:96], cosT[:, lt, :], sinT[:, lt, :])
                    else:
                        CP("act", k_fin[:, :, 64:96], tl[:], ["tl"], ["fin"])
                    CP("act", V_all[:, tt, :, 0:64], kvv[:, :, 64:128], [kvfn], ["V_all"])
                    pk = pb[7][:].bitcast(BF16).rearrange("p (k t) -> p k t", k=8)
                    for h in range(8):
                        TR(pk[0:96, h, :], k_fin[:, h, :], ident[:], ["fin", "ident"], [PB[7]], inc=(h == 7))
                    evac_copy(kT_all[0:96, :, tok], pk[0:96, :, :], [PB[7]], ["kT_all"])
                    yield

                def run2(gens):
                    gens = [g for g in gens if g is not None]
                    while gens:
                        for g in list(gens):
                            try:
                                next(g)
                            except StopIteration:
                                gens.remove(g)

                run2([stream_X(0)])
                for tt in range(NT):
                    run2([stream_Y(tt), stream_X(tt + 1) if tt + 1 < NT else None])
                tap("qT_all", qT_all[0:96, :, :], [96, 8, T_LAT], BF16, ["qT_all"])
                tap("kT_all", kT_all[0:96, :, :], [96, 8, T_ALL], BF16, ["kT_all"])
                tap("V_all", V_all[:], [128, NT, 8, 65], BF16, ["V_all"])
                fw.barrier()

            if stop_after != "C2":
                with ExitStack() as pe_:
                    PT = [fw.sbuf("PT%d" % i, [128, 2, 512], BF16, pe_) for i in range(2)]
                    o_tok = fw.sbuf("o_tok", [128, 4, 512], BF16, pe_)
                    rec = fw.sbuf("rec", [128, 4], F32, pe_)
                    SCALE = 96.0 ** -0.5
                    steps = [(g, h, jp) for g in range(4) for h in range(8) for jp in range(9)]

                    def acc_of(g, h):
                        ab = 4 + ((g * 8 + h) % 2)
                        return ab, pb[ab][:, 0:260].rearrange("p (q c) -> p q c", c=65)

                    def scores(k):
                        g, h, jp = steps[k]
                        sb = 2 * (k % 2)
                        for t in range(2):
                            kt_ = 2 * jp + t
                            MM(pb[sb + t][:], kT_all[0:96, h, kt_ * 128:(kt_ + 1) * 128], qT_all[0:96, h, g * 512:(g + 1) * 512],
                               ["kT_all", "qT_all"], [PB[sb + t]], inc=(t == 1))

                    def expo(k):
                        sb = 2 * (k % 2)
                        Pt, Ptn = PT[k % 2], "PT%d" % (k % 2)
                        op("act", lambda e: e.activation(out=Pt[:], in_=psall[:, sb:sb + 2, :], func=AF.Exp, scale=SCALE,
                                                         bias=negC[:]), [PB[sb], PB[sb + 1], "negC"], [Ptn])

                    def pv(k):
                        g, h, jp = steps[k]
                        ab, acc = acc_of(g, h)
                        Pt, Ptn = PT[k % 2], "PT%d" % (k % 2)
                        for t in range(2):
                            kt_ = 2 * jp + t
                            for qs in range(4):
                                first = (jp == 0 and t == 0 and qs == 0)
                                lastmm = (jp == 8 and t == 1 and qs == 3)
                                op("pe", lambda e: e.matmul(acc[:, qs, :], lhsT=Pt[:, t, qs * 128:(qs + 1) * 128],
                                                            rhs=V_all[:, kt_, h, :], start=first, stop=lastmm,
                                                            skip_group_check=True),
                                   [Ptn, "V_all"], [PB[ab]], inc=(t == 1 and qs == 3))

                    scores(0)
                    for k in range(len(steps)):
                        g, h, jp = steps[k]
                        expo(k)
                        if k + 1 < len(steps):
                            scores(k + 1)
                        pv(k)
                        if jp != 8:
                            continue
                        ab, acc = acc_of(g, h)
                        qs_tok = slice(g * 512, (g + 1) * 512)
                        op("dve", lambda e: e.reciprocal(out=rec[:], in_=acc[:, :, 64]), [PB[ab]], ["rec"])
                        TT("dve", o_tok[:, :, h * 64:(h + 1) * 64], acc[:, :, 0:64], rec[:].unsqueeze(2).to_broadcast([128, 4, 64]),
                           ALU.mult, [PB[ab], "rec"], ["o_tok"])
                        if h != 7:
                            continue
                        for half in range(2):
                            bk = 6 + half
                            pT = pb[bk][:].bitcast(BF16).rearrange("p (k t) -> p k t", k=8)
                            for qq in range(2):
                                qs = half * 2 + qq
                                for c4 in range(4):
                                    TR(pT[:, qq * 4 + c4, :], o_tok[:, qs, c4 * 128:(c4 + 1) * 128], ident[:], ["o_tok", "ident"], [PB[bk]],
                                       inc=(qq == 1 and c4 == 3))
                            dst = omT[:, :, qs_tok].rearrange("p c (qs t) -> p qs c t", qs=4)[:, half * 2:half * 2 + 2, :, :]
                            evac_copy(dst, pT.rearrange("p (qq c) t -> p qq c t", qq=2), [PB[bk]], ["omT"])
                    tap("omT", omT[:], [128, 4, T_LAT], BF16, ["omT"])
                    fw.barrier()
            mla_era.close()

            if stop_after not in ("C2", "E"):
                with ExitStack() as pf:
                    A_ = lambda name, shape, dt: fw.sbuf(name, shape, dt, pf)
                    wz = A_("wz", [128, 8, 3072], BF16)
                    wmo = A_("wmo", [128, 4, D], BF16)
                    wdo = A_("wdo", [128, 4, D], BF16)
                    wo = A_("wo", [128, 8, D], BF16)
                    sg = A_("sg", [128, 16, 512], BF16)
                    szm = A_("szm", [128, 512], BF16)
                    om_s = A_("om_s", [128, 4, 512], BF16)
                    od_s = A_("od_s", [128, 4, 512], BF16)
                    mT = A_("mT", [128, 8, 512], BF16)
                    t1S = [A_("t1%d" % i, [128, 512], F32) for i in range(2)]
                    t2S = [A_("t2%d" % i, [128, 512], F32) for i in range(2)]
                    xt = [A_("xt%d" % i, [128, D], F32) for i in range(2)]
                    ot = [A_("ot%d" % i, [128, D], F32) for i in range(1)]
                    dma("pool", wz[:, :, 0:512], winv[:, :, 672:1184], writes=["wz_m"])
                    dma("pool", wz[:, :, 512:1024], winv[:, :, 2720:3232], writes=["wz_d"])
                    for i in range(4):
                        dma("pool", wz[:, :, 1024 + i * 512:1536 + i * 512], winv[:, :, 3264 + i * 512:3776 + i * 512], writes=["wz_g%d" % i])
                    dma("pool", wmo[:], mla_w_o.rearrange("(c p) n -> p c n", p=128), writes=["wmo"])
                    dma("pool", wdo[:], dn_w_o.rearrange("(c p) n -> p c n", p=128), writes=["wdo"])
                    dma("pool", wo[:], w_out.rearrange("(c p) n -> p c n", p=128), writes=["wo"])
                    rr = 0
                    for g in range(4):
                        lt0 = g * 4
                        ltok = slice(g * 512, (g + 1) * 512)
                        htok = slice(256 + g * 512, 256 + (g + 1) * 512)
                        hb = hT_bufs(256 + g * 512, 512)
                        for (which, src_T, srcn, dst_s, dsn, wn) in ((0, omT, "omT", om_s, "om_s", "wz_m"), (1, odT, "odT", od_s, "od_s", "wz_d")):
                            for f in range(4):
                                bk = rr % 4
                                rr += 1
                                for kt in range(8):
                                    MM(pb[bk][:], wz[:, kt, which * 512 + f * 128:which * 512 + (f + 1) * 128], hT[:, kt, htok], hb + [wn],
                                       [PB[bk]], start=(kt == 0), stop=(kt == 7), inc=(kt == 7))
                                ACTF(szm[:], pb[bk][:], AF.Silu, [PB[bk]], ["szm"])
                                TT("dve", dst_s[:, f, :], src_T[:, f, ltok], szm[:], ALU.mult, [srcn, "szm"], [dsn])
                        for c in range(16):
                            bk = rr % 4
                            rr += 1
                            for kt in range(8):
                                MM(pb[bk][:], wz[:, kt, 1024 + c * 128:1024 + (c + 1) * 128], hT[:, kt, htok], hb + ["wz_g%d" % (c // 4)],
                                   [PB[bk]], start=(kt == 0), stop=(kt == 7), inc=(kt == 7))
                            ACTF(sg[:, c, :], pb[bk][:], AF.Sigmoid, [PB[bk]], ["sg"])
                        for f8 in range(8):
                            ba, bb_ = (4, 5) if f8 % 2 == 0 else (6, 7)
                            t1_, t1n = t1S[f8 % 2], "t1%d" % (f8 % 2)
                            t2_, t2n = t2S[f8 % 2], "t2%d" % (f8 % 2)
                            for c4 in range(4):
                                MM(pb[ba][:], wmo[:, c4, f8 * 128:(f8 + 1) * 128], om_s[:, c4, :], ["wmo", "om_s"], [PB[ba]],
                                   start=(c4 == 0), stop=(c4 == 3), inc=(c4 == 3))
                            for c4 in range(4):
                                MM(pb[bb_][:], wdo[:, c4, f8 * 128:(f8 + 1) * 128], od_s[:, c4, :], ["wdo", "od_s"], [PB[bb_]],
                                   start=(c4 == 0), stop=(c4 == 3), inc=(c4 == 3))
                            TT("dve", t1_[:], pb[ba][:], sg[:, f8, :], ALU.mult, [PB[ba], "sg"], [t1n])
                            TT("dve", t2_[:], pb[bb_][:], sg[:, 8 + f8, :], ALU.mult, [PB[bb_], "sg"], [t2n])
                            TT("pool", mT[:, f8, :], t1_[:], t2_[:], ALU.add, [t1n, t2n], ["mT"])
                        for qs in range(4):
                            lt = lt0 + qs
                            xb_, xbn = xt[lt % 2], "xt%d" % (lt % 2)
                            ob_, obn = ot[0], "ot0"
                            if lt == 0:
                                dma("sp", xb_[:], x[0:128, :], writes=[xbn])
                            if lt + 1 < 16:
                                dma("sp", xt[(lt + 1) % 2][:], x[(lt + 1) * 128:(lt + 2) * 128, :], writes=["xt%d" % ((lt + 1) % 2)])
                            for nh in range(2):
                                bk = 2 * (qs % 2) + nh
                                for f8 in range(8):
                                    MM(pb[bk][:], mT[:, f8, qs * 128:(qs + 1) * 128], wo[:, f8, nh * 512:(nh + 1) * 512], ["mT", "wo"], [PB[bk]],
                                       start=(f8 == 0), stop=(f8 == 7), inc=(f8 == 7))
                                cs = slice(nh * 512, (nh + 1) * 512)
                                TT("dve", ob_[:, cs], pb[bk][:], gateB[:, cs], ALU.mult, [PB[bk], "gateB"], [obn])
                                TT("pool", xb_[:, cs], ob_[:, cs], xb_[:, cs], ALU.add, [obn, xbn], [xbn])
                            dma("sp", out[lt * 128:(lt + 1) * 128, :], xb_[:], reads=[xbn], writes=["out_dram"], slot="out%d" % (lt % 2))
                    fw.wait_bufs("sp", ["out_dram"])
                    fw.barrier()

        fw.wait_bufs("sp", ["tap_" + n for n in tap_out] + ([] if stop_after else []))
        fw.barrier()
        print("[build] ops=%d waits=%d sems=%d" % (fw.n_ops, fw.n_waits, fw.nsem))
        print("[build] per-engine incs:", {n: e.cnt for n, e in fw.engs.items()})
        stuck = fw.simulate()
        print("[build] deadlock check:", "OK" if not stuck else "STUCK %s" % stuck)
    return nc, tap_out


def _in_maps(inputs):
    cst = _host_consts()
    maps = []
    f = lambda a: np.ascontiguousarray(np.asarray(a, dtype=np.float32))
    for b in range(8):
        m = {
            "x": f(inputs["x"][b]), "ctx": f(inputs["ctx"][b]), "c": f(inputs["c"][b]), "c_ctx": f(inputs["c_ctx"]),
            "w_mod": f(inputs["w_mod"][0]), "b_mod": f(inputs["b_mod"][0]), "norm_w": f(inputs["norm_w"][0]),
            "w_in": f(inputs["w_in"][0]), "dn_conv_w": f(inputs["dn_conv_w"][0]),
            "dn_a_log": f(inputs["dn_a_log"][0]).reshape(16), "dn_dt_bias": f(inputs["dn_dt_bias"][0]).reshape(16),
            "dn_out_norm_w": f(inputs["dn_out_norm_w"][0]),
            "mla_q_norm_w": f(inputs["mla_q_norm_w"][0]), "mla_w_uq": f(inputs["mla_w_uq"][0]),
            "mla_kv_norm_w": f(inputs["mla_kv_norm_w"][0]), "mla_w_ukv": f(inputs["mla_w_ukv"][0]),
            "mla_q_head_norm_w": f(inputs["mla_q_head_norm_w"][0]), "mla_k_head_norm_w": f(inputs["mla_k_head_norm_w"][0]),
            "mla_w_o": f(inputs["mla_w_o"][0]), "dn_w_o": f(inputs["dn_w_o"][0]), "w_out": f(inputs["w_out"][0]),
        }
        m.update(cst)
        maps.append(m)
    return maps


def kernel(**inputs):
    nc, _ = build_program()
    res = run_bass_kernel_spmd(nc, _in_maps(inputs), core_ids=list(range(8)))
    return np.stack([r["out"] for r in res.results], axis=0).astype(np.float32)
```
